# Optimizing a Trainium2 kernel written in Bass

```python
import jax, jax.numpy as jnp
from jax import lax
import numpy as np

D_MODEL = 4096
BATCH = 1
SEQ = 16384
DEPTH = 2

CHUNK = 64
Q_BLOCK = 128
MLA_HEADS = D_MODEL // 256
QK_NOPE = 128
QK_ROPE = 64
V_DIM = 128
Q_LORA = D_MODEL // 4
KV_LORA = 512
ROPE_THETA = 10000.0
POOL_WINDOWS = (2, 4, 8, 16)
POOL_CH = D_MODEL - MLA_HEADS * V_DIM
POOL_GROUP = POOL_CH // len(POOL_WINDOWS)
IN_COLS = Q_LORA + KV_LORA + QK_ROPE + POOL_CH
MIX_WIDTH = MLA_HEADS * V_DIM + POOL_CH
MEM_LEN = 256
X_HEADS = 4
X_HEAD_DIM = D_MODEL // 16
X_WIDTH = X_HEADS * X_HEAD_DIM
D_FF = ((8 * D_MODEL // 3 + 255) // 256) * 256
CONV_W = 3
EPS = 1e-6

kernel_name = "hybrid_mla_pool_memxattn_convffn"


def rmsnorm(x, g):
    x32 = x.astype(jnp.float32)
    y = x32 * lax.rsqrt(jnp.mean(x32 * x32, axis=-1, keepdims=True) + EPS)
    return (y * g.astype(jnp.float32)).astype(x.dtype)


def rope_tables(positions):
    inv = 1.0 / (ROPE_THETA ** (jnp.arange(0, QK_ROPE, 2, dtype=jnp.float32) / QK_ROPE))
    ang = positions.astype(jnp.float32)[..., None] * inv
    return jnp.cos(ang)[:, :, None, :], jnp.sin(ang)[:, :, None, :]


def apply_rope(t, cos, sin):
    t32 = t.astype(jnp.float32)
    t1, t2 = jnp.split(t32, 2, axis=-1)
    out = jnp.concatenate([t1 * cos - t2 * sin, t2 * cos + t1 * sin], axis=-1)
    return out.astype(t.dtype)


def block_causal_attention(q, k, v):
    B, S, H, Dqk = q.shape
    scale = Dqk ** -0.5
    k_chunk = jnp.arange(S) // CHUNK

    def one_block(bi):
        qs = bi * Q_BLOCK
        qb = lax.dynamic_slice_in_dim(q, qs, Q_BLOCK, axis=1)
        s = jnp.einsum('bqhd,bkhd->bhqk', qb, k).astype(jnp.float32) * scale
        q_chunk = (qs + jnp.arange(Q_BLOCK)) // CHUNK
        mask = k_chunk[None, :] <= q_chunk[:, None]
        p = jax.nn.softmax(jnp.where(mask, s, -jnp.inf), axis=-1)
        return jnp.einsum('bhqk,bkhd->bqhd', p.astype(v.dtype), v)

    out = lax.map(one_block, jnp.arange(S // Q_BLOCK))
    return jnp.transpose(out, (1, 0, 2, 3, 4)).reshape(B, S, H, v.shape[-1])


def mla_group(cq, ckv, kr, cos, sin, g_q, w_uq, g_kv, w_ukv):
    B, S, _ = cq.shape
    q = (rmsnorm(cq, g_q) @ w_uq).reshape(B, S, MLA_HEADS, QK_NOPE + QK_ROPE)
    q = jnp.concatenate([q[..., :QK_NOPE], apply_rope(q[..., QK_NOPE:], cos, sin)], axis=-1)
    kv = (rmsnorm(ckv, g_kv) @ w_ukv).reshape(B, S, MLA_HEADS, QK_NOPE + V_DIM)
    k_rope = apply_rope(kr[:, :, None, :], cos, sin)
    k = jnp.concatenate([kv[..., :QK_NOPE],
                         jnp.broadcast_to(k_rope, (B, S, MLA_HEADS, QK_ROPE))], axis=-1)
    v = kv[..., QK_NOPE:]
    return block_causal_attention(q, k, v).reshape(B, S, MLA_HEADS * V_DIM)


def pool_group(u, w_pool, s_pool):
    B, S, _ = u.shape
    u32 = u.astype(jnp.float32)
    csum = lax.cumsum(u32, axis=1)
    t = jnp.arange(S)
    outs = []
    for g, w in enumerate(POOL_WINDOWS):
        sl = slice(g * POOL_GROUP, (g + 1) * POOL_GROUP)
        cg = csum[..., sl]
        lag = jnp.pad(cg, ((0, 0), (w, 0), (0, 0)))[:, :S]
        cnt = jnp.minimum(t + 1, w).astype(jnp.float32)[None, :, None]
        outs.append((cg - lag) / cnt - u32[..., sl])
    p = jnp.stack(outs, axis=2).astype(u.dtype)
    y = jnp.einsum('bsgc,gcd->bsgd', p, w_pool).reshape(B, S, POOL_CH)
    return y * s_pool


def memory_cross_attention(h, mem_n, w_cq, w_ck, w_cv, w_co):
    B, S, _ = h.shape
    M = mem_n.shape[1]
    q = (h @ w_cq).reshape(B, S, X_HEADS, X_HEAD_DIM)
    k = (mem_n @ w_ck).reshape(B, M, X_HEADS, X_HEAD_DIM)
    v = (mem_n @ w_cv).reshape(B, M, X_HEADS, X_HEAD_DIM)
    s = jnp.einsum('bqhd,bkhd->bhqk', q, k).astype(jnp.float32) * (X_HEAD_DIM ** -0.5)
    p = jax.nn.softmax(s, axis=-1).astype(v.dtype)
    o = jnp.einsum('bhqk,bkhd->bqhd', p, v).reshape(B, S, X_WIDTH)
    return o @ w_co


def causal_dwconv(h, w, b):
    F = h.shape[-1]
    y = lax.conv_general_dilated(h, w[:, None, :], window_strides=(1,),
                                 padding=((CONV_W - 1, 0),),
                                 dimension_numbers=('NWC', 'WIO', 'NWC'),
                                 feature_group_count=F)
    return y + b


def setup_inputs(seed: int = 0) -> dict:
    key = jax.random.key(seed)
    ks = iter(jax.random.split(key, 32))

    def nrm(shape, fan_in):
        return jax.random.normal(next(ks), shape, jnp.float32) * (fan_in ** -0.5)

    def gain(shape):
        return 1.0 + 0.02 * jax.random.normal(next(ks), shape, jnp.float32)

    L = DEPTH
    x = jax.random.normal(next(ks), (BATCH, SEQ, D_MODEL), jnp.float32)
    mem = jax.random.normal(next(ks), (BATCH, MEM_LEN, D_MODEL), jnp.float32)
    start = jax.random.randint(next(ks), (BATCH, 1), 0, 4096, dtype=jnp.int32)
    positions = (start + jnp.arange(SEQ, dtype=jnp.int32)[None, :]).astype(jnp.int32)
    return {
        "x": x,
        "mem": mem,
        "positions": positions,
        "g_mix_pre": gain((L, D_MODEL)),
        "g_mix_post": gain((L, D_MODEL)),
        "w_in": nrm((L, D_MODEL, IN_COLS), D_MODEL),
        "g_q": gain((L, Q_LORA)),
        "w_uq": nrm((L, Q_LORA, MLA_HEADS * (QK_NOPE + QK_ROPE)), Q_LORA),
        "g_kv": gain((L, KV_LORA)),
        "w_ukv": nrm((L, KV_LORA, MLA_HEADS * (QK_NOPE + V_DIM)), KV_LORA),
        "w_pool": nrm((L, len(POOL_WINDOWS), POOL_GROUP, POOL_GROUP), POOL_GROUP),
        "s_pool": 1.0 + 0.1 * jax.random.normal(next(ks), (L, POOL_CH), jnp.float32),
        "w_out": nrm((L, MIX_WIDTH, D_MODEL), MIX_WIDTH),
        "g_x_pre": gain((L, D_MODEL)),
        "g_x_post": gain((L, D_MODEL)),
        "g_mem": gain((L, D_MODEL)),
        "w_cq": nrm((L, D_MODEL, X_WIDTH), D_MODEL),
        "w_ck": nrm((L, D_MODEL, X_WIDTH), D_MODEL),
        "w_cv": nrm((L, D_MODEL, X_WIDTH), D_MODEL),
        "w_co": nrm((L, X_WIDTH, D_MODEL), X_WIDTH),
        "g_ffn_pre": gain((L, D_MODEL)),
        "g_ffn_post": gain((L, D_MODEL)),
        "w_gate": nrm((L, D_MODEL, D_FF), D_MODEL),
        "w_up": nrm((L, D_MODEL, D_FF), D_MODEL),
        "conv_w": nrm((L, CONV_W, D_FF), CONV_W),
        "conv_b": 0.01 * jax.random.normal(next(ks), (L, D_FF), jnp.float32),
        "w_down": nrm((L, D_FF, D_MODEL), D_FF),
    }


def reference(x, mem, positions, g_mix_pre, g_mix_post, w_in, g_q, w_uq, g_kv, w_ukv,
              w_pool, s_pool, w_out, g_x_pre, g_x_post, g_mem, w_cq, w_ck, w_cv, w_co,
              g_ffn_pre, g_ffn_post, w_gate, w_up, conv_w, conv_b, w_down):
    cos, sin = rope_tables(positions)
    c1 = Q_LORA
    c2 = c1 + KV_LORA
    c3 = c2 + QK_ROPE
    for l in range(DEPTH):
        h = rmsnorm(x, g_mix_pre[l])
        z = h @ w_in[l]
        a = mla_group(z[..., :c1], z[..., c1:c2], z[..., c2:c3], cos, sin,
                      g_q[l], w_uq[l], g_kv[l], w_ukv[l])
        p = pool_group(z[..., c3:], w_pool[l], s_pool[l])
        m = jnp.concatenate([a, p], axis=-1) @ w_out[l]
        x = x + rmsnorm(m, g_mix_post[l])
        h = rmsnorm(x, g_x_pre[l])
        mem_n = rmsnorm(mem, g_mem[l])
        c = memory_cross_attention(h, mem_n, w_cq[l], w_ck[l], w_cv[l], w_co[l])
        x = x + rmsnorm(c, g_x_post[l])
        h = rmsnorm(x, g_ffn_pre[l])
        gate = causal_dwconv(h @ w_gate[l], conv_w[l], conv_b[l])
        f = (jax.nn.silu(gate) * (h @ w_up[l])) @ w_down[l]
        x = x + rmsnorm(f, g_ffn_post[l])
    return x
```

```python
import numpy as np
from contextlib import ExitStack
import concourse.bass as bass
import concourse.mybir as mybir
from concourse.bass_utils import run_bass_kernel_spmd
import ml_dtypes

F32 = mybir.dt.float32
BF16 = mybir.dt.bfloat16
I32 = mybir.dt.int32
AF = mybir.ActivationFunctionType
ALU = mybir.AluOpType
NPBF = ml_dtypes.bfloat16
NCORES = 8

D = 4096
HEADS = 16
QLORA = 1024
KVLORA = 512
ROPE = 64
NOPE = 128
VD = 128
POOLCH = 2048
INCOLS = QLORA + KVLORA + ROPE + POOLCH
DFF = 11008
FCH = DFF // 128
MEM = 256
XH = 4
XHD = 256
EPS = 1e-6
PI = float(np.pi)


class DSem:
    def __init__(self, sem):
        self.sem = sem
        self.count = 0


def _flat(deps):
    out = []
    for t in deps:
        if t is None:
            continue
        if isinstance(t, list):
            out.extend(_flat(t))
        else:
            out.append(t)
    return out


class Prog:
    ENGS = ("pe", "act", "dve", "pool", "sp")

    def __init__(self, nc, stack):
        self.nc = nc
        self.stack = stack
        self.ops = {e: [] for e in self.ENGS}
        self.sems = {e: stack.enter_context(nc.semaphore("sem_" + e)) for e in self.ENGS}
        self.cnt = {e: 0 for e in self.ENGS}
        self.waited = {e: {} for e in self.ENGS}
        self.nsem = 0
        self.nt = 0

    def sbuf(self, shape, dt, name=None):
        self.nt += 1
        return self.stack.enter_context(
            self.nc.sbuf_tensor(name or ("sb%d" % self.nt), list(shape), dt))

    def psum(self, shape, dt, name=None):
        self.nt += 1
        return self.stack.enter_context(
            self.nc.psum_tensor(name or ("ps%d" % self.nt), list(shape), dt))

    def dsem(self):
        self.nsem += 1
        return DSem(self.stack.enter_context(self.nc.semaphore("dsem%d" % self.nsem)))

    def _waits(self, eng, deps):
        w = self.waited[eng]
        best = {}
        for sem, val in _flat(deps):
            k = id(sem)
            if k not in best or best[k][1] < val:
                best[k] = (sem, val)
        waits = []
        for k, (sem, val) in best.items():
            if w.get(k, 0) < val:
                w[k] = val
                waits.append((sem, val))
        return waits

    def op(self, eng, fn, deps=(), sig=True):
        waits = self._waits(eng, deps)
        tok = None
        if sig:
            self.cnt[eng] += 1
            tok = (self.sems[eng], self.cnt[eng])
        self.ops[eng].append((waits, fn, 1 if sig else 0, None))
        return tok

    def dma(self, q, out, in_, dsem, deps=()):
        waits = self._waits(q, deps)
        dsem.count += 16
        self.ops[q].append((waits, ("dma", out, in_), 16, dsem.sem))
        return (dsem.sem, dsem.count)

    def emit(self, final_tokens):
        nc = self.nc
        self.op("sp", None, deps=final_tokens, sig=False)
        block = self.stack.enter_context(nc.Block())
        engmap = {"pe": block.tensor, "act": block.scalar, "dve": block.vector,
                  "pool": block.gpsimd, "sp": block.sync}
        for ename in self.ENGS:
            ops = self.ops[ename]
            mysem = self.sems[ename]

            def body(e, ops=ops, mysem=mysem):
                for waits, fn, inc, dsem in ops:
                    for sem, val in waits:
                        e.wait_ge(sem, val)
                    if fn is None:
                        continue
                    if isinstance(fn, tuple):
                        _, out, in_ = fn
                        e.dma_start(out=out, in_=in_).then_inc(dsem, 16)
                    else:
                        ins = fn(e)
                        if inc:
                            ins.then_inc(mysem, 1)
            engmap[ename](body)


class Ring:
    def __init__(self, P, bufs, with_dsem=False):
        self.bufs = bufs
        self.free = [None] * len(bufs)
        self.i = 0
        self.dsems = [P.dsem() for _ in bufs] if with_dsem else None

    def get(self):
        i = self.i
        self.i = (i + 1) % len(self.bufs)
        return i, self.bufs[i], self.free[i]

    def rel(self, i, tok):
        self.free[i] = tok


def build_att(S, HPC=2):
    nc = bass.Bass("TRN2", target_bir_lowering=False)
    NQT = S // 512
    NKT = S // 128
    scale = float((NOPE + ROPE) ** -0.5)
    qn = nc.dram_tensor("qn", [HPC, 128, S], BF16, kind="ExternalInput").ap()
    qr = nc.dram_tensor("qr", [HPC, 64, S], BF16, kind="ExternalInput").ap()
    kn = nc.dram_tensor("kn", [HPC, 128, S], BF16, kind="ExternalInput").ap()
    kr = nc.dram_tensor("kr", [64, S], BF16, kind="ExternalInput").ap()
    v = nc.dram_tensor("v", [S, HPC * 128], BF16, kind="ExternalInput").ap()
    aT = nc.dram_tensor("aT", [HPC * 128, S], BF16, kind="ExternalOutput").ap()
    with ExitStack() as st:
        P = Prog(nc, st)
        kn_sb = P.sbuf([128, S], BF16)
        kr_sb = P.sbuf([64, S], BF16)
        v_sb = P.sbuf([128, NKT, 128], BF16)
        ones = P.sbuf([128, 128], BF16)
        qn_r = Ring(P, [P.sbuf([128, 512], BF16) for _ in range(2)], True)
        qr_r = Ring(P, [P.sbuf([64, 512], BF16) for _ in range(2)], True)
        pt_r = Ring(P, [P.sbuf([128, 512], BF16) for _ in range(4)])
        ps_s = Ring(P, [P.psum([128, 512], F32) for _ in range(3)])
        ps_o = Ring(P, [P.psum([128, 512], F32) for _ in range(2)])
        ps_l = Ring(P, [P.psum([128, 512], F32) for _ in range(2)])
        rl_r = Ring(P, [P.sbuf([128, 512], F32) for _ in range(2)])
        o_r = Ring(P, [P.sbuf([128, 512], BF16) for _ in range(2)], True)
        dk = P.dsem()
        t_ones = P.op("pool", lambda e: e.memset(ones[:], 1.0))
        t_kr = P.dma("sp", kr_sb[:], kr[:, :], dk)
        kv_free = None
        outs = []
        for h in range(HPC):
            t1 = P.dma("sp", kn_sb[:], kn[h], dk, deps=[kv_free])
            for part in range(4):
                r0 = part * (S // 4)
                tkv = P.dma("sp", v_sb[:, part * (NKT // 4):(part + 1) * (NKT // 4), :],
                            v[r0:r0 + S // 4, h * 128:(h + 1) * 128].rearrange("(kt p) d -> p kt d", p=128),
                            dk, deps=[kv_free])
            last_pe = None
            for qt in range(NQT):
                qi, qnb, qfree = qn_r.get()
                tq1 = P.dma("sp", qnb[:], qn[h, :, qt * 512:(qt + 1) * 512], qn_r.dsems[qi], deps=[qfree])
                qj, qrb, qfree2 = qr_r.get()
                tq2 = P.dma("sp", qrb[:], qr[h, :, qt * 512:(qt + 1) * 512], qr_r.dsems[qj], deps=[qfree2])
                oi, pso, ofree = ps_o.get()
                li, psl, lfree = ps_l.get()
                nk = 4 * (qt + 1)
                for kt in range(nk):
                    j = kt - 4 * qt
                    c0 = 128 * j if j > 0 else 0
                    si, pss, sfree = ps_s.get()
                    P.op("pe", lambda e, pss=pss, kt=kt, qnb=qnb, c0=c0: e.matmul(
                        pss[:, c0:], lhsT=kn_sb[:, kt * 128:(kt + 1) * 128], rhs=qnb[:, c0:],
                        start=True, stop=False), deps=[tkv, tq1, tq2, t_kr, sfree], sig=False)
                    tmm = P.op("pe", lambda e, pss=pss, kt=kt, qrb=qrb, c0=c0: e.matmul(
                        pss[:, c0:], lhsT=kr_sb[:, kt * 128:(kt + 1) * 128], rhs=qrb[:, c0:],
                        start=False, stop=True))
                    pi, ptb, pfree = pt_r.get()
                    tex = P.op("act", lambda e, ptb=ptb, pss=pss, c0=c0: e.activation(
                        out=ptb[:, c0:], in_=pss[:, c0:], func=AF.Exp, scale=scale), deps=[tmm, pfree])
                    ps_s.rel(si, tex)
                    tp = tex
                    if j >= 0:
                        tp = P.op("pool", lambda e, ptb=ptb, c0=c0: e.memset(ptb[64:128, c0:c0 + 64], 0.0),
                                  deps=[tex])
                    P.op("pe", lambda e, pso=pso, kt=kt, ptb=ptb, c0=c0, nk=nk: e.matmul(
                        pso[:, c0:], lhsT=v_sb[:, kt, :], rhs=ptb[:, c0:],
                        start=(kt == 0), stop=(kt == nk - 1)), deps=[tp, ofree, lfree, t_ones], sig=False)
                    tpv = P.op("pe", lambda e, psl=psl, kt=kt, ptb=ptb, c0=c0, nk=nk: e.matmul(
                        psl[:, c0:], lhsT=ones[:], rhs=ptb[:, c0:],
                        start=(kt == 0), stop=(kt == nk - 1)))
                    pt_r.rel(pi, tpv)
                    last_pe = tpv
                qn_r.rel(qi, last_pe)
                qr_r.rel(qj, last_pe)
                ri, rlb, rfree = rl_r.get()
                trl = P.op("dve", lambda e, rlb=rlb, psl=psl: e.reciprocal(out=rlb[:], in_=psl[:]),
                           deps=[last_pe, rfree])
                ps_l.rel(li, trl)
                ob_i, ob, obfree = o_r.get()
                tmul = P.op("dve", lambda e, ob=ob, pso=pso, rlb=rlb: e.tensor_tensor(
                    out=ob[:], in0=pso[:], in1=rlb[:], op=ALU.mult), deps=[trl, obfree])
                ps_o.rel(oi, tmul)
                rl_r.rel(ri, tmul)
                tst = P.dma("sp", aT[h * 128:(h + 1) * 128, qt * 512:(qt + 1) * 512], ob[:],
                            o_r.dsems[ob_i], deps=[tmul])
                o_r.rel(ob_i, tst)
                outs.append(tst)
            kv_free = last_pe
        P.emit(outs[-2:])
    return nc


def wlayout(spec):
    offs = {}
    o = 0
    for name, R, C in spec:
        offs[name] = (o, R, C)
        o += R * C
    blk = 1024 * 2048
    tot = ((o + blk - 1) // blk) * blk
    return offs, tot // 1024


def host_wpieces(spec, arrays, Lq):
    flat = np.zeros(1024 * Lq, np.float32)
    o = 0
    for name, R, C in spec:
        flat[o:o + R * C] = np.asarray(arrays[name], np.float32).reshape(-1)
        o += R * C
    return flat.reshape(NCORES, 128, Lq)


def wprep(P, nc, wpiece, Lq):
    CW = 512
    piece_bf = nc.dram_tensor("wpiece_bf", [128, Lq], BF16)
    wflat = nc.dram_tensor("wflat", [NCORES * 128, Lq], BF16)
    st_r = Ring(P, [P.sbuf([128, CW], F32) for _ in range(2)], True)
    bf_r = Ring(P, [P.sbuf([128, CW], BF16) for _ in range(2)], True)
    engs = ["act", "dve"]
    toks = []
    for ci in range(Lq // CW):
        si, sb, sfree = st_r.get()
        tl = P.dma("sp", sb[:], wpiece[:, ci * CW:(ci + 1) * CW], st_r.dsems[si], deps=[sfree])
        bi, bb, bfree = bf_r.get()
        eng = engs[ci % 2]
        if eng == "act":
            tc = P.op("act", lambda e, bb=bb, sb=sb: e.activation(out=bb[:], in_=sb[:], func=AF.Copy),
                      deps=[tl, bfree])
        else:
            tc = P.op(eng, lambda e, bb=bb, sb=sb: e.tensor_copy(out=bb[:], in_=sb[:]), deps=[tl, bfree])
        st_r.rel(si, tc)
        ts = P.dma("sp", piece_bf.ap()[:, ci * CW:(ci + 1) * CW], bb[:], bf_r.dsems[bi], deps=[tc])
        bf_r.rel(bi, ts)
        toks.append(ts)
    tg = P.op("pool", lambda e: e.collective_compute(
        "AllGather", ALU.bypass, replica_groups=[list(range(NCORES))],
        ins=[piece_bf.ap().opt()], outs=[wflat.ap().opt()]), deps=toks[-2:])
    return wflat, tg


def wchunk_ap(wflat, off, C, KC, f0, M, k0=0):
    return bass.AP(wflat, off + k0 * C + f0, [[C, 128], [128 * C, KC], [1, M]])


class Ctx:
    pass


def gemm(P, C, wloads, KC, fchunks, rhs_fn, groups, gwidth, epi, wring, rhs_tok, krows=128):
    for fc, M in fchunks:
        wi, wb, wfree = wring.get()
        wtok = None
        for (o, i) in wloads(fc, wb):
            wtok = P.dma("sp", o, i, wring.dsems[wi], deps=[wfree])
        last = None
        for g in groups(fc):
            pi, ps, pfree = C.psg.get()
            n = gwidth(g)
            for kc in range(KC):
                last = P.op("pe", lambda e, ps=ps, wb=wb, kc=kc, M=M, n=n, g=g: e.matmul(
                    ps[:M, :n], lhsT=wb[:krows, kc, :M], rhs=rhs_fn(kc, g),
                    start=(kc == 0), stop=(kc == KC - 1)),
                    deps=[wtok, rhs_tok, pfree] if kc == 0 else [], sig=(kc == KC - 1))
            tok = epi(fc, M, g, ps, last)
            C.psg.rel(pi, tok)
        wring.rel(wi, last)
    return last


def rstd_bc(P, C, ps_ap, n, Dn, out_ap, deps):
    t = P.op("act", lambda e: e.activation(out=out_ap, in_=ps_ap, func=AF.Sqrt,
                                           bias=C.eps[:, 0:1], scale=1.0 / Dn), deps=deps)
    t2 = P.op("dve", lambda e: e.reciprocal(out=out_ap, in_=out_ap), deps=[t])
    return t, t2


A_SPEC = [("w_in", D, INCOLS), ("w_uq", QLORA, HEADS * 192), ("w_ukv", KVLORA, HEADS * 256),
          ("w_pool", 4 * 512, 512)]


def build_a(T):
    nc = bass.Bass("TRN2", target_bir_lowering=False)
    NT = T // 512
    offs, Lq = wlayout(A_SPEC)
    xT_t = nc.dram_tensor("xT", [D, 16 + T], F32, kind="ExternalInput")
    xT = xT_t.ap()
    w_in_t = nc.dram_tensor("w_in", [D, INCOLS], BF16, kind="ExternalInput")
    w_uq_t = nc.dram_tensor("w_uq", [QLORA, HEADS * 192], BF16, kind="ExternalInput")
    w_ukv_t = nc.dram_tensor("w_ukv", [KVLORA, HEADS * 256], BF16, kind="ExternalInput")
    w_pool_t = nc.dram_tensor("w_pool", [2048, 512], BF16, kind="ExternalInput")
    pos_t = nc.dram_tensor("pos", [1, T], I32, kind="ExternalInput")
    tix_t = nc.dram_tensor("tix", [1, T], F32, kind="ExternalInput")
    gpre = nc.dram_tensor("gpre", [128, 32], F32, kind="ExternalInput").ap()
    gq = nc.dram_tensor("gq", [128, 8], F32, kind="ExternalInput").ap()
    gkv = nc.dram_tensor("gkv", [128, 4], F32, kind="ExternalInput").ap()
    spool = nc.dram_tensor("spool", [128, 16], F32, kind="ExternalInput").ap()
    invf = nc.dram_tensor("invf", [64, 1], F32, kind="ExternalInput").ap()
    sgn = nc.dram_tensor("sgn", [64, 1], F32, kind="ExternalInput").ap()
    qn_o = nc.dram_tensor("qn_o", [HEADS, 128, T], BF16, kind="ExternalOutput").ap()
    qr_o = nc.dram_tensor("qr_o", [HEADS, 64, T], BF16, kind="ExternalOutput").ap()
    kn_o = nc.dram_tensor("kn_o", [HEADS, 128, T], BF16, kind="ExternalOutput").ap()
    kr_o = nc.dram_tensor("kr_o", [64, T], BF16, kind="ExternalOutput").ap()
    v_o = nc.dram_tensor("v_o", [T, HEADS * 128], BF16, kind="ExternalOutput").ap()
    y_o = nc.dram_tensor("y_o", [POOLCH, T], BF16, kind="ExternalOutput").ap()
    with ExitStack() as st:
        P = Prog(nc, st)
        C = Ctx()
        tg = None
        C.eps = P.sbuf([128, 1], F32)
        negpi = P.sbuf([64, 1], F32)
        ones = P.sbuf([128, 128], BF16)
        gpre_sb = P.sbuf([128, 32], F32)
        gq_sb = P.sbuf([128, 8], F32)
        gkv_sb = P.sbuf([128, 4], F32)
        sp_sb = P.sbuf([128, 16], F32)
        invf_sb = P.sbuf([64, 1], F32)
        sgn_sb = P.sbuf([64, 1], F32)
        dc = P.dsem()
        tcs = [P.op("pool", lambda e: e.memset(C.eps[:], EPS)),
               P.op("pool", lambda e: e.memset(negpi[:], -PI)),
               P.op("pool", lambda e: e.memset(ones[:], 1.0))]
        for sb, src in ((gpre_sb, gpre), (gq_sb, gq), (gkv_sb, gkv), (sp_sb, spool), (invf_sb, invf), (sgn_sb, sgn)):
            tcd = P.dma("sp", sb[:], src[:, :], dc)
        tconst = tcs + [tcd]
        hT = P.sbuf([128, 32, 528], BF16)
        pT = P.sbuf([128, 16, 512], BF16)
        cq32 = P.sbuf([128, 8, 512], F32)
        ckv32 = P.sbuf([128, 4, 512], F32)
        cqn = P.sbuf([128, 8, 512], BF16)
        ckvn = P.sbuf([128, 4, 512], BF16)
        wv_sb = P.sbuf([128, 4, HEADS * 128], BF16)
        rstd = P.sbuf([128, 528], F32)
        rstd2 = P.sbuf([128, 512], F32)
        cos2 = P.sbuf([64, 512], F32)
        sinS = P.sbuf([64, 512], F32)
        posi = P.sbuf([64, 512], I32)
        ang = P.sbuf([64, 512], F32)
        angm = P.sbuf([64, 512], F32)
        ang2 = P.sbuf([64, 512], F32)
        tixb = P.sbuf([128, 512], F32)
        icnt = P.sbuf([128, 4, 512], F32)
        kr32 = P.sbuf([64, 512], F32)
        krs32 = P.sbuf([64, 512], F32)
        xs_r = Ring(P, [P.sbuf([128, 528], F32) for _ in range(3)], True)
        sq_r = Ring(P, [P.sbuf([128, 528], BF16) for _ in range(9)])
        u_r = Ring(P, [P.sbuf([128, 528], F32) for _ in range(2)])
        s_r = Ring(P, [P.sbuf([128, 528], F32) for _ in range(3)])
        t_r = Ring(P, [P.sbuf([128, 512], F32) for _ in range(4)])
        ob_r = Ring(P, [P.sbuf([128, 512], BF16) for _ in range(4)], True)
        win_r = Ring(P, [P.sbuf([128, 32, 128], BF16) for _ in range(2)], True)
        wq_r = Ring(P, [P.sbuf([128, 8, 256], BF16) for _ in range(2)], True)
        wk_r = Ring(P, [P.sbuf([128, 4, 128], BF16) for _ in range(2)], True)
        C.psg = Ring(P, [P.psum([128, 512], F32) for _ in range(5)])
        ps_ssm = P.psum([128, 512], F32)
        ps_ssh = P.psum([128, 512], F32)
        ps_ss2 = P.psum([128, 512], F32)
        ss_free = None
        ss2_free = None
        dwv = P.dsem()
        for kc in range(4):
            twv = P.dma("sp", wv_sb[:, kc, :].rearrange("p (h d) -> p h d", h=HEADS),
                        bass.AP(w_ukv_t, kc * 128 * 4096 + 128, [[4096, 128], [256, HEADS], [1, 128]]),
                        dwv, deps=[tg])
        outs = []
        hT_free = None
        cq_free = None
        pT_free = None
        pe_prev = None
        for j in range(NT):
            c0 = j * 512
            dtab = P.dsem()
            tp1 = P.dma("sp", posi[:], bass.AP(pos_t, c0, [[0, 64], [1, 512]]), dtab, deps=[cq_free])
            tp2 = P.dma("sp", tixb[:], bass.AP(tix_t, c0, [[0, 128], [1, 512]]), dtab, deps=[cq_free])
            ta = P.op("dve", lambda e: e.tensor_copy(out=ang[:], in_=posi[:]), deps=[tp2, tconst])
            ta = P.op("dve", lambda e: e.tensor_scalar(out=ang[:], in0=ang[:], scalar1=invf_sb[:, 0:1], scalar2=None,
                                                       op0=ALU.mult), deps=[ta])
            def sin_of(dst, shift, dep):
                t0 = P.op("dve", lambda e: e.tensor_scalar(out=angm[:], in0=ang[:], scalar1=shift, scalar2=1.0 / (2 * PI),
                                                           op0=ALU.add, op1=ALU.mult), deps=[dep])
                t0 = P.op("dve", lambda e: e.tensor_copy(out=posi[:], in_=angm[:]), deps=[t0])
                t0 = P.op("dve", lambda e: e.tensor_copy(out=angm[:], in_=posi[:]), deps=[t0])
                t0 = P.op("dve", lambda e: e.tensor_scalar(out=angm[:], in0=angm[:], scalar1=-2 * PI, scalar2=shift,
                                                           op0=ALU.mult, op1=ALU.add), deps=[t0])
                t0 = P.op("dve", lambda e: e.tensor_tensor(out=angm[:], in0=angm[:], in1=ang[:], op=ALU.add), deps=[t0])
                t1 = P.op("dve", lambda e: e.tensor_scalar(out=ang2[:], in0=angm[:], scalar1=PI, scalar2=-2 * PI,
                                                           op0=ALU.is_gt, op1=ALU.mult), deps=[t0])
                t1 = P.op("dve", lambda e: e.tensor_tensor(out=angm[:], in0=angm[:], in1=ang2[:], op=ALU.add), deps=[t1])
                t1 = P.op("dve", lambda e: e.tensor_scalar(out=ang2[:], in0=angm[:], scalar1=-PI, scalar2=2 * PI,
                                                           op0=ALU.is_lt, op1=ALU.mult), deps=[t1])
                t1 = P.op("dve", lambda e: e.tensor_tensor(out=angm[:], in0=angm[:], in1=ang2[:], op=ALU.add), deps=[t1])
                return P.op("act", lambda e: e.activation(out=dst[:], in_=angm[:], func=AF.Sin), deps=[t1])

            tsin = sin_of(sinS, 0.0, ta)
            tcos = sin_of(cos2, 0.5 * PI, tsin)
            tsin = P.op("dve", lambda e: e.tensor_scalar(out=sinS[:], in0=sinS[:], scalar1=sgn_sb[:, 0:1],
                                                         scalar2=None, op0=ALU.mult), deps=[tsin])
            trope = [tsin, tcos]
            tic = None
            for g, w in enumerate((2, 4, 8, 16)):
                tic = P.op("dve", lambda e, g=g, w=w: e.tensor_scalar(
                    out=icnt[:, g, :], in0=tixb[:], scalar1=1.0, scalar2=float(w), op0=ALU.add, op1=ALU.min),
                    deps=[tp2, tic])
                tic = P.op("dve", lambda e, g=g: e.reciprocal(out=icnt[:, g, :], in_=icnt[:, g, :]), deps=[tic])
            sqt = None
            for c in range(32):
                xi, xb, xfree = xs_r.get()
                tl = P.dma("sp", xb[:], xT[c * 128:(c + 1) * 128, c0:c0 + 528], xs_r.dsems[xi], deps=[xfree])
                qi, qb, qfree = sq_r.get()
                tsq = P.op("act", lambda e, qb=qb, xb=xb: e.activation(out=qb[:], in_=xb[:], func=AF.Square),
                           deps=[tl, qfree])
                xs_r.rel(xi, tsq)
                P.op("pe", lambda e, qb=qb, c=c: e.matmul(ps_ssm[:, :], lhsT=ones[:], rhs=qb[:, 16:528],
                                                        start=(c == 0), stop=(c == 31)),
                     deps=[tsq, ss_free, tconst], sig=False)
                tmm = P.op("pe", lambda e, qb=qb, c=c: e.matmul(ps_ssh[:, :16], lhsT=ones[:], rhs=qb[:, 0:16],
                                                              start=(c == 0), stop=(c == 31)))
                sq_r.rel(qi, tmm)
            ta1, tr1 = rstd_bc(P, C, ps_ssm[:, :], 512, D, rstd[:, 16:528], [tmm, hT_free])
            ta2, tr2 = rstd_bc(P, C, ps_ssh[:, :16], 16, D, rstd[:, 0:16], [tmm])
            ss_free = [ta1, ta2]
            th = None
            for c in range(32):
                xi, xb, xfree = xs_r.get()
                tl = P.dma("sp", xb[:], xT[c * 128:(c + 1) * 128, c0:c0 + 528], xs_r.dsems[xi], deps=[xfree])
                th = P.op("dve", lambda e, xb=xb, c=c: e.scalar_tensor_tensor(
                    out=hT[:, c, :], in0=xb[:], scalar=gpre_sb[:, c:c + 1], in1=rstd[:], op0=ALU.mult, op1=ALU.mult),
                    deps=[tl, tr1, tr2, hT_free])
                xs_r.rel(xi, th)
                if c == 30:
                    th30 = th
            th_all = [th, th30]
            pend = []

            def win_loads(fc, wb):
                kind, idx = fc
                if kind == "cq":
                    return [(wb[:, :, :], wchunk_ap(w_in_t, 0, INCOLS, 32, idx * 128, 128))]
                if kind == "ckv":
                    return [(wb[:, :, :], wchunk_ap(w_in_t, 0, INCOLS, 32, 1024 + idx * 128, 128))]
                if kind == "kr":
                    return [(wb[:, :, 0:64], wchunk_ap(w_in_t, 0, INCOLS, 32, 1536, 64))]
                if kind == "krs":
                    return [(wb[:, :, 0:32], wchunk_ap(w_in_t, 0, INCOLS, 32, 1536 + 32, 32)),
                            (wb[:, :, 32:64], wchunk_ap(w_in_t, 0, INCOLS, 32, 1536, 32))]
                return [(wb[:, :, :], wchunk_ap(w_in_t, 0, INCOLS, 32, 1600 + idx * 128, 128))]

            ust = {}

            def win_epi(fc, M, g, ps, mm):
                kind, idx = fc
                if kind in ("cq", "ckv"):
                    dst = cq32 if kind == "cq" else ckv32
                    t1 = P.op("act", lambda e: e.activation(out=dst[:, idx, :], in_=ps[:, :], func=AF.Copy),
                              deps=[mm, cq_free])
                    qi, qb, qfree = sq_r.get()
                    t2 = P.op("act", lambda e: e.activation(out=qb[:, 0:512], in_=ps[:, :], func=AF.Square),
                              deps=[qfree])
                    pend.append((qi, qb, t2))
                    return t2
                if kind in ("kr", "krs"):
                    dst = kr32 if kind == "kr" else krs32
                    return P.op("act", lambda e: e.activation(out=dst[:, :], in_=ps[:64, :], func=AF.Copy),
                                deps=[mm, cq_free])
                if g == "halo":
                    ui, ub, ufree = u_r.get()
                    ust["u"] = (ui, ub)
                    t = P.op("act", lambda e: e.activation(out=ub[:, 0:16], in_=ps[:, 0:16], func=AF.Copy),
                             deps=[mm, ufree])
                    ust["t"] = t
                    return t
                ui, ub = ust["u"]
                tu = P.op("act", lambda e: e.activation(out=ub[:, 16:528], in_=ps[:, :], func=AF.Copy), deps=[mm])
                grp = idx // 4
                cur = ub
                tcur = [tu, ust["t"]]
                sh = 1
                rel = []
                for step in range(grp + 1):
                    si, sb, sfree = s_r.get()
                    eng = "dve" if (idx + step) % 2 == 0 else "pool"
                    tn = P.op(eng, lambda e, sb=sb, cur=cur, sh=sh: e.tensor_tensor(
                        out=sb[:, sh:528], in0=cur[:, sh:528], in1=cur[:, 0:528 - sh], op=ALU.add),
                        deps=[tcur, sfree])
                    if step > 0:
                        s_r.rel(psi, tn)
                    psi = si
                    cur = sb
                    tcur = [tn]
                    sh *= 2
                ti, tb_, tfree = t_r.get()
                tm = P.op("dve", lambda e, tb_=tb_, cur=cur, grp=grp: e.tensor_tensor(
                    out=tb_[:], in0=cur[:, 16:528], in1=icnt[:, grp, :], op=ALU.mult), deps=[tcur, tfree, tic])
                s_r.rel(psi, tm)
                tp = P.op("pool", lambda e, tb_=tb_, ub=ub: e.tensor_tensor(
                    out=pT[:, idx, :], in0=tb_[:], in1=ub[:, 16:528], op=ALU.subtract), deps=[tm, pT_free])
                t_r.rel(ti, tp)
                u_r.rel(ui, tp)
                ust["last"] = tp
                return tu

            fch = [(("cq", i), 128) for i in range(8)]
            gemm(P, C, win_loads, 32, fch, lambda kc, g: hT[:, kc, 16:528], lambda fc: ["main"], lambda g: 512,
                 win_epi, win_r, [th_all, tg])
            for n_, (qi, qb, t2) in enumerate(pend):
                tmm = P.op("pe", lambda e, qb=qb, n_=n_: e.matmul(ps_ss2[:, :], lhsT=ones[:], rhs=qb[:, 0:512],
                                                                start=(n_ == 0), stop=(n_ == 7)),
                           deps=[t2, ss2_free])
                sq_r.rel(qi, tmm)
            pend.clear()
            ta_, trq = rstd_bc(P, C, ps_ss2[:, :], 512, QLORA, rstd2[:, :], [tmm])
            ss2_free = ta_
            tq = None
            for i in range(8):
                tq = P.op("dve", lambda e, i=i: e.scalar_tensor_tensor(
                    out=cqn[:, i, :], in0=cq32[:, i, :], scalar=gq_sb[:, i:i + 1], in1=rstd2[:], op0=ALU.mult,
                    op1=ALU.mult), deps=[trq, pe_prev])
            fch = [(("ckv", i), 128) for i in range(4)] + [(("kr", 0), 64), (("krs", 0), 64)]
            gemm(P, C, win_loads, 32, fch, lambda kc, g: hT[:, kc, 16:528], lambda fc: ["main"], lambda g: 512,
                 win_epi, win_r, [th_all, tg])
            for n_, (qi, qb, t2) in enumerate(pend):
                tmm = P.op("pe", lambda e, qb=qb, n_=n_: e.matmul(ps_ss2[:, :], lhsT=ones[:], rhs=qb[:, 0:512],
                                                                start=(n_ == 0), stop=(n_ == 3)),
                           deps=[t2, ss2_free, tq])
                sq_r.rel(qi, tmm)
            pend.clear()
            fch = [(("pool", i), 128) for i in range(16)]
            hT_free = gemm(P, C, win_loads, 32, fch,
                           lambda kc, g: hT[:, kc, 16:528] if g == "main" else hT[:, kc, 0:16],
                           lambda fc: ["halo", "main"], lambda g: 512 if g == "main" else 16,
                           win_epi, win_r, [th_all, tg])
            ti, tb_, tfree = t_r.get()
            t1 = P.op("dve", lambda e, tb_=tb_: e.tensor_tensor(out=tb_[:64, :], in0=kr32[:], in1=cos2[:], op=ALU.mult),
                      deps=[trope, tfree, C.psg.free])
            ti2, tb2_, tfree2 = t_r.get()
            t2 = P.op("dve", lambda e, tb2_=tb2_: e.tensor_tensor(out=tb2_[:64, :], in0=krs32[:], in1=sinS[:],
                                                                 op=ALU.mult), deps=[trope, tfree2])
            oi, ob, ofree = ob_r.get()
            t3 = P.op("dve", lambda e, ob=ob, tb_=tb_, tb2_=tb2_: e.tensor_tensor(
                out=ob[:64, :], in0=tb_[:64, :], in1=tb2_[:64, :], op=ALU.add), deps=[t1, t2, ofree])
            t_r.rel(ti, t3)
            t_r.rel(ti2, t3)
            ts = P.dma("sp", kr_o[:, c0:c0 + 512], ob[:64, :], ob_r.dsems[oi], deps=[t3])
            ob_r.rel(oi, ts)
            outs.append(ts)
            qst = {}

            def wq_loads(fc, wb):
                h = fc
                return [(wb[:, :, 0:192], wchunk_ap(w_uq_t, 0, HEADS * 192, 8, h * 192, 192)),
                        (wb[:, :, 192:224], wchunk_ap(w_uq_t, 0, HEADS * 192, 8, h * 192 + 160, 32)),
                        (wb[:, :, 224:256], wchunk_ap(w_uq_t, 0, HEADS * 192, 8, h * 192 + 128, 32))]

            for h in range(HEADS):
                wi, wb, wfree = wq_r.get()
                for (o, i_) in wq_loads(h, wb):
                    wtok = P.dma("sp", o, i_, wq_r.dsems[wi], deps=[wfree, tg])
                res = []
                for (m0, M) in ((0, 128), (128, 64), (192, 64)):
                    pi, ps, pfree = C.psg.get()
                    for kc in range(8):
                        last = P.op("pe", lambda e, ps=ps, wb=wb, kc=kc, m0=m0, M=M: e.matmul(
                            ps[:M, :], lhsT=wb[:, kc, m0:m0 + M], rhs=cqn[:, kc, :], start=(kc == 0), stop=(kc == 7)),
                            deps=[wtok, tq, pfree] if kc == 0 else [], sig=(kc == 7))
                    res.append((pi, ps, last))
                wq_r.rel(wi, last)
                (p0, ps0, l0), (p1, ps1, l1), (p2, ps2, l2) = res
                oi, ob, ofree = ob_r.get()
                tn = P.op("act", lambda e, ob=ob, ps0=ps0: e.activation(out=ob[:], in_=ps0[:, :], func=AF.Copy),
                          deps=[l0, ofree])
                C.psg.rel(p0, tn)
                ts = P.dma("sp", qn_o[h, :, c0:c0 + 512], ob[:], ob_r.dsems[oi], deps=[tn])
                ob_r.rel(oi, ts)
                ti, tb_, tfree = t_r.get()
                t1 = P.op("dve", lambda e, tb_=tb_, ps1=ps1: e.tensor_tensor(out=tb_[:64, :], in0=ps1[:64, :],
                                                                           in1=cos2[:], op=ALU.mult),
                          deps=[l1, trope, tfree])
                C.psg.rel(p1, t1)
                ti2, tb2_, tfree2 = t_r.get()
                t2 = P.op("dve", lambda e, tb2_=tb2_, ps2=ps2: e.tensor_tensor(out=tb2_[:64, :], in0=ps2[:64, :],
                                                                             in1=sinS[:], op=ALU.mult),
                          deps=[l2, tfree2])
                C.psg.rel(p2, t2)
                oi, ob, ofree = ob_r.get()
                t3 = P.op("pool", lambda e, ob=ob, tb_=tb_, tb2_=tb2_: e.tensor_tensor(
                    out=ob[:64, :], in0=tb_[:64, :], in1=tb2_[:64, :], op=ALU.add), deps=[t1, t2, ofree])
                t_r.rel(ti, t3)
                t_r.rel(ti2, t3)
                ts = P.dma("sp", qr_o[h, :, c0:c0 + 512], ob[:64, :], ob_r.dsems[oi], deps=[t3])
                ob_r.rel(oi, ts)
                outs.append(ts)
            ta_, trk = rstd_bc(P, C, ps_ss2[:, :], 512, KVLORA, rstd2[:, :], [tmm, tq])
            ss2_free = ta_
            tkv = None
            for i in range(4):
                tkv = P.op("dve", lambda e, i=i: e.scalar_tensor_tensor(
                    out=ckvn[:, i, :], in0=ckv32[:, i, :], scalar=gkv_sb[:, i:i + 1], in1=rstd2[:], op0=ALU.mult,
                    op1=ALU.mult), deps=[trk, pe_prev])
            cq_free = tkv
            def wk_loads(fc, wb):
                return [(wb[:, :, :], wchunk_ap(w_ukv_t, 0, HEADS * 256, 4, fc * 256, 128))]

            def k_epi(fc, M, g, ps, mm):
                oi, ob, ofree = ob_r.get()
                tn = P.op("act", lambda e: e.activation(out=ob[:], in_=ps[:, :], func=AF.Copy), deps=[mm, ofree])
                ts = P.dma("sp", kn_o[fc, :, c0:c0 + 512], ob[:], ob_r.dsems[oi], deps=[tn])
                ob_r.rel(oi, ts)
                outs.append(ts)
                return tn

            gemm(P, C, wk_loads, 4, [(h, 128) for h in range(HEADS)], lambda kc, g: ckvn[:, kc, :],
                 lambda fc: ["main"], lambda g: 512, k_epi, wk_r, [tkv, tg])
            for s in range(4):
                for cg in range(4):
                    pi, ps, pfree = C.psg.get()
                    for kc in range(4):
                        last = P.op("pe", lambda e, ps=ps, kc=kc, s=s, cg=cg: e.matmul(
                            ps[:, :], lhsT=ckvn[:, kc, s * 128:(s + 1) * 128], rhs=wv_sb[:, kc, cg * 512:(cg + 1) * 512],
                            start=(kc == 0), stop=(kc == 3)), deps=[tkv, twv, pfree] if kc == 0 else [],
                            sig=(kc == 3))
                    oi, ob, ofree = ob_r.get()
                    tn = P.op("act", lambda e, ob=ob, ps=ps: e.activation(out=ob[:], in_=ps[:, :], func=AF.Copy),
                              deps=[last, ofree])
                    C.psg.rel(pi, tn)
                    ts = P.dma("sp", v_o[c0 + s * 128:c0 + (s + 1) * 128, cg * 512:(cg + 1) * 512], ob[:],
                               ob_r.dsems[oi], deps=[tn])
                    ob_r.rel(oi, ts)
                    outs.append(ts)
            def wp_loads(fc, wb):
                g_, fi = fc
                return [(wb[:, :, :], wchunk_ap(w_pool_t, g_ * 512 * 512, 512, 4, fi * 128, 128))]

            def p_epi(fc, M, g, ps, mm):
                g_, fi = fc
                ch = g_ * 4 + fi
                oi, ob, ofree = ob_r.get()
                tn = P.op("act", lambda e: e.activation(out=ob[:], in_=ps[:, :], func=AF.Copy,
                                                        scale=sp_sb[:, ch:ch + 1]), deps=[mm, ofree])
                ts = P.dma("sp", y_o[ch * 128:(ch + 1) * 128, c0:c0 + 512], ob[:], ob_r.dsems[oi], deps=[tn])
                ob_r.rel(oi, ts)
                outs.append(ts)
                return tn

            for g_ in range(4):
                pe_last = gemm(P, C, wp_loads, 4, [((g_, fi), 128) for fi in range(4)],
                               lambda kc, g, g_=g_: pT[:, g_ * 4 + kc, :], lambda fc: ["main"], lambda g: 512,
                               p_epi, wk_r, [ust["last"], tg])
            pT_free = pe_last
            pe_prev = pe_last
        P.emit(outs[-8:])
    return nc


def wsl(w, KC, f0, M, k0=0):
    return w[k0:k0 + KC * 128, f0:f0 + M].rearrange("(kc p) m -> p kc m", p=128)


def build_b(T):
    nc = bass.Bass("TRN2", target_bir_lowering=False)
    NT = T // 512
    TW = 16 + T
    xT = nc.dram_tensor("xT", [D, TW], F32, kind="ExternalInput").ap()
    aT = nc.dram_tensor("aT", [2048, TW], BF16, kind="ExternalInput").ap()
    yT = nc.dram_tensor("yT", [2048, TW], BF16, kind="ExternalInput").ap()
    memT = nc.dram_tensor("memT", [D, MEM], F32, kind="ExternalInput").ap()
    gains = nc.dram_tensor("gains", [128, 6, 32], F32, kind="ExternalInput").ap()
    cwb = nc.dram_tensor("cwb", [128, FCH, 4], F32, kind="ExternalInput").ap()
    hflag = nc.dram_tensor("hflag", [128, 1], F32, kind="ExternalInput").ap()
    w_out = nc.dram_tensor("w_out", [D, D], BF16, kind="ExternalInput").ap()
    w_cq = nc.dram_tensor("w_cq", [D, 1024], BF16, kind="ExternalInput").ap()
    w_ck = nc.dram_tensor("w_ck", [D, 1024], BF16, kind="ExternalInput").ap()
    w_cv = nc.dram_tensor("w_cv", [D, 1024], BF16, kind="ExternalInput").ap()
    w_co = nc.dram_tensor("w_co", [1024, D], BF16, kind="ExternalInput").ap()
    w_gate = nc.dram_tensor("w_gate", [D, DFF], BF16, kind="ExternalInput").ap()
    w_up = nc.dram_tensor("w_up", [D, DFF], BF16, kind="ExternalInput").ap()
    w_down = nc.dram_tensor("w_down", [DFF, D], BF16, kind="ExternalInput").ap()
    xo = nc.dram_tensor("xo", [D, T], F32, kind="ExternalOutput").ap()
    br = nc.dram_tensor("br", [D, 512], F32).ap()
    xa = nc.dram_tensor("xa", [D, 512], F32).ap()
    xb = nc.dram_tensor("xb", [D, 512], F32).ap()
    hidd = nc.dram_tensor("hidd", [128, FCH, 512], BF16).ap()
    xscale = float(XHD ** -0.5)
    with ExitStack() as st:
        P = Prog(nc, st)
        C = Ctx()
        C.eps = P.sbuf([128, 1], F32)
        ones = P.sbuf([128, 128], BF16)
        g_sb = P.sbuf([128, 6, 32], F32)
        cw_sb = P.sbuf([128, FCH, 4], F32)
        hf_sb = P.sbuf([128, 1], F32)
        dc = P.dsem()
        P.dma("sp", hf_sb[:], hflag[:, :], dc)
        tconst = [P.op("pool", lambda e: e.memset(C.eps[:], EPS)),
                  P.op("pool", lambda e: e.memset(ones[:], 1.0)),
                  P.dma("sp", g_sb[:], gains[:, :, :], dc), P.dma("sp", cw_sb[:], cwb[:, :, :], dc)]
        tconst = [tconst[0], tconst[1], tconst[3]]
        hidb = P.sbuf([128, FCH, 512], BF16)
        scr = P.sbuf([128, 28672], BF16)
        hT = scr[:, 0:16384].rearrange("p (c n) -> p c n", n=512)
        wA = [scr[:, 16384 + i * 4096:16384 + (i + 1) * 4096].rearrange("p (k m) -> p k m", m=128) for i in range(3)]
        wD = [scr[:, i * 11008:(i + 1) * 11008].rearrange("p (k m) -> p k m", m=128) for i in range(2)]
        wA_r = Ring(P, wA, True)
        wD_r = Ring(P, wD, True)
        in1 = hidb[:, 0:32, :]
        qx = hidb[:, 32:40, :]
        ox = hidb[:, 40:48, :]
        kx_sb = P.sbuf([128, 8, MEM], BF16)
        vx_sb = P.sbuf([128, 2, 1024], BF16)
        rstd = P.sbuf([128, 512], F32)
        ghalo = P.sbuf([128, FCH, 2], F32)
        xs_r = Ring(P, [P.sbuf([128, 512], F32) for _ in range(3)], True)
        bs_r = Ring(P, [P.sbuf([128, 512], F32) for _ in range(3)], True)
        sq_r = Ring(P, [P.sbuf([128, 512], BF16) for _ in range(4)])
        st_r = Ring(P, [P.sbuf([128, 512], F32) for _ in range(3)], True)
        gb_r = Ring(P, [P.sbuf([128, 514], F32) for _ in range(2)])
        t_r = Ring(P, [P.sbuf([128, 512], F32) for _ in range(3)])
        pt_r = Ring(P, [P.sbuf([128, 512], BF16) for _ in range(2)])
        hb_r = Ring(P, [P.sbuf([128, 512], BF16) for _ in range(3)], True)
        C.psg = Ring(P, [P.psum([128, 512], F32) for _ in range(5)])
        ps_ss = P.psum([128, 512], F32)
        ps_o2 = [P.psum([128, 512], F32) for _ in range(2)]
        state = {"ss_free": None, "scr_free": None, "hid_free": None, "o2_free": None}
        dmisc = P.dsem()

        def prenorm(src, gi, n, dst, extra):
            tmm = None
            for c in range(32):
                xi, xbuf, xfree = xs_r.get()
                tl = P.dma("sp", xbuf[:, :n], src(c), xs_r.dsems[xi], deps=[xfree, extra])
                qi, qb, qfree = sq_r.get()
                tsq = P.op("act", lambda e, qb=qb, xbuf=xbuf: e.activation(out=qb[:, :n], in_=xbuf[:, :n],
                                                                          func=AF.Square), deps=[tl, qfree])
                xs_r.rel(xi, tsq)
                tmm = P.op("pe", lambda e, qb=qb, c=c: e.matmul(ps_ss[:, :n], lhsT=ones[:], rhs=qb[:, :n],
                                                              start=(c == 0), stop=(c == 31)),
                           deps=[tsq, state["ss_free"], tconst])
                sq_r.rel(qi, tmm)
            ta, tr = rstd_bc(P, C, ps_ss[:, :n], n, D, rstd[:, :n], [tmm])
            state["ss_free"] = ta
            th = None
            for c in range(32):
                xi, xbuf, xfree = xs_r.get()
                tl = P.dma("sp", xbuf[:, :n], src(c), xs_r.dsems[xi], deps=[xfree])
                th = P.op("dve", lambda e, xbuf=xbuf, c=c: e.scalar_tensor_tensor(
                    out=dst[:, c, :n], in0=xbuf[:, :n], scalar=g_sb[:, gi, c:c + 1], in1=rstd[:, :n],
                    op0=ALU.mult, op1=ALU.mult), deps=[tl, tr, state["scr_free"]])
                xs_r.rel(xi, th)
            return th

        def branch_epi(n, stores):
            def epi(fc, M, g, ps, mm):
                si, sb, sfree = st_r.get()
                t1 = P.op("act", lambda e: e.activation(out=sb[:, :n], in_=ps[:, :n], func=AF.Copy), deps=[mm, sfree])
                qi, qb, qfree = sq_r.get()
                t2 = P.op("act", lambda e: e.activation(out=qb[:, :n], in_=ps[:, :n], func=AF.Square), deps=[qfree])
                ts = P.dma("sp", br[fc * 128:(fc + 1) * 128, :n], sb[:, :n], st_r.dsems[si], deps=[t1])
                st_r.rel(si, ts)
                stores.append(ts)
                tmm = P.op("pe", lambda e: e.matmul(ps_ss[:, :n], lhsT=ones[:], rhs=qb[:, :n],
                                                    start=(fc == 0), stop=(fc == 31)), deps=[t2, state["ss_free"]])
                sq_r.rel(qi, tmm)
                stores.append(tmm)
                return t2
            return epi

        def postnorm(n, gi, xsrc, xdst, stores, final=None):
            ta, tr = rstd_bc(P, C, ps_ss[:, :n], n, D, rstd[:, :n], [stores[-1]])
            state["ss_free"] = ta
            touts = []
            for c in range(32):
                xi, xbuf, xfree = xs_r.get()
                tl = P.dma("sp", xbuf[:, :n], xsrc(c), xs_r.dsems[xi], deps=[xfree])
                bi, bb, bfree = bs_r.get()
                tl2 = P.dma("sp", bb[:, :n], br[c * 128:(c + 1) * 128, :n], bs_r.dsems[bi], deps=[bfree, stores])
                t1 = P.op("dve", lambda e, bb=bb, c=c: e.scalar_tensor_tensor(
                    out=bb[:, :n], in0=bb[:, :n], scalar=g_sb[:, gi, c:c + 1], in1=rstd[:, :n],
                    op0=ALU.mult, op1=ALU.mult), deps=[tl2, tr])
                t2 = P.op("dve", lambda e, bb=bb, xbuf=xbuf: e.tensor_tensor(
                    out=bb[:, :n], in0=bb[:, :n], in1=xbuf[:, :n], op=ALU.add), deps=[t1, tl])
                xs_r.rel(xi, t2)
                ts = P.dma("sp", xdst(c), bb[:, :n], bs_r.dsems[bi], deps=[t2])
                bs_r.rel(bi, ts)
                touts.append(ts)
            return touts[-3:]

        memn = hT[:, :, 0:MEM]
        th = prenorm(lambda c: memT[c * 128:(c + 1) * 128, :], 3, MEM, hT, None)

        def kx_epi(fc, M, g, ps, mm):
            return P.op("act", lambda e: e.activation(out=kx_sb[:, fc, :], in_=ps[:, :MEM], func=AF.Copy), deps=[mm])

        gemm(P, C, lambda fc, wb: [(wb[:, :, :], wsl(w_ck, 32, fc * 128, 128))], 32, [(i, 128) for i in range(8)],
             lambda kc, g: hT[:, kc, 0:MEM], lambda fc: ["m"], lambda g: MEM, kx_epi, wA_r, [th])
        wvx = hidb[:, 0:64, :].rearrange("p a b -> p (a b)")[:, 0:32768].rearrange("p (k m) -> p k m", m=1024)
        twv = P.dma("sp", wvx, w_cv.rearrange("(kc p) m -> p kc m", p=128), dmisc)
        last = None
        for mt in range(2):
            for cg in range(2):
                pi, ps, pfree = C.psg.get()
                for kc in range(32):
                    last = P.op("pe", lambda e, ps=ps, kc=kc, mt=mt, cg=cg: e.matmul(
                        ps[:, :], lhsT=hT[:, kc, mt * 128:(mt + 1) * 128], rhs=wvx[:, kc, cg * 512:(cg + 1) * 512],
                        start=(kc == 0), stop=(kc == 31)), deps=[th, twv, pfree] if kc == 0 else [], sig=(kc == 31))
                tn = P.op("act", lambda e, ps=ps, mt=mt, cg=cg: e.activation(
                    out=vx_sb[:, mt, cg * 512:(cg + 1) * 512], in_=ps[:, :], func=AF.Copy), deps=[last])
                C.psg.rel(pi, tn)
        state["scr_free"] = last
        state["hid_free"] = last
        outs = []
        tiles = [(0, 16)] + [(16 + j * 512, 512) for j in range(NT)]
        def do_tile(c0, n):
            halo = (n == 16)
            dl = P.dsem()
            t_in = P.dma("sp", in1[:, 0:16, :n], aT[:, c0:c0 + n].rearrange("(c p) n -> p c n", p=128), dl,
                         deps=[state["hid_free"]])
            t_in = P.dma("sp", in1[:, 16:32, :n], yT[:, c0:c0 + n].rearrange("(c p) n -> p c n", p=128), dl,
                         deps=[state["hid_free"]])
            stores = []
            gemm(P, C, lambda fc, wb: [(wb[:, :, :], wsl(w_out, 32, fc * 128, 128))], 32,
                 [(i, 128) for i in range(32)], lambda kc, g: in1[:, kc, :n], lambda fc: ["m"], lambda g: n,
                 branch_epi(n, stores), wA_r, [t_in])
            t1 = postnorm(n, 0, lambda c: xT[c * 128:(c + 1) * 128, c0:c0 + n],
                          lambda c: xa[c * 128:(c + 1) * 128, :n], stores)
            th = prenorm(lambda c: xa[c * 128:(c + 1) * 128, :n], 1, n, hT, t1)

            def q_epi(fc, M, g, ps, mm):
                return P.op("act", lambda e: e.activation(out=qx[:, fc, :n], in_=ps[:, :n], func=AF.Copy), deps=[mm])

            lastq = gemm(P, C, lambda fc, wb: [(wb[:, :, :], wsl(w_cq, 32, fc * 128, 128))], 32,
                         [(i, 128) for i in range(8)], lambda kc, g: hT[:, kc, :n], lambda fc: ["m"], lambda g: n,
                         q_epi, wA_r, [th])
            tq_done = C.psg.free[(C.psg.i - 1) % 5]
            tox = None
            for h in range(XH):
                pts = []
                for mt in range(2):
                    pi, ps, pfree = C.psg.get()
                    for dcn in range(2):
                        tmm = P.op("pe", lambda e, ps=ps, h=h, dcn=dcn, mt=mt: e.matmul(
                            ps[:, :n], lhsT=kx_sb[:, 2 * h + dcn, mt * 128:(mt + 1) * 128], rhs=qx[:, 2 * h + dcn, :n],
                            start=(dcn == 0), stop=(dcn == 1)), deps=[tq_done, pfree], sig=(dcn == 1))
                    qi, ptb, pfree2 = pt_r.get()
                    tex = P.op("act", lambda e, ptb=ptb, ps=ps: e.activation(out=ptb[:, :n], in_=ps[:, :n], func=AF.Exp,
                                                                             scale=xscale), deps=[tmm, pfree2])
                    C.psg.rel(pi, tex)
                    pts.append((qi, ptb, tex))
                pl_i, psl, plfree = C.psg.get()
                for mt in range(2):
                    tl_ = P.op("pe", lambda e, psl=psl, mt=mt: e.matmul(psl[:, :n], lhsT=ones[:], rhs=pts[mt][1][:, :n],
                                                                       start=(mt == 0), stop=(mt == 1)),
                               deps=[pts[mt][2], plfree])
                for dv in range(2):
                    for mt in range(2):
                        tpv = P.op("pe", lambda e, dv=dv, mt=mt, h=h: e.matmul(
                            ps_o2[dv][:, :n], lhsT=vx_sb[:, mt, h * 256 + dv * 128:h * 256 + (dv + 1) * 128],
                            rhs=pts[mt][1][:, :n], start=(mt == 0), stop=(mt == 1)), deps=[state["o2_free"]])
                for (qi, ptb, tex) in pts:
                    pt_r.rel(qi, tpv)
                ti, tb_, tfree = t_r.get()
                trl = P.op("dve", lambda e, tb_=tb_, psl=psl: e.reciprocal(out=tb_[:, :n], in_=psl[:, :n]),
                           deps=[tl_, tfree])
                C.psg.rel(pl_i, trl)
                for dv in range(2):
                    tox = P.op("dve", lambda e, dv=dv, h=h, tb_=tb_: e.tensor_tensor(
                        out=ox[:, 2 * h + dv, :n], in0=ps_o2[dv][:, :n], in1=tb_[:, :n], op=ALU.mult),
                        deps=[tpv, trl])
                state["o2_free"] = tox
                t_r.rel(ti, tox)
            stores = []
            gemm(P, C, lambda fc, wb: [(wb[:, 0:8, :], wsl(w_co, 8, fc * 128, 128))], 8,
                 [(i, 128) for i in range(32)], lambda kc, g: ox[:, kc, :n], lambda fc: ["m"], lambda g: n,
                 branch_epi(n, stores), wA_r, [tox])
            t2 = postnorm(n, 2, lambda c: xa[c * 128:(c + 1) * 128, :n],
                          lambda c: xb[c * 128:(c + 1) * 128, :n], stores)
            th = prenorm(lambda c: xb[c * 128:(c + 1) * 128, :n], 4, n, hT, t2)
            hst = []
            lastff = None
            for fc in range(FCH):
                gps = {}
                for which, w in (("g", w_gate), ("u", w_up)):
                    if halo and which == "u":
                        continue
                    wi, wb, wfree = wA_r.get()
                    wtok = P.dma("sp", wb[:, :, :], wsl(w, 32, fc * 128, 128), wA_r.dsems[wi], deps=[wfree])
                    pi, ps, pfree = C.psg.get()
                    for kc in range(32):
                        lastff = P.op("pe", lambda e, ps=ps, wb=wb, kc=kc: e.matmul(
                            ps[:, :n], lhsT=wb[:, kc, :], rhs=hT[:, kc, :n], start=(kc == 0), stop=(kc == 31)),
                            deps=[wtok, th, pfree] if kc == 0 else [], sig=(kc == 31))
                    wA_r.rel(wi, lastff)
                    gps[which] = (pi, ps, lastff)
                gi_, gb, gfree = gb_r.get()
                pi, ps, mm = gps["g"]
                tg1 = P.op("act", lambda e, gb=gb, ps=ps: e.activation(out=gb[:, 2:2 + n], in_=ps[:, :n], func=AF.Copy),
                           deps=[mm, gfree])
                C.psg.rel(pi, tg1)
                if halo:
                    tsv = P.op("dve", lambda e, gb=gb, fc=fc: e.tensor_scalar(
                        out=ghalo[:, fc, :], in0=gb[:, n:n + 2], scalar1=hf_sb[:, 0:1], scalar2=None, op0=ALU.mult),
                        deps=[tg1, tconst])
                    gb_r.rel(gi_, tsv)
                    continue
                tg0 = P.op("pool", lambda e, gb=gb, fc=fc: e.tensor_copy(out=gb[:, 0:2], in_=ghalo[:, fc, :]),
                           deps=[gfree])
                tsv = P.op("pool", lambda e, gb=gb, fc=fc: e.tensor_copy(out=ghalo[:, fc, :], in_=gb[:, n:n + 2]),
                           deps=[tg0, tg1])
                ti, tb_, tfree = t_r.get()
                tc1 = P.op("act", lambda e, tb_=tb_, gb=gb, fc=fc: e.activation(
                    out=tb_[:, :n], in_=gb[:, 2:2 + n], func=AF.Identity, bias=cw_sb[:, fc, 3:4],
                    scale=cw_sb[:, fc, 2:3]), deps=[tg1, tfree, tconst])
                tc2 = P.op("dve", lambda e, tb_=tb_, gb=gb, fc=fc: e.scalar_tensor_tensor(
                    out=tb_[:, :n], in0=gb[:, 1:1 + n], scalar=cw_sb[:, fc, 1:2], in1=tb_[:, :n], op0=ALU.mult,
                    op1=ALU.add), deps=[tc1, tg0])
                tc3 = P.op("dve", lambda e, tb_=tb_, gb=gb, fc=fc: e.scalar_tensor_tensor(
                    out=tb_[:, :n], in0=gb[:, 0:n], scalar=cw_sb[:, fc, 0:1], in1=tb_[:, :n], op0=ALU.mult,
                    op1=ALU.add), deps=[tc2])
                gb_r.rel(gi_, [tc3, tsv])
                tsl = P.op("act", lambda e, tb_=tb_: e.activation(out=tb_[:, :n], in_=tb_[:, :n], func=AF.Silu),
                           deps=[tc3])
                pi2, ps2, mm2 = gps["u"]
                hi, hb, hfree = hb_r.get()
                thd = P.op("dve", lambda e, hb=hb, tb_=tb_, ps2=ps2: e.tensor_tensor(
                    out=hb[:, :n], in0=ps2[:, :n], in1=tb_[:, :n], op=ALU.mult), deps=[tsl, mm2, hfree])
                C.psg.rel(pi2, thd)
                t_r.rel(ti, thd)
                tsh = P.dma("sp", hidd[:, fc, :n], hb[:, :n], hb_r.dsems[hi], deps=[thd])
                hb_r.rel(hi, tsh)
                hst.append(tsh)
            if halo:
                state["scr_free"] = lastff
                state["hid_free"] = lastq
                return
            dh = P.dsem()
            for part in range(2):
                thl = P.dma("sp", hidb[:, part * 43:(part + 1) * 43, :n], hidd[:, part * 43:(part + 1) * 43, :n], dh,
                            deps=[hst[-3:], lastq])
            stores = []
            lastd = gemm(P, C, lambda fc, wb: [(wb[:, 0:43, :], wsl(w_down, 43, fc * 128, 128)),
                                               (wb[:, 43:86, :], wsl(w_down, 43, fc * 128, 128, k0=43 * 128))],
                         FCH, [(i, 128) for i in range(32)], lambda kc, g: hidb[:, kc, :n], lambda fc: ["m"],
                         lambda g: n, branch_epi(n, stores), wD_r, [thl, lastff])
            state["scr_free"] = lastd
            state["hid_free"] = lastd
            t3 = postnorm(n, 5, lambda c: xb[c * 128:(c + 1) * 128, :n],
                          lambda c, c0=c0: xo[c * 128:(c + 1) * 128, c0 - 16:c0 - 16 + n], stores)
            state["outs"] = t3
        for (c0_, n_) in tiles:
            do_tile(c0_, n_)
        P.emit(state["outs"])
    return nc


def build_cast(L):
    nc = bass.Bass("TRN2", target_bir_lowering=False)
    CW = 4096
    src = nc.dram_tensor("src", [128, L], F32, kind="ExternalInput").ap()
    dst = nc.dram_tensor("dst", [128, L], BF16, kind="ExternalOutput").ap()
    with ExitStack() as st:
        P = Prog(nc, st)
        st_r = Ring(P, [P.sbuf([128, CW], F32) for _ in range(3)], True)
        bf_r = Ring(P, [P.sbuf([128, CW], BF16) for _ in range(3)], True)
        toks = []
        for ci in range(L // CW):
            si, sb, sfree = st_r.get()
            tl = P.dma("sp", sb[:], src[:, ci * CW:(ci + 1) * CW], st_r.dsems[si], deps=[sfree])
            bi, bb, bfree = bf_r.get()
            if ci % 2 == 0:
                tc = P.op("act", lambda e, bb=bb, sb=sb: e.activation(out=bb[:], in_=sb[:], func=AF.Copy),
                          deps=[tl, bfree])
            else:
                tc = P.op("dve", lambda e, bb=bb, sb=sb: e.tensor_copy(out=bb[:], in_=sb[:]), deps=[tl, bfree])
            st_r.rel(si, tc)
            ts = P.dma("sp", dst[:, ci * CW:(ci + 1) * CW], bb[:], bf_r.dsems[bi], deps=[tc])
            bf_r.rel(bi, ts)
            toks.append(ts)
        P.emit(toks[-3:])
    return nc


W_NAMES = [("w_in", D, INCOLS), ("w_uq", QLORA, HEADS * 192), ("w_ukv", KVLORA, HEADS * 256),
           ("w_pool", 2048, 512), ("w_out", D, D), ("w_cq", D, 1024), ("w_ck", D, 1024), ("w_cv", D, 1024),
           ("w_co", 1024, D), ("w_gate", D, DFF), ("w_up", D, DFF), ("w_down", DFF, D)]


def _run(nc, ins):
    return run_bass_kernel_spmd(nc, ins, core_ids=list(range(NCORES))).results


def _tm(v):
    v = np.asarray(v, np.float32)
    return np.ascontiguousarray(v.reshape(-1, 128).T)


def kernel(**inp):
    inp = {k: np.asarray(v) for k, v in inp.items()}
    S = inp["x"].shape[1]
    T = S // NCORES
    L_layers = inp["w_in"].shape[0]
    tot = sum(R * Cc for _, R, Cc in W_NAMES) * L_layers
    blk = 1024 * 4096
    totp = ((tot + blk - 1) // blk) * blk
    flat = np.zeros(totp, np.float32)
    o = 0
    for l in range(L_layers):
        for name, R, Cc in W_NAMES:
            flat[o:o + R * Cc] = inp[name][l].reshape(-1)
            o += R * Cc
    Lc = totp // 1024
    pieces = flat.reshape(NCORES, 128, Lc)
    res = _run(build_cast(Lc), [{"src": pieces[c]} for c in range(NCORES)])
    del flat
    wbf = np.concatenate([res[c]["dst"].reshape(-1) for c in range(NCORES)])
    W = []
    o = 0
    for l in range(L_layers):
        d = {}
        for name, R, Cc in W_NAMES:
            d[name] = wbf[o:o + R * Cc].reshape(R, Cc)
            o += R * Cc
        W.append(d)
    inv = (1.0 / (10000.0 ** (np.arange(0, 64, 2, dtype=np.float32) / 64))).astype(np.float32)
    invf = np.concatenate([inv, inv])[:, None].astype(np.float32)
    sgn = np.concatenate([-np.ones(32), np.ones(32)])[:, None].astype(np.float32)
    pos = inp["positions"][0].astype(np.int32)
    XT = np.ascontiguousarray(inp["x"][0].T)
    memT = np.ascontiguousarray(inp["mem"][0].T)
    nc_a = build_a(T)
    nc_att = build_att(S)
    nc_b = build_b(T)
    for l in range(L_layers):
        xpad = np.concatenate([np.zeros((D, 16), np.float32), XT], axis=1)
        ins = []
        for c in range(NCORES):
            ins.append({
                "xT": np.ascontiguousarray(xpad[:, c * T:c * T + T + 16]),
                "w_in": W[l]["w_in"], "w_uq": W[l]["w_uq"], "w_ukv": W[l]["w_ukv"], "w_pool": W[l]["w_pool"],
                "pos": pos[None, c * T:(c + 1) * T], "tix": np.arange(c * T, (c + 1) * T, dtype=np.float32)[None, :],
                "gpre": _tm(inp["g_mix_pre"][l]), "gq": _tm(inp["g_q"][l]), "gkv": _tm(inp["g_kv"][l]),
                "spool": _tm(inp["s_pool"][l]), "invf": invf, "sgn": sgn})
        ra = _run(nc_a, ins)
        QN = np.concatenate([ra[c]["qn_o"] for c in range(NCORES)], axis=2)
        QR = np.concatenate([ra[c]["qr_o"] for c in range(NCORES)], axis=2)
        KN = np.concatenate([ra[c]["kn_o"] for c in range(NCORES)], axis=2)
        KR = np.concatenate([ra[c]["kr_o"] for c in range(NCORES)], axis=1)
        V = np.concatenate([ra[c]["v_o"] for c in range(NCORES)], axis=0)
        YT = np.concatenate([ra[c]["y_o"] for c in range(NCORES)], axis=1)
        del ra
        ins = []
        for c in range(NCORES):
            ins.append({"qn": np.ascontiguousarray(QN[2 * c:2 * c + 2]), "qr": np.ascontiguousarray(QR[2 * c:2 * c + 2]),
                        "kn": np.ascontiguousarray(KN[2 * c:2 * c + 2]), "kr": KR,
                        "v": np.ascontiguousarray(V[:, 256 * c:256 * c + 256])})
        rt = _run(nc_att, ins)
        AT = np.concatenate([rt[c]["aT"] for c in range(NCORES)], axis=0)
        del rt, QN, QR, KN, V
        apad = np.concatenate([np.zeros((2048, 16), NPBF), AT], axis=1)
        ypad = np.concatenate([np.zeros((2048, 16), NPBF), YT], axis=1)
        gains = np.ascontiguousarray(np.stack(
            [_tm(inp[k][l]) for k in ("g_mix_post", "g_x_pre", "g_x_post", "g_mem", "g_ffn_pre", "g_ffn_post")], axis=1))
        cwb = np.ascontiguousarray(np.stack([_tm(inp["conv_w"][l][0]), _tm(inp["conv_w"][l][1]),
                                             _tm(inp["conv_w"][l][2]), _tm(inp["conv_b"][l])], axis=2))
        ins = []
        for c in range(NCORES):
            d = {"xT": np.ascontiguousarray(xpad[:, c * T:c * T + T + 16]),
                 "aT": np.ascontiguousarray(apad[:, c * T:c * T + T + 16]),
                 "yT": np.ascontiguousarray(ypad[:, c * T:c * T + T + 16]),
                 "memT": memT, "gains": gains, "cwb": cwb,
                 "hflag": np.full((128, 1), 0.0 if c == 0 else 1.0, np.float32)}
            for k in ("w_out", "w_cq", "w_ck", "w_cv", "w_co", "w_gate", "w_up", "w_down"):
                d[k] = W[l][k]
            ins.append(d)
        rb = _run(nc_b, ins)
        XT = np.concatenate([rb[c]["xo"] for c in range(NCORES)], axis=1)
        del rb
    return np.ascontiguousarray(XT.T)[None].astype(np.float32)
```

```python
import numpy as np
from contextlib import ExitStack
import concourse.bass as bass
import concourse.mybir as mybir
from concourse.bass_utils import run_bass_kernel_spmd
import ml_dtypes

F32 = mybir.dt.float32
BF16 = mybir.dt.bfloat16
I32 = mybir.dt.int32
AF = mybir.ActivationFunctionType
ALU = mybir.AluOpType
NPBF = ml_dtypes.bfloat16
NCORES = 8

D = 4096
HEADS = 16
QLORA = 1024
KVLORA = 512
ROPE = 64
NOPE = 128
VD = 128
POOLCH = 2048
INCOLS = QLORA + KVLORA + ROPE + POOLCH
DFF = 11008
FCH = DFF // 128
MEM = 256
XH = 4
XHD = 256
EPS = 1e-6
PI = float(np.pi)
CHUNK_B = ("w_gate", "w_up", "w_out", "w_cq", "w_ck", "w_down")


class DSem:
    def __init__(self, sem):
        self.sem = sem
        self.count = 0


def _flat(deps):
    out = []
    for t in deps:
        if t is None:
            continue
        if isinstance(t, list):
            out.extend(_flat(t))
        else:
            out.append(t)
    return out


class Prog:
    ENGS = ("pe", "act", "dve", "pool", "sp")

    def __init__(self, nc, stack):
        self.nc = nc
        self.stack = stack
        self.ops = {e: [] for e in self.ENGS}
        self.sems = {e: stack.enter_context(nc.semaphore("sem_" + e)) for e in self.ENGS}
        self.cnt = {e: 0 for e in self.ENGS}
        self.waited = {e: {} for e in self.ENGS}
        self.nsem = 0
        self.nt = 0

    def sbuf(self, shape, dt, name=None):
        self.nt += 1
        return self.stack.enter_context(
            self.nc.sbuf_tensor(name or ("sb%d" % self.nt), list(shape), dt))

    def psum(self, shape, dt, name=None):
        self.nt += 1
        return self.stack.enter_context(
            self.nc.psum_tensor(name or ("ps%d" % self.nt), list(shape), dt))

    def dsem(self):
        self.nsem += 1
        return DSem(self.stack.enter_context(self.nc.semaphore("dsem%d" % self.nsem)))

    def _waits(self, eng, deps):
        w = self.waited[eng]
        best = {}
        for sem, val in _flat(deps):
            k = id(sem)
            if k not in best or best[k][1] < val:
                best[k] = (sem, val)
        waits = []
        for k, (sem, val) in best.items():
            if w.get(k, 0) < val:
                w[k] = val
                waits.append((sem, val))
        return waits

    def op(self, eng, fn, deps=(), sig=True):
        waits = self._waits(eng, deps)
        tok = None
        if sig:
            self.cnt[eng] += 1
            tok = (self.sems[eng], self.cnt[eng])
        self.ops[eng].append((waits, fn, 1 if sig else 0, None))
        return tok

    def dma(self, q, out, in_, dsem, deps=()):
        waits = self._waits(q, deps)
        dsem.count += 16
        self.ops[q].append((waits, ("dma", out, in_), 16, dsem.sem))
        return (dsem.sem, dsem.count)

    def emit(self, final_tokens):
        nc = self.nc
        self.op("sp", None, deps=final_tokens, sig=False)
        block = self.stack.enter_context(nc.Block())
        engmap = {"pe": block.tensor, "act": block.scalar, "dve": block.vector,
                  "pool": block.gpsimd, "sp": block.sync}
        for ename in self.ENGS:
            ops = self.ops[ename]
            mysem = self.sems[ename]

            def body(e, ops=ops, mysem=mysem):
                for waits, fn, inc, dsem in ops:
                    for sem, val in waits:
                        e.wait_ge(sem, val)
                    if fn is None:
                        continue
                    if isinstance(fn, tuple):
                        _, out, in_ = fn
                        e.dma_start(out=out, in_=in_).then_inc(dsem, 16)
                    else:
                        ins = fn(e)
                        if inc:
                            ins.then_inc(mysem, 1)
            engmap[ename](body)


class Ring:
    def __init__(self, P, bufs, with_dsem=False):
        self.bufs = bufs
        self.free = [None] * len(bufs)
        self.i = 0
        self.dsems = [P.dsem() for _ in bufs] if with_dsem else None

    def get(self):
        i = self.i
        self.i = (i + 1) % len(self.bufs)
        return i, self.bufs[i], self.free[i]

    def rel(self, i, tok):
        self.free[i] = tok


def build_att(S, HPC=2):
    nc = bass.Bass("TRN2", target_bir_lowering=False)
    NQT = S // 512
    NKT = S // 128
    scale = float((NOPE + ROPE) ** -0.5)
    qn = nc.dram_tensor("qn", [HPC, 128, S], BF16, kind="ExternalInput").ap()
    qr = nc.dram_tensor("qr", [HPC, 64, S], BF16, kind="ExternalInput").ap()
    kn = nc.dram_tensor("kn", [HPC, 128, S], BF16, kind="ExternalInput").ap()
    kr = nc.dram_tensor("kr", [64, S], BF16, kind="ExternalInput").ap()
    v = nc.dram_tensor("v", [S, HPC * 128], BF16, kind="ExternalInput").ap()
    aT = nc.dram_tensor("aT", [HPC * 128, S], BF16, kind="ExternalOutput").ap()
    with ExitStack() as st:
        P = Prog(nc, st)
        kn_sb = P.sbuf([128, S], BF16)
        kr_sb = P.sbuf([64, S], BF16)
        v_sb = P.sbuf([128, NKT, 128], BF16)
        ones = P.sbuf([128, 128], BF16)
        qn_r = Ring(P, [P.sbuf([128, 512], BF16) for _ in range(2)], True)
        qr_r = Ring(P, [P.sbuf([64, 512], BF16) for _ in range(2)], True)
        pt_r = Ring(P, [P.sbuf([128, 512], BF16) for _ in range(4)])
        ps_s = Ring(P, [P.psum([128, 512], F32) for _ in range(3)])
        ps_o = Ring(P, [P.psum([128, 512], F32) for _ in range(2)])
        ps_l = Ring(P, [P.psum([128, 512], F32) for _ in range(2)])
        rl_r = Ring(P, [P.sbuf([128, 512], F32) for _ in range(2)])
        o_r = Ring(P, [P.sbuf([128, 512], BF16) for _ in range(2)], True)
        dk = P.dsem()
        t_ones = P.op("pool", lambda e: e.memset(ones[:], 1.0))
        t_kr = P.dma("sp", kr_sb[:], kr[:, :], dk)
        kv_free = None
        outs = []
        for h in range(HPC):
            t1 = P.dma("sp", kn_sb[:], kn[h], dk, deps=[kv_free])
            for part in range(4):
                r0 = part * (S // 4)
                tkv = P.dma("sp", v_sb[:, part * (NKT // 4):(part + 1) * (NKT // 4), :],
                            v[r0:r0 + S // 4, h * 128:(h + 1) * 128].rearrange("(kt p) d -> p kt d", p=128),
                            dk, deps=[kv_free])
            last_pe = None
            def do_qt(qt, h, tkv):
                qi, qnb, qfree = qn_r.get()
                tq1 = P.dma("sp", qnb[:], qn[h, :, qt * 512:(qt + 1) * 512], qn_r.dsems[qi], deps=[qfree])
                qj, qrb, qfree2 = qr_r.get()
                tq2 = P.dma("sp", qrb[:], qr[h, :, qt * 512:(qt + 1) * 512], qr_r.dsems[qj], deps=[qfree2])
                oi, pso, ofree = ps_o.get()
                li, psl, lfree = ps_l.get()
                nk = 4 * (qt + 1)
                pend = []

                def issue_s(kt):
                    j = kt - 4 * qt
                    c0 = 128 * j if j > 0 else 0
                    si, pss, sfree = ps_s.get()
                    P.op("pe", lambda e, pss=pss, kt=kt, c0=c0: e.matmul(
                        pss[:, c0:], lhsT=kn_sb[:, kt * 128:(kt + 1) * 128], rhs=qnb[:, c0:],
                        start=True, stop=False), deps=[tkv, tq1, tq2, t_kr, sfree], sig=False)
                    tmm = P.op("pe", lambda e, pss=pss, kt=kt, c0=c0: e.matmul(
                        pss[:, c0:], lhsT=kr_sb[:, kt * 128:(kt + 1) * 128], rhs=qrb[:, c0:],
                        start=False, stop=True))
                    pend.append((kt, j, c0, si, pss, tmm))

                issue_s(0)
                if nk > 1:
                    issue_s(1)
                for kt_ in range(nk):
                    if kt_ + 2 < nk:
                        issue_s(kt_ + 2)
                    kt, j, c0, si, pss, tmm = pend.pop(0)
                    pi, ptb, pfree = pt_r.get()
                    tex = P.op("act", lambda e, ptb=ptb, pss=pss, c0=c0: e.activation(
                        out=ptb[:, c0:], in_=pss[:, c0:], func=AF.Exp, scale=scale), deps=[tmm, pfree])
                    ps_s.rel(si, tex)
                    tp = tex
                    if j >= 0:
                        tp = P.op("pool", lambda e, ptb=ptb, c0=c0: e.memset(ptb[64:128, c0:c0 + 64], 0.0),
                                  deps=[tex])
                    P.op("pe", lambda e, kt=kt, ptb=ptb, c0=c0: e.matmul(
                        pso[:, c0:], lhsT=v_sb[:, kt, :], rhs=ptb[:, c0:],
                        start=(kt == 0), stop=(kt == nk - 1)), deps=[tp, ofree, lfree, t_ones], sig=False)
                    tpv = P.op("pe", lambda e, kt=kt, ptb=ptb, c0=c0: e.matmul(
                        psl[:, c0:], lhsT=ones[:], rhs=ptb[:, c0:],
                        start=(kt == 0), stop=(kt == nk - 1)))
                    pt_r.rel(pi, tpv)
                    last_pe = tpv
                qn_r.rel(qi, last_pe)
                qr_r.rel(qj, last_pe)
                ri, rlb, rfree = rl_r.get()
                trl = P.op("dve", lambda e, rlb=rlb, psl=psl: e.reciprocal(out=rlb[:], in_=psl[:]),
                           deps=[last_pe, rfree])
                ps_l.rel(li, trl)
                ob_i, ob, obfree = o_r.get()
                tmul = P.op("dve", lambda e, ob=ob, pso=pso, rlb=rlb: e.tensor_tensor(
                    out=ob[:], in0=pso[:], in1=rlb[:], op=ALU.mult), deps=[trl, obfree])
                ps_o.rel(oi, tmul)
                rl_r.rel(ri, tmul)
                tst = P.dma("sp", aT[h * 128:(h + 1) * 128, qt * 512:(qt + 1) * 512], ob[:],
                            o_r.dsems[ob_i], deps=[tmul])
                o_r.rel(ob_i, tst)
                outs.append(tst)
                return last_pe

            for qt in range(NQT):
                last_pe = do_qt(qt, h, tkv)
            kv_free = last_pe
        P.emit(outs[-2:])
    return nc


def wlayout(spec):
    offs = {}
    o = 0
    for name, R, C in spec:
        offs[name] = (o, R, C)
        o += R * C
    blk = 1024 * 2048
    tot = ((o + blk - 1) // blk) * blk
    return offs, tot // 1024


def host_wpieces(spec, arrays, Lq):
    flat = np.zeros(1024 * Lq, np.float32)
    o = 0
    for name, R, C in spec:
        flat[o:o + R * C] = np.asarray(arrays[name], np.float32).reshape(-1)
        o += R * C
    return flat.reshape(NCORES, 128, Lq)


def wprep(P, nc, wpiece, Lq):
    CW = 512
    piece_bf = nc.dram_tensor("wpiece_bf", [128, Lq], BF16)
    wflat = nc.dram_tensor("wflat", [NCORES * 128, Lq], BF16)
    st_r = Ring(P, [P.sbuf([128, CW], F32) for _ in range(2)], True)
    bf_r = Ring(P, [P.sbuf([128, CW], BF16) for _ in range(2)], True)
    engs = ["act", "dve"]
    toks = []
    for ci in range(Lq // CW):
        si, sb, sfree = st_r.get()
        tl = P.dma("sp", sb[:], wpiece[:, ci * CW:(ci + 1) * CW], st_r.dsems[si], deps=[sfree])
        bi, bb, bfree = bf_r.get()
        eng = engs[ci % 2]
        if eng == "act":
            tc = P.op("act", lambda e, bb=bb, sb=sb: e.activation(out=bb[:], in_=sb[:], func=AF.Copy),
                      deps=[tl, bfree])
        else:
            tc = P.op(eng, lambda e, bb=bb, sb=sb: e.tensor_copy(out=bb[:], in_=sb[:]), deps=[tl, bfree])
        st_r.rel(si, tc)
        ts = P.dma("sp", piece_bf.ap()[:, ci * CW:(ci + 1) * CW], bb[:], bf_r.dsems[bi], deps=[tc])
        bf_r.rel(bi, ts)
        toks.append(ts)
    tg = P.op("pool", lambda e: e.collective_compute(
        "AllGather", ALU.bypass, replica_groups=[list(range(NCORES))],
        ins=[piece_bf.ap().opt()], outs=[wflat.ap().opt()]), deps=toks[-2:])
    return wflat, tg


def wchunk_ap(wflat, off, C, KC, f0, M, k0=0):
    return bass.AP(wflat, off + k0 * C + f0, [[C, 128], [128 * C, KC], [1, M]])


class Ctx:
    pass


def gemm(P, C, wloads, KC, fchunks, rhs_fn, groups, gwidth, epi, wring, rhs_tok, krows=128):
    for fc, M in fchunks:
        wi, wb, wfree = wring.get()
        wtok = None
        for (o, i) in wloads(fc, wb):
            wtok = P.dma("sp", o, i, wring.dsems[wi], deps=[wfree])
        last = None
        for g in groups(fc):
            pi, ps, pfree = C.psg.get()
            n = gwidth(g)
            for kc in range(KC):
                last = P.op("pe", lambda e, ps=ps, wb=wb, kc=kc, M=M, n=n, g=g: e.matmul(
                    ps[:M, :n], lhsT=wb[:krows, kc, :M], rhs=rhs_fn(kc, g),
                    start=(kc == 0), stop=(kc == KC - 1)),
                    deps=[wtok, rhs_tok, pfree] if kc == 0 else [], sig=(kc == KC - 1))
            tok = epi(fc, M, g, ps, last)
            C.psg.rel(pi, tok)
        wring.rel(wi, last)
    return last


def rstd_bc(P, C, ps_ap, n, Dn, out_ap, deps):
    t = P.op("act", lambda e: e.activation(out=out_ap, in_=ps_ap, func=AF.Sqrt,
                                           bias=C.eps[:, 0:1], scale=1.0 / Dn), deps=deps)
    t2 = P.op("dve", lambda e: e.reciprocal(out=out_ap, in_=out_ap), deps=[t])
    return t, t2


A_SPEC = [("w_in", D, INCOLS), ("w_uq", QLORA, HEADS * 192), ("w_ukv", KVLORA, HEADS * 256),
          ("w_pool", 4 * 512, 512)]


def build_a(T):
    nc = bass.Bass("TRN2", target_bir_lowering=False)
    NT = T // 512
    offs, Lq = wlayout(A_SPEC)
    xT_t = nc.dram_tensor("xT", [D, 16 + T], F32, kind="ExternalInput")
    xT = xT_t.ap()
    w_in_c = nc.dram_tensor("w_in", [30, 128, 4096], BF16, kind="ExternalInput").ap()
    w_uq_c = nc.dram_tensor("w_uq", [HEADS, 128, 8 * 256], BF16, kind="ExternalInput").ap()
    w_uk_c = nc.dram_tensor("w_uk", [HEADS, 128, 4 * 128], BF16, kind="ExternalInput").ap()
    w_uv = nc.dram_tensor("w_uv", [KVLORA, HEADS * 128], BF16, kind="ExternalInput").ap()
    w_pool_c = nc.dram_tensor("w_pool", [16, 128, 4 * 128], BF16, kind="ExternalInput").ap()
    pos_t = nc.dram_tensor("pos", [1, T], I32, kind="ExternalInput")
    tix_t = nc.dram_tensor("tix", [1, T], F32, kind="ExternalInput")
    gpre = nc.dram_tensor("gpre", [128, 32], F32, kind="ExternalInput").ap()
    gq = nc.dram_tensor("gq", [128, 8], F32, kind="ExternalInput").ap()
    gkv = nc.dram_tensor("gkv", [128, 4], F32, kind="ExternalInput").ap()
    spool = nc.dram_tensor("spool", [128, 16], F32, kind="ExternalInput").ap()
    invf = nc.dram_tensor("invf", [64, 1], F32, kind="ExternalInput").ap()
    sgn = nc.dram_tensor("sgn", [64, 1], F32, kind="ExternalInput").ap()
    qn_o = nc.dram_tensor("qn_o", [HEADS, 128, T], BF16, kind="ExternalOutput").ap()
    qr_o = nc.dram_tensor("qr_o", [HEADS, 64, T], BF16, kind="ExternalOutput").ap()
    kn_o = nc.dram_tensor("kn_o", [HEADS, 128, T], BF16, kind="ExternalOutput").ap()
    kr_o = nc.dram_tensor("kr_o", [64, T], BF16, kind="ExternalOutput").ap()
    v_o = nc.dram_tensor("v_o", [T, HEADS * 128], BF16, kind="ExternalOutput").ap()
    y_o = nc.dram_tensor("y_o", [POOLCH, T], BF16, kind="ExternalOutput").ap()
    with ExitStack() as st:
        P = Prog(nc, st)
        C = Ctx()
        tg = None
        C.eps = P.sbuf([128, 1], F32)
        negpi = P.sbuf([64, 1], F32)
        ones = P.sbuf([128, 128], BF16)
        gpre_sb = P.sbuf([128, 32], F32)
        gq_sb = P.sbuf([128, 8], F32)
        gkv_sb = P.sbuf([128, 4], F32)
        sp_sb = P.sbuf([128, 16], F32)
        invf_sb = P.sbuf([64, 1], F32)
        sgn_sb = P.sbuf([64, 1], F32)
        dc = P.dsem()
        tcs = [P.op("pool", lambda e: e.memset(C.eps[:], EPS)),
               P.op("pool", lambda e: e.memset(negpi[:], -PI)),
               P.op("pool", lambda e: e.memset(ones[:], 1.0))]
        for sb, src in ((gpre_sb, gpre), (gq_sb, gq), (gkv_sb, gkv), (sp_sb, spool), (invf_sb, invf), (sgn_sb, sgn)):
            tcd = P.dma("sp", sb[:], src[:, :], dc)
        tconst = tcs + [tcd]
        hT = P.sbuf([128, 32, 528], BF16)
        pT = P.sbuf([128, 16, 512], BF16)
        cq32 = P.sbuf([128, 8, 512], F32)
        ckv32 = P.sbuf([128, 4, 512], F32)
        cqn = P.sbuf([128, 8, 512], BF16)
        ckvn = P.sbuf([128, 4, 512], BF16)
        wv_sb = P.sbuf([128, 4, HEADS * 128], BF16)
        rstd = P.sbuf([128, 528], F32)
        rstd2 = P.sbuf([128, 512], F32)
        cos2 = P.sbuf([64, 512], F32)
        sinS = P.sbuf([64, 512], F32)
        posi = P.sbuf([64, 512], I32)
        ang = P.sbuf([64, 512], F32)
        angm = P.sbuf([64, 512], F32)
        ang2 = P.sbuf([64, 512], F32)
        tixb = P.sbuf([128, 512], F32)
        icnt = P.sbuf([128, 4, 512], F32)
        kr32 = P.sbuf([64, 512], F32)
        krs32 = P.sbuf([64, 512], F32)
        xs_r = Ring(P, [P.sbuf([128, 528], F32) for _ in range(3)], True)
        sq_r = Ring(P, [P.sbuf([128, 528], BF16) for _ in range(9)])
        u_r = Ring(P, [P.sbuf([128, 528], F32) for _ in range(2)])
        s_r = Ring(P, [P.sbuf([128, 528], F32) for _ in range(3)])
        t_r = Ring(P, [P.sbuf([128, 512], F32) for _ in range(4)])
        ob_r = Ring(P, [P.sbuf([128, 512], BF16) for _ in range(4)], True)
        win_r = Ring(P, [P.sbuf([128, 32, 128], BF16) for _ in range(2)], True)
        wq_r = Ring(P, [P.sbuf([128, 8, 256], BF16) for _ in range(2)], True)
        wk_r = Ring(P, [P.sbuf([128, 4, 128], BF16) for _ in range(2)], True)
        C.psg = Ring(P, [P.psum([128, 512], F32) for _ in range(5)])
        ps_ssm = P.psum([128, 512], F32)
        ps_ssh = P.psum([128, 512], F32)
        ps_ss2 = P.psum([128, 512], F32)
        ss_free = None
        ss2_free = None
        dwv = P.dsem()
        twv = P.dma("sp", wv_sb[:, :, :], w_uv.rearrange("(kc p) m -> p kc m", p=128), dwv)
        outs = []
        hT_free = None
        cq_free = None
        pT_free = None
        pe_prev = None
        for j in range(NT):
            c0 = j * 512
            dtab = P.dsem()
            tp1 = P.dma("sp", posi[:], bass.AP(pos_t, c0, [[0, 64], [1, 512]]), dtab, deps=[cq_free])
            tp2 = P.dma("sp", tixb[:], bass.AP(tix_t, c0, [[0, 128], [1, 512]]), dtab, deps=[cq_free])
            ta = P.op("dve", lambda e: e.tensor_copy(out=ang[:], in_=posi[:]), deps=[tp2, tconst])
            ta = P.op("dve", lambda e: e.tensor_scalar(out=ang[:], in0=ang[:], scalar1=invf_sb[:, 0:1], scalar2=None,
                                                       op0=ALU.mult), deps=[ta])
            def sin_of(dst, shift, dep):
                t0 = P.op("dve", lambda e: e.tensor_scalar(out=angm[:], in0=ang[:], scalar1=shift, scalar2=1.0 / (2 * PI),
                                                           op0=ALU.add, op1=ALU.mult), deps=[dep])
                t0 = P.op("dve", lambda e: e.tensor_copy(out=posi[:], in_=angm[:]), deps=[t0])
                t0 = P.op("dve", lambda e: e.tensor_copy(out=angm[:], in_=posi[:]), deps=[t0])
                t0 = P.op("dve", lambda e: e.tensor_scalar(out=angm[:], in0=angm[:], scalar1=-2 * PI, scalar2=shift,
                                                           op0=ALU.mult, op1=ALU.add), deps=[t0])
                t0 = P.op("dve", lambda e: e.tensor_tensor(out=angm[:], in0=angm[:], in1=ang[:], op=ALU.add), deps=[t0])
                t1 = P.op("dve", lambda e: e.tensor_scalar(out=ang2[:], in0=angm[:], scalar1=PI, scalar2=-2 * PI,
                                                           op0=ALU.is_gt, op1=ALU.mult), deps=[t0])
                t1 = P.op("dve", lambda e: e.tensor_tensor(out=angm[:], in0=angm[:], in1=ang2[:], op=ALU.add), deps=[t1])
                t1 = P.op("dve", lambda e: e.tensor_scalar(out=ang2[:], in0=angm[:], scalar1=-PI, scalar2=2 * PI,
                                                           op0=ALU.is_lt, op1=ALU.mult), deps=[t1])
                t1 = P.op("dve", lambda e: e.tensor_tensor(out=angm[:], in0=angm[:], in1=ang2[:], op=ALU.add), deps=[t1])
                return P.op("act", lambda e: e.activation(out=dst[:], in_=angm[:], func=AF.Sin), deps=[t1])

            tsin = sin_of(sinS, 0.0, ta)
            tcos = sin_of(cos2, 0.5 * PI, tsin)
            tsin = P.op("dve", lambda e: e.tensor_scalar(out=sinS[:], in0=sinS[:], scalar1=sgn_sb[:, 0:1],
                                                         scalar2=None, op0=ALU.mult), deps=[tsin])
            trope = [tsin, tcos]
            tic = None
            for g, w in enumerate((2, 4, 8, 16)):
                tic = P.op("dve", lambda e, g=g, w=w: e.tensor_scalar(
                    out=icnt[:, g, :], in0=tixb[:], scalar1=1.0, scalar2=float(w), op0=ALU.add, op1=ALU.min),
                    deps=[tp2, tic])
                tic = P.op("dve", lambda e, g=g: e.reciprocal(out=icnt[:, g, :], in_=icnt[:, g, :]), deps=[tic])
            sqt = None
            for c in range(32):
                xi, xb, xfree = xs_r.get()
                tl = P.dma("sp", xb[:], xT[c * 128:(c + 1) * 128, c0:c0 + 528], xs_r.dsems[xi], deps=[xfree])
                qi, qb, qfree = sq_r.get()
                tsq = P.op("act", lambda e, qb=qb, xb=xb: e.activation(out=qb[:], in_=xb[:], func=AF.Square),
                           deps=[tl, qfree])
                xs_r.rel(xi, tsq)
                P.op("pe", lambda e, qb=qb, c=c: e.matmul(ps_ssm[:, :], lhsT=ones[:], rhs=qb[:, 16:528],
                                                        start=(c == 0), stop=(c == 31)),
                     deps=[tsq, ss_free, tconst], sig=False)
                tmm = P.op("pe", lambda e, qb=qb, c=c: e.matmul(ps_ssh[:, :16], lhsT=ones[:], rhs=qb[:, 0:16],
                                                              start=(c == 0), stop=(c == 31)))
                sq_r.rel(qi, tmm)
            ta1, tr1 = rstd_bc(P, C, ps_ssm[:, :], 512, D, rstd[:, 16:528], [tmm, hT_free])
            ta2, tr2 = rstd_bc(P, C, ps_ssh[:, :16], 16, D, rstd[:, 0:16], [tmm])
            ss_free = [ta1, ta2]
            th = None
            for c in range(32):
                xi, xb, xfree = xs_r.get()
                tl = P.dma("sp", xb[:], xT[c * 128:(c + 1) * 128, c0:c0 + 528], xs_r.dsems[xi], deps=[xfree])
                th = P.op("dve", lambda e, xb=xb, c=c: e.scalar_tensor_tensor(
                    out=hT[:, c, :], in0=xb[:], scalar=gpre_sb[:, c:c + 1], in1=rstd[:], op0=ALU.mult, op1=ALU.mult),
                    deps=[tl, tr1, tr2, hT_free])
                xs_r.rel(xi, th)
                if c == 30:
                    th30 = th
            th_all = [th, th30]
            pend = []

            def win_loads(fc, wb):
                kind, idx = fc
                ci = {"cq": idx, "ckv": 8 + idx, "kr": 12, "krs": 13, "pool": 14 + idx}[kind]
                return [(wb.rearrange("p k m -> p (k m)"), w_in_c[ci])]

            ust = {}

            def win_epi(fc, M, g, ps, mm):
                kind, idx = fc
                if kind in ("cq", "ckv"):
                    dst = cq32 if kind == "cq" else ckv32
                    t1 = P.op("act", lambda e: e.activation(out=dst[:, idx, :], in_=ps[:, :], func=AF.Copy),
                              deps=[mm, cq_free])
                    qi, qb, qfree = sq_r.get()
                    t2 = P.op("act", lambda e: e.activation(out=qb[:, 0:512], in_=ps[:, :], func=AF.Square),
                              deps=[qfree])
                    pend.append((qi, qb, t2))
                    return t2
                if kind in ("kr", "krs"):
                    dst = kr32 if kind == "kr" else krs32
                    return P.op("act", lambda e: e.activation(out=dst[:, :], in_=ps[:64, :], func=AF.Copy),
                                deps=[mm, cq_free])
                if g == "halo":
                    ui, ub, ufree = u_r.get()
                    ust["u"] = (ui, ub)
                    t = P.op("act", lambda e: e.activation(out=ub[:, 0:16], in_=ps[:, 0:16], func=AF.Copy),
                             deps=[mm, ufree])
                    ust["t"] = t
                    return t
                ui, ub = ust["u"]
                tu = P.op("act", lambda e: e.activation(out=ub[:, 16:528], in_=ps[:, :], func=AF.Copy), deps=[mm])
                grp = idx // 4
                cur = ub
                tcur = [tu, ust["t"]]
                sh = 1
                rel = []
                for step in range(grp + 1):
                    si, sb, sfree = s_r.get()
                    eng = "dve" if (idx + step) % 2 == 0 else "pool"
                    tn = P.op(eng, lambda e, sb=sb, cur=cur, sh=sh: e.tensor_tensor(
                        out=sb[:, sh:528], in0=cur[:, sh:528], in1=cur[:, 0:528 - sh], op=ALU.add),
                        deps=[tcur, sfree])
                    if step > 0:
                        s_r.rel(psi, tn)
                    psi = si
                    cur = sb
                    tcur = [tn]
                    sh *= 2
                ti, tb_, tfree = t_r.get()
                tm = P.op("dve", lambda e, tb_=tb_, cur=cur, grp=grp: e.tensor_tensor(
                    out=tb_[:], in0=cur[:, 16:528], in1=icnt[:, grp, :], op=ALU.mult), deps=[tcur, tfree, tic])
                s_r.rel(psi, tm)
                tp = P.op("pool", lambda e, tb_=tb_, ub=ub: e.tensor_tensor(
                    out=pT[:, idx, :], in0=tb_[:], in1=ub[:, 16:528], op=ALU.subtract), deps=[tm, pT_free])
                t_r.rel(ti, tp)
                u_r.rel(ui, tp)
                ust["last"] = tp
                return tu

            fch = [(("cq", i), 128) for i in range(8)]
            gemm(P, C, win_loads, 32, fch, lambda kc, g: hT[:, kc, 16:528], lambda fc: ["main"], lambda g: 512,
                 win_epi, win_r, [th_all, tg])
            for n_, (qi, qb, t2) in enumerate(pend):
                tmm = P.op("pe", lambda e, qb=qb, n_=n_: e.matmul(ps_ss2[:, :], lhsT=ones[:], rhs=qb[:, 0:512],
                                                                start=(n_ == 0), stop=(n_ == 7)),
                           deps=[t2, ss2_free])
                sq_r.rel(qi, tmm)
            pend.clear()
            ta_, trq = rstd_bc(P, C, ps_ss2[:, :], 512, QLORA, rstd2[:, :], [tmm])
            ss2_free = ta_
            tq = None
            for i in range(8):
                tq = P.op("dve", lambda e, i=i: e.scalar_tensor_tensor(
                    out=cqn[:, i, :], in0=cq32[:, i, :], scalar=gq_sb[:, i:i + 1], in1=rstd2[:], op0=ALU.mult,
                    op1=ALU.mult), deps=[trq, pe_prev])
            fch = [(("ckv", i), 128) for i in range(4)] + [(("kr", 0), 64), (("krs", 0), 64)]
            gemm(P, C, win_loads, 32, fch, lambda kc, g: hT[:, kc, 16:528], lambda fc: ["main"], lambda g: 512,
                 win_epi, win_r, [th_all, tg])
            for n_, (qi, qb, t2) in enumerate(pend):
                tmm = P.op("pe", lambda e, qb=qb, n_=n_: e.matmul(ps_ss2[:, :], lhsT=ones[:], rhs=qb[:, 0:512],
                                                                start=(n_ == 0), stop=(n_ == 3)),
                           deps=[t2, ss2_free, tq])
                sq_r.rel(qi, tmm)
            pend.clear()
            fch = [(("pool", i), 128) for i in range(16)]
            hT_free = gemm(P, C, win_loads, 32, fch,
                           lambda kc, g: hT[:, kc, 16:528] if g == "main" else hT[:, kc, 0:16],
                           lambda fc: ["halo", "main"], lambda g: 512 if g == "main" else 16,
                           win_epi, win_r, [th_all, tg])
            ti, tb_, tfree = t_r.get()
            t1 = P.op("dve", lambda e, tb_=tb_: e.tensor_tensor(out=tb_[:64, :], in0=kr32[:], in1=cos2[:], op=ALU.mult),
                      deps=[trope, tfree, C.psg.free])
            ti2, tb2_, tfree2 = t_r.get()
            t2 = P.op("dve", lambda e, tb2_=tb2_: e.tensor_tensor(out=tb2_[:64, :], in0=krs32[:], in1=sinS[:],
                                                                 op=ALU.mult), deps=[trope, tfree2])
            oi, ob, ofree = ob_r.get()
            t3 = P.op("dve", lambda e, ob=ob, tb_=tb_, tb2_=tb2_: e.tensor_tensor(
                out=ob[:64, :], in0=tb_[:64, :], in1=tb2_[:64, :], op=ALU.add), deps=[t1, t2, ofree])
            t_r.rel(ti, t3)
            t_r.rel(ti2, t3)
            ts = P.dma("sp", kr_o[:, c0:c0 + 512], ob[:64, :], ob_r.dsems[oi], deps=[t3])
            ob_r.rel(oi, ts)
            outs.append(ts)
            qst = {}

            def wq_loads(fc, wb):
                return [(wb.rearrange("p k m -> p (k m)"), w_uq_c[fc])]

            for h in range(HEADS):
                wi, wb, wfree = wq_r.get()
                for (o, i_) in wq_loads(h, wb):
                    wtok = P.dma("sp", o, i_, wq_r.dsems[wi], deps=[wfree, tg])
                res = []
                for (m0, M) in ((0, 128), (128, 64), (192, 64)):
                    pi, ps, pfree = C.psg.get()
                    for kc in range(8):
                        last = P.op("pe", lambda e, ps=ps, wb=wb, kc=kc, m0=m0, M=M: e.matmul(
                            ps[:M, :], lhsT=wb[:, kc, m0:m0 + M], rhs=cqn[:, kc, :], start=(kc == 0), stop=(kc == 7)),
                            deps=[wtok, tq, pfree] if kc == 0 else [], sig=(kc == 7))
                    res.append((pi, ps, last))
                wq_r.rel(wi, last)
                (p0, ps0, l0), (p1, ps1, l1), (p2, ps2, l2) = res
                oi, ob, ofree = ob_r.get()
                tn = P.op("act", lambda e, ob=ob, ps0=ps0: e.activation(out=ob[:], in_=ps0[:, :], func=AF.Copy),
                          deps=[l0, ofree])
                C.psg.rel(p0, tn)
                ts = P.dma("sp", qn_o[h, :, c0:c0 + 512], ob[:], ob_r.dsems[oi], deps=[tn])
                ob_r.rel(oi, ts)
                ti, tb_, tfree = t_r.get()
                t1 = P.op("dve", lambda e, tb_=tb_, ps1=ps1: e.tensor_tensor(out=tb_[:64, :], in0=ps1[:64, :],
                                                                           in1=cos2[:], op=ALU.mult),
                          deps=[l1, trope, tfree])
                C.psg.rel(p1, t1)
                ti2, tb2_, tfree2 = t_r.get()
                t2 = P.op("dve", lambda e, tb2_=tb2_, ps2=ps2: e.tensor_tensor(out=tb2_[:64, :], in0=ps2[:64, :],
                                                                             in1=sinS[:], op=ALU.mult),
                          deps=[l2, tfree2])
                C.psg.rel(p2, t2)
                oi, ob, ofree = ob_r.get()
                t3 = P.op("pool", lambda e, ob=ob, tb_=tb_, tb2_=tb2_: e.tensor_tensor(
                    out=ob[:64, :], in0=tb_[:64, :], in1=tb2_[:64, :], op=ALU.add), deps=[t1, t2, ofree])
                t_r.rel(ti, t3)
                t_r.rel(ti2, t3)
                ts = P.dma("sp", qr_o[h, :, c0:c0 + 512], ob[:64, :], ob_r.dsems[oi], deps=[t3])
                ob_r.rel(oi, ts)
                outs.append(ts)
            ta_, trk = rstd_bc(P, C, ps_ss2[:, :], 512, KVLORA, rstd2[:, :], [tmm, tq])
            ss2_free = ta_
            tkv = None
            for i in range(4):
                tkv = P.op("dve", lambda e, i=i: e.scalar_tensor_tensor(
                    out=ckvn[:, i, :], in0=ckv32[:, i, :], scalar=gkv_sb[:, i:i + 1], in1=rstd2[:], op0=ALU.mult,
                    op1=ALU.mult), deps=[trk, pe_prev])
            cq_free = tkv
            def wk_loads(fc, wb):
                return [(wb.rearrange("p k m -> p (k m)"), w_uk_c[fc])]

            def k_epi(fc, M, g, ps, mm):
                oi, ob, ofree = ob_r.get()
                tn = P.op("act", lambda e: e.activation(out=ob[:], in_=ps[:, :], func=AF.Copy), deps=[mm, ofree])
                ts = P.dma("sp", kn_o[fc, :, c0:c0 + 512], ob[:], ob_r.dsems[oi], deps=[tn])
                ob_r.rel(oi, ts)
                outs.append(ts)
                return tn

            gemm(P, C, wk_loads, 4, [(h, 128) for h in range(HEADS)], lambda kc, g: ckvn[:, kc, :],
                 lambda fc: ["main"], lambda g: 512, k_epi, wk_r, [tkv, tg])
            for s in range(4):
                for cg in range(4):
                    pi, ps, pfree = C.psg.get()
                    for kc in range(4):
                        last = P.op("pe", lambda e, ps=ps, kc=kc, s=s, cg=cg: e.matmul(
                            ps[:, :], lhsT=ckvn[:, kc, s * 128:(s + 1) * 128], rhs=wv_sb[:, kc, cg * 512:(cg + 1) * 512],
                            start=(kc == 0), stop=(kc == 3)), deps=[tkv, twv, pfree] if kc == 0 else [],
                            sig=(kc == 3))
                    oi, ob, ofree = ob_r.get()
                    tn = P.op("act", lambda e, ob=ob, ps=ps: e.activation(out=ob[:], in_=ps[:, :], func=AF.Copy),
                              deps=[last, ofree])
                    C.psg.rel(pi, tn)
                    ts = P.dma("sp", v_o[c0 + s * 128:c0 + (s + 1) * 128, cg * 512:(cg + 1) * 512], ob[:],
                               ob_r.dsems[oi], deps=[tn])
                    ob_r.rel(oi, ts)
                    outs.append(ts)
            def wp_loads(fc, wb):
                g_, fi = fc
                return [(wb.rearrange("p k m -> p (k m)"), w_pool_c[g_ * 4 + fi])]

            def p_epi(fc, M, g, ps, mm):
                g_, fi = fc
                ch = g_ * 4 + fi
                oi, ob, ofree = ob_r.get()
                tn = P.op("act", lambda e: e.activation(out=ob[:], in_=ps[:, :], func=AF.Copy,
                                                        scale=sp_sb[:, ch:ch + 1]), deps=[mm, ofree])
                ts = P.dma("sp", y_o[ch * 128:(ch + 1) * 128, c0:c0 + 512], ob[:], ob_r.dsems[oi], deps=[tn])
                ob_r.rel(oi, ts)
                outs.append(ts)
                return tn

            for g_ in range(4):
                pe_last = gemm(P, C, wp_loads, 4, [((g_, fi), 128) for fi in range(4)],
                               lambda kc, g, g_=g_: pT[:, g_ * 4 + kc, :], lambda fc: ["main"], lambda g: 512,
                               p_epi, wk_r, [ust["last"], tg])
            pT_free = pe_last
            pe_prev = pe_last
        P.emit(outs[-8:])
    return nc


def wsl(w, KC, f0, M, k0=0):
    return w[k0:k0 + KC * 128, f0:f0 + M].rearrange("(kc p) m -> p kc m", p=128)


def build_b(T):
    nc = bass.Bass("TRN2", target_bir_lowering=False)
    NT = T // 512
    TW = 16 + T
    xT = nc.dram_tensor("xT", [D, TW], F32, kind="ExternalInput").ap()
    aT = nc.dram_tensor("aT", [2048, TW], BF16, kind="ExternalInput").ap()
    yT = nc.dram_tensor("yT", [2048, TW], BF16, kind="ExternalInput").ap()
    memT = nc.dram_tensor("memT", [D, MEM], F32, kind="ExternalInput").ap()
    gains = nc.dram_tensor("gains", [128, 6, 32], F32, kind="ExternalInput").ap()
    cwb = nc.dram_tensor("cwb", [128, FCH, 4], F32, kind="ExternalInput").ap()
    hflag = nc.dram_tensor("hflag", [128, 1], F32, kind="ExternalInput").ap()
    def wdecl(name, K, F):
        if name in CHUNK_B:
            return nc.dram_tensor(name, [F // 128, 128, K], BF16, kind="ExternalInput").ap()
        return nc.dram_tensor(name, [K, F], BF16, kind="ExternalInput").ap()

    w_out = wdecl("w_out", D, D)
    w_cq = wdecl("w_cq", D, 1024)
    w_ck = wdecl("w_ck", D, 1024)
    w_cv = nc.dram_tensor("w_cv", [D, 1024], BF16, kind="ExternalInput").ap()
    w_co = wdecl("w_co", 1024, D)
    w_gate = wdecl("w_gate", D, DFF)
    w_up = wdecl("w_up", D, DFF)
    w_down = wdecl("w_down", DFF, D)
    wname = {id(w_out): "w_out", id(w_cq): "w_cq", id(w_ck): "w_ck", id(w_co): "w_co", id(w_gate): "w_gate",
             id(w_up): "w_up", id(w_down): "w_down"}
    xo = nc.dram_tensor("xo", [D, T], F32, kind="ExternalOutput").ap()
    br = nc.dram_tensor("br", [D, 512], F32).ap()
    xa = nc.dram_tensor("xa", [D, 512], F32).ap()
    xb = nc.dram_tensor("xb", [D, 512], F32).ap()
    hidd = nc.dram_tensor("hidd", [128, FCH, 512], BF16).ap()
    xscale = float(XHD ** -0.5)
    with ExitStack() as st:
        P = Prog(nc, st)
        C = Ctx()
        C.eps = P.sbuf([128, 1], F32)
        ones = P.sbuf([128, 128], BF16)
        g_sb = P.sbuf([128, 6, 32], F32)
        cw_sb = P.sbuf([128, FCH, 4], F32)
        hf_sb = P.sbuf([128, 1], F32)
        dc = P.dsem()
        P.dma("sp", hf_sb[:], hflag[:, :], dc)
        tconst = [P.op("pool", lambda e: e.memset(C.eps[:], EPS)),
                  P.op("pool", lambda e: e.memset(ones[:], 1.0)),
                  P.dma("sp", g_sb[:], gains[:, :, :], dc), P.dma("sp", cw_sb[:], cwb[:, :, :], dc)]
        tconst = [tconst[0], tconst[1], tconst[3]]
        hidb = P.sbuf([128, FCH, 512], BF16)
        scr = P.sbuf([128, 28672], BF16)
        hT = scr[:, 0:16384].rearrange("p (c n) -> p c n", n=512)
        wA = [scr[:, 16384 + i * 4096:16384 + (i + 1) * 4096].rearrange("p (k m) -> p k m", m=128) for i in range(3)]
        wD = [scr[:, i * 11008:(i + 1) * 11008].rearrange("p (k m) -> p k m", m=128) for i in range(2)]
        wA_r = Ring(P, wA, True)
        wD_r = Ring(P, wD, True)
        flat_of = {}
        for i in range(3):
            flat_of[id(wA[i])] = scr[:, 16384 + i * 4096:16384 + (i + 1) * 4096]
        for i in range(2):
            flat_of[id(wD[i])] = scr[:, i * 11008:(i + 1) * 11008]

        def FL(wb):
            return flat_of[id(wb)]

        def ld(w, fc, wb, KC):
            if wname[id(w)] in CHUNK_B:
                return [(FL(wb)[:, 0:KC * 128], w[fc])]
            return [(wb[:, 0:KC, :], wsl(w, KC, fc * 128, 128))]

        def ld_down(fc, wb):
            if "w_down" in CHUNK_B:
                return [(FL(wb)[:, q * 2752:(q + 1) * 2752], w_down[fc][:, q * 2752:(q + 1) * 2752]) for q in range(4)]
            return [(wb[:, 0:43, :], wsl(w_down, 43, fc * 128, 128)),
                    (wb[:, 43:86, :], wsl(w_down, 43, fc * 128, 128, k0=43 * 128))]
        in1 = hidb[:, 0:32, :]
        qx = hidb[:, 32:40, :]
        ox = hidb[:, 40:48, :]
        kx_sb = P.sbuf([128, 8, MEM], BF16)
        vx_sb = P.sbuf([128, 2, 1024], BF16)
        rstd = P.sbuf([128, 512], F32)
        ghalo = P.sbuf([128, FCH, 2], F32)
        xs_r = Ring(P, [P.sbuf([128, 512], F32) for _ in range(3)], True)
        bs_r = Ring(P, [P.sbuf([128, 512], F32) for _ in range(3)], True)
        sq_r = Ring(P, [P.sbuf([128, 512], BF16) for _ in range(4)])
        st_r = Ring(P, [P.sbuf([128, 512], F32) for _ in range(3)], True)
        gb_r = Ring(P, [P.sbuf([128, 514], F32) for _ in range(2)])
        t_r = Ring(P, [P.sbuf([128, 512], F32) for _ in range(3)])
        pt_r = Ring(P, [P.sbuf([128, 512], BF16) for _ in range(2)])
        hb_r = Ring(P, [P.sbuf([128, 512], BF16) for _ in range(3)], True)
        C.psg = Ring(P, [P.psum([128, 512], F32) for _ in range(5)])
        ps_ss = P.psum([128, 512], F32)
        ps_o2 = [P.psum([128, 512], F32) for _ in range(2)]
        state = {"ss_free": None, "scr_free": None, "hid_free": None, "o2_free": None}
        dmisc = P.dsem()

        def prenorm(src, gi, n, dst, extra):
            tmm = None
            for c in range(32):
                xi, xbuf, xfree = xs_r.get()
                tl = P.dma("sp", xbuf[:, :n], src(c), xs_r.dsems[xi], deps=[xfree, extra])
                qi, qb, qfree = sq_r.get()
                tsq = P.op("act", lambda e, qb=qb, xbuf=xbuf: e.activation(out=qb[:, :n], in_=xbuf[:, :n],
                                                                          func=AF.Square), deps=[tl, qfree])
                xs_r.rel(xi, tsq)
                tmm = P.op("pe", lambda e, qb=qb, c=c: e.matmul(ps_ss[:, :n], lhsT=ones[:], rhs=qb[:, :n],
                                                              start=(c == 0), stop=(c == 31)),
                           deps=[tsq, state["ss_free"], tconst])
                sq_r.rel(qi, tmm)
            ta, tr = rstd_bc(P, C, ps_ss[:, :n], n, D, rstd[:, :n], [tmm])
            state["ss_free"] = ta
            th = None
            for c in range(32):
                xi, xbuf, xfree = xs_r.get()
                tl = P.dma("sp", xbuf[:, :n], src(c), xs_r.dsems[xi], deps=[xfree])
                th = P.op("dve", lambda e, xbuf=xbuf, c=c: e.scalar_tensor_tensor(
                    out=dst[:, c, :n], in0=xbuf[:, :n], scalar=g_sb[:, gi, c:c + 1], in1=rstd[:, :n],
                    op0=ALU.mult, op1=ALU.mult), deps=[tl, tr, state["scr_free"]])
                xs_r.rel(xi, th)
            return th

        def branch_epi(n, stores):
            def epi(fc, M, g, ps, mm):
                si, sb, sfree = st_r.get()
                t1 = P.op("act", lambda e: e.activation(out=sb[:, :n], in_=ps[:, :n], func=AF.Copy), deps=[mm, sfree])
                qi, qb, qfree = sq_r.get()
                t2 = P.op("act", lambda e: e.activation(out=qb[:, :n], in_=ps[:, :n], func=AF.Square), deps=[qfree])
                ts = P.dma("sp", br[fc * 128:(fc + 1) * 128, :n], sb[:, :n], st_r.dsems[si], deps=[t1])
                st_r.rel(si, ts)
                stores.append(ts)
                tmm = P.op("pe", lambda e: e.matmul(ps_ss[:, :n], lhsT=ones[:], rhs=qb[:, :n],
                                                    start=(fc == 0), stop=(fc == 31)), deps=[t2, state["ss_free"]])
                sq_r.rel(qi, tmm)
                stores.append(tmm)
                return t2
            return epi

        def postnorm(n, gi, xsrc, xdst, stores, final=None):
            ta, tr = rstd_bc(P, C, ps_ss[:, :n], n, D, rstd[:, :n], [stores[-1]])
            state["ss_free"] = ta
            touts = []
            for c in range(32):
                xi, xbuf, xfree = xs_r.get()
                tl = P.dma("sp", xbuf[:, :n], xsrc(c), xs_r.dsems[xi], deps=[xfree])
                bi, bb, bfree = bs_r.get()
                tl2 = P.dma("sp", bb[:, :n], br[c * 128:(c + 1) * 128, :n], bs_r.dsems[bi], deps=[bfree, stores])
                t1 = P.op("dve", lambda e, bb=bb, c=c: e.scalar_tensor_tensor(
                    out=bb[:, :n], in0=bb[:, :n], scalar=g_sb[:, gi, c:c + 1], in1=rstd[:, :n],
                    op0=ALU.mult, op1=ALU.mult), deps=[tl2, tr])
                t2 = P.op("dve", lambda e, bb=bb, xbuf=xbuf: e.tensor_tensor(
                    out=bb[:, :n], in0=bb[:, :n], in1=xbuf[:, :n], op=ALU.add), deps=[t1, tl])
                xs_r.rel(xi, t2)
                ts = P.dma("sp", xdst(c), bb[:, :n], bs_r.dsems[bi], deps=[t2])
                bs_r.rel(bi, ts)
                touts.append(ts)
            return touts[-3:]

        memn = hT[:, :, 0:MEM]
        th = prenorm(lambda c: memT[c * 128:(c + 1) * 128, :], 3, MEM, hT, None)

        def kx_epi(fc, M, g, ps, mm):
            return P.op("act", lambda e: e.activation(out=kx_sb[:, fc, :], in_=ps[:, :MEM], func=AF.Copy), deps=[mm])

        gemm(P, C, lambda fc, wb: ld(w_ck, fc, wb, 32), 32, [(i, 128) for i in range(8)],
             lambda kc, g: hT[:, kc, 0:MEM], lambda fc: ["m"], lambda g: MEM, kx_epi, wA_r, [th])
        wvx = hidb[:, 0:64, :].rearrange("p a b -> p (a b)")[:, 0:32768].rearrange("p (k m) -> p k m", m=1024)
        twv = P.dma("sp", wvx, w_cv.rearrange("(kc p) m -> p kc m", p=128), dmisc)
        last = None
        for mt in range(2):
            for cg in range(2):
                pi, ps, pfree = C.psg.get()
                for kc in range(32):
                    last = P.op("pe", lambda e, ps=ps, kc=kc, mt=mt, cg=cg: e.matmul(
                        ps[:, :], lhsT=hT[:, kc, mt * 128:(mt + 1) * 128], rhs=wvx[:, kc, cg * 512:(cg + 1) * 512],
                        start=(kc == 0), stop=(kc == 31)), deps=[th, twv, pfree] if kc == 0 else [], sig=(kc == 31))
                tn = P.op("act", lambda e, ps=ps, mt=mt, cg=cg: e.activation(
                    out=vx_sb[:, mt, cg * 512:(cg + 1) * 512], in_=ps[:, :], func=AF.Copy), deps=[last])
                C.psg.rel(pi, tn)
        state["scr_free"] = last
        state["hid_free"] = last
        outs = []
        tiles = [(0, 16)] + [(16 + j * 512, 512) for j in range(NT)]
        def do_tile(c0, n):
            halo = (n == 16)
            dl = P.dsem()
            t_in = P.dma("sp", in1[:, 0:16, :n], aT[:, c0:c0 + n].rearrange("(c p) n -> p c n", p=128), dl,
                         deps=[state["hid_free"]])
            t_in = P.dma("sp", in1[:, 16:32, :n], yT[:, c0:c0 + n].rearrange("(c p) n -> p c n", p=128), dl,
                         deps=[state["hid_free"]])
            stores = []
            gemm(P, C, lambda fc, wb: ld(w_out, fc, wb, 32), 32,
                 [(i, 128) for i in range(32)], lambda kc, g: in1[:, kc, :n], lambda fc: ["m"], lambda g: n,
                 branch_epi(n, stores), wA_r, [t_in])
            t1 = postnorm(n, 0, lambda c: xT[c * 128:(c + 1) * 128, c0:c0 + n],
                          lambda c: xa[c * 128:(c + 1) * 128, :n], stores)
            th = prenorm(lambda c: xa[c * 128:(c + 1) * 128, :n], 1, n, hT, t1)

            def q_epi(fc, M, g, ps, mm):
                return P.op("act", lambda e: e.activation(out=qx[:, fc, :n], in_=ps[:, :n], func=AF.Copy), deps=[mm])

            lastq = gemm(P, C, lambda fc, wb: ld(w_cq, fc, wb, 32), 32,
                         [(i, 128) for i in range(8)], lambda kc, g: hT[:, kc, :n], lambda fc: ["m"], lambda g: n,
                         q_epi, wA_r, [th])
            tq_done = C.psg.free[(C.psg.i - 1) % 5]
            tox = None
            for h in range(XH):
                pts = []
                for mt in range(2):
                    pi, ps, pfree = C.psg.get()
                    for dcn in range(2):
                        tmm = P.op("pe", lambda e, ps=ps, h=h, dcn=dcn, mt=mt: e.matmul(
                            ps[:, :n], lhsT=kx_sb[:, 2 * h + dcn, mt * 128:(mt + 1) * 128], rhs=qx[:, 2 * h + dcn, :n],
                            start=(dcn == 0), stop=(dcn == 1)), deps=[tq_done, pfree], sig=(dcn == 1))
                    qi, ptb, pfree2 = pt_r.get()
                    tex = P.op("act", lambda e, ptb=ptb, ps=ps: e.activation(out=ptb[:, :n], in_=ps[:, :n], func=AF.Exp,
                                                                             scale=xscale), deps=[tmm, pfree2])
                    C.psg.rel(pi, tex)
                    pts.append((qi, ptb, tex))
                pl_i, psl, plfree = C.psg.get()
                for mt in range(2):
                    tl_ = P.op("pe", lambda e, psl=psl, mt=mt: e.matmul(psl[:, :n], lhsT=ones[:], rhs=pts[mt][1][:, :n],
                                                                       start=(mt == 0), stop=(mt == 1)),
                               deps=[pts[mt][2], plfree])
                for dv in range(2):
                    for mt in range(2):
                        tpv = P.op("pe", lambda e, dv=dv, mt=mt, h=h: e.matmul(
                            ps_o2[dv][:, :n], lhsT=vx_sb[:, mt, h * 256 + dv * 128:h * 256 + (dv + 1) * 128],
                            rhs=pts[mt][1][:, :n], start=(mt == 0), stop=(mt == 1)), deps=[state["o2_free"]])
                for (qi, ptb, tex) in pts:
                    pt_r.rel(qi, tpv)
                ti, tb_, tfree = t_r.get()
                trl = P.op("dve", lambda e, tb_=tb_, psl=psl: e.reciprocal(out=tb_[:, :n], in_=psl[:, :n]),
                           deps=[tl_, tfree])
                C.psg.rel(pl_i, trl)
                for dv in range(2):
                    tox = P.op("dve", lambda e, dv=dv, h=h, tb_=tb_: e.tensor_tensor(
                        out=ox[:, 2 * h + dv, :n], in0=ps_o2[dv][:, :n], in1=tb_[:, :n], op=ALU.mult),
                        deps=[tpv, trl])
                state["o2_free"] = tox
                t_r.rel(ti, tox)
            stores = []
            gemm(P, C, lambda fc, wb: ld(w_co, fc, wb, 8), 8,
                 [(i, 128) for i in range(32)], lambda kc, g: ox[:, kc, :n], lambda fc: ["m"], lambda g: n,
                 branch_epi(n, stores), wA_r, [tox])
            t2 = postnorm(n, 2, lambda c: xa[c * 128:(c + 1) * 128, :n],
                          lambda c: xb[c * 128:(c + 1) * 128, :n], stores)
            th = prenorm(lambda c: xb[c * 128:(c + 1) * 128, :n], 4, n, hT, t2)
            hst = []
            lastff = None
            for fc in range(FCH):
                gps = {}
                for which, w in (("g", w_gate), ("u", w_up)):
                    if halo and which == "u":
                        continue
                    wi, wb, wfree = wA_r.get()
                    (o_, i_), = ld(w, fc, wb, 32)
                    wtok = P.dma("sp", o_, i_, wA_r.dsems[wi], deps=[wfree])
                    pi, ps, pfree = C.psg.get()
                    for kc in range(32):
                        lastff = P.op("pe", lambda e, ps=ps, wb=wb, kc=kc: e.matmul(
                            ps[:, :n], lhsT=wb[:, kc, :], rhs=hT[:, kc, :n], start=(kc == 0), stop=(kc == 31)),
                            deps=[wtok, th, pfree] if kc == 0 else [], sig=(kc == 31))
                    wA_r.rel(wi, lastff)
                    gps[which] = (pi, ps, lastff)
                gi_, gb, gfree = gb_r.get()
                pi, ps, mm = gps["g"]
                tg1 = P.op("act", lambda e, gb=gb, ps=ps: e.activation(out=gb[:, 2:2 + n], in_=ps[:, :n], func=AF.Copy),
                           deps=[mm, gfree])
                C.psg.rel(pi, tg1)
                if halo:
                    tsv = P.op("dve", lambda e, gb=gb, fc=fc: e.tensor_scalar(
                        out=ghalo[:, fc, :], in0=gb[:, n:n + 2], scalar1=hf_sb[:, 0:1], scalar2=None, op0=ALU.mult),
                        deps=[tg1, tconst])
                    gb_r.rel(gi_, tsv)
                    continue
                tg0 = P.op("pool", lambda e, gb=gb, fc=fc: e.tensor_copy(out=gb[:, 0:2], in_=ghalo[:, fc, :]),
                           deps=[gfree])
                tsv = P.op("pool", lambda e, gb=gb, fc=fc: e.tensor_copy(out=ghalo[:, fc, :], in_=gb[:, n:n + 2]),
                           deps=[tg0, tg1])
                ti, tb_, tfree = t_r.get()
                tc1 = P.op("act", lambda e, tb_=tb_, gb=gb, fc=fc: e.activation(
                    out=tb_[:, :n], in_=gb[:, 2:2 + n], func=AF.Identity, bias=cw_sb[:, fc, 3:4],
                    scale=cw_sb[:, fc, 2:3]), deps=[tg1, tfree, tconst])
                tc2 = P.op("dve", lambda e, tb_=tb_, gb=gb, fc=fc: e.scalar_tensor_tensor(
                    out=tb_[:, :n], in0=gb[:, 1:1 + n], scalar=cw_sb[:, fc, 1:2], in1=tb_[:, :n], op0=ALU.mult,
                    op1=ALU.add), deps=[tc1, tg0])
                tc3 = P.op("dve", lambda e, tb_=tb_, gb=gb, fc=fc: e.scalar_tensor_tensor(
                    out=tb_[:, :n], in0=gb[:, 0:n], scalar=cw_sb[:, fc, 0:1], in1=tb_[:, :n], op0=ALU.mult,
                    op1=ALU.add), deps=[tc2])
                gb_r.rel(gi_, [tc3, tsv])
                tsl = P.op("act", lambda e, tb_=tb_: e.activation(out=tb_[:, :n], in_=tb_[:, :n], func=AF.Silu),
                           deps=[tc3])
                pi2, ps2, mm2 = gps["u"]
                hi, hb, hfree = hb_r.get()
                thd = P.op("dve", lambda e, hb=hb, tb_=tb_, ps2=ps2: e.tensor_tensor(
                    out=hb[:, :n], in0=ps2[:, :n], in1=tb_[:, :n], op=ALU.mult), deps=[tsl, mm2, hfree])
                C.psg.rel(pi2, thd)
                t_r.rel(ti, thd)
                tsh = P.dma("sp", hidd[:, fc, :n], hb[:, :n], hb_r.dsems[hi], deps=[thd])
                hb_r.rel(hi, tsh)
                hst.append(tsh)
            if halo:
                state["scr_free"] = lastff
                state["hid_free"] = lastq
                return
            dh = P.dsem()
            for part in range(2):
                thl = P.dma("sp", hidb[:, part * 43:(part + 1) * 43, :n], hidd[:, part * 43:(part + 1) * 43, :n], dh,
                            deps=[hst[-3:], lastq])
            stores = []
            lastd = gemm(P, C, ld_down,
                         FCH, [(i, 128) for i in range(32)], lambda kc, g: hidb[:, kc, :n], lambda fc: ["m"],
                         lambda g: n, branch_epi(n, stores), wD_r, [thl, lastff])
            state["scr_free"] = lastd
            state["hid_free"] = lastd
            t3 = postnorm(n, 5, lambda c: xb[c * 128:(c + 1) * 128, :n],
                          lambda c, c0=c0: xo[c * 128:(c + 1) * 128, c0 - 16:c0 - 16 + n], stores)
            state["outs"] = t3
        for (c0_, n_) in tiles:
            do_tile(c0_, n_)
        P.emit(state["outs"])
    return nc


def build_cast(L):
    nc = bass.Bass("TRN2", target_bir_lowering=False)
    CW = 4096
    src = nc.dram_tensor("src", [128, L], F32, kind="ExternalInput").ap()
    dst = nc.dram_tensor("dst", [128, L], BF16, kind="ExternalOutput").ap()
    with ExitStack() as st:
        P = Prog(nc, st)
        st_r = Ring(P, [P.sbuf([128, CW], F32) for _ in range(3)], True)
        bf_r = Ring(P, [P.sbuf([128, CW], BF16) for _ in range(3)], True)
        toks = []
        for ci in range(L // CW):
            si, sb, sfree = st_r.get()
            tl = P.dma("sp", sb[:], src[:, ci * CW:(ci + 1) * CW], st_r.dsems[si], deps=[sfree])
            bi, bb, bfree = bf_r.get()
            if ci % 2 == 0:
                tc = P.op("act", lambda e, bb=bb, sb=sb: e.activation(out=bb[:], in_=sb[:], func=AF.Copy),
                          deps=[tl, bfree])
            else:
                tc = P.op("dve", lambda e, bb=bb, sb=sb: e.tensor_copy(out=bb[:], in_=sb[:]), deps=[tl, bfree])
            st_r.rel(si, tc)
            ts = P.dma("sp", dst[:, ci * CW:(ci + 1) * CW], bb[:], bf_r.dsems[bi], deps=[tc])
            bf_r.rel(bi, ts)
            toks.append(ts)
        P.emit(toks[-3:])
    return nc


W_NAMES = [("w_in", D, INCOLS), ("w_uq", QLORA, HEADS * 192), ("w_ukv", KVLORA, HEADS * 256),
           ("w_pool", 2048, 512), ("w_out", D, D), ("w_cq", D, 1024), ("w_ck", D, 1024), ("w_cv", D, 1024),
           ("w_co", 1024, D), ("w_gate", D, DFF), ("w_up", D, DFF), ("w_down", DFF, D)]


def chunked(Wm, M=128):
    K, F = Wm.shape
    return np.ascontiguousarray(
        Wm.reshape(K // 128, 128, F // M, M).transpose(2, 1, 0, 3)).reshape(F // M, 128, (K // 128) * M)


def prep_a_weights(Wl):
    w_in = Wl["w_in"]
    z64 = np.zeros((D, 64), w_in.dtype)
    kr = w_in[:, 1536:1600]
    krs = np.concatenate([w_in[:, 1568:1600], w_in[:, 1536:1568]], axis=1)
    cols = np.concatenate([w_in[:, 0:1536], kr, z64, krs, z64, w_in[:, 1600:]], axis=1)
    uq = Wl["w_uq"].reshape(QLORA, HEADS, 192)
    uq = np.concatenate([uq, uq[:, :, 160:192], uq[:, :, 128:160]], axis=2).reshape(QLORA, HEADS * 256)
    ukv = Wl["w_ukv"].reshape(KVLORA, HEADS, 256)
    uk = np.ascontiguousarray(ukv[:, :, 0:128]).reshape(KVLORA, HEADS * 128)
    uv = np.ascontiguousarray(ukv[:, :, 128:256]).reshape(KVLORA, HEADS * 128)
    wp = Wl["w_pool"].reshape(4, 512, 512)
    wpc = np.concatenate([chunked(wp[g]) for g in range(4)], axis=0)
    return {"w_in": chunked(cols), "w_uq": chunked(uq, 256), "w_uk": chunked(uk), "w_uv": uv, "w_pool": wpc}


def prep_b_weights(Wl):
    d = {k: (chunked(Wl[k]) if k in CHUNK_B else Wl[k])
         for k in ("w_out", "w_cq", "w_ck", "w_co", "w_gate", "w_up", "w_down")}
    d["w_cv"] = Wl["w_cv"]
    return d


def _run(nc, ins):
    return run_bass_kernel_spmd(nc, ins, core_ids=list(range(NCORES))).results


def _tm(v):
    v = np.asarray(v, np.float32)
    return np.ascontiguousarray(v.reshape(-1, 128).T)


def kernel(**inp):
    inp = {k: np.asarray(v) for k, v in inp.items()}
    S = inp["x"].shape[1]
    T = S // NCORES
    L_layers = inp["w_in"].shape[0]
    tot = sum(R * Cc for _, R, Cc in W_NAMES) * L_layers
    blk = 1024 * 4096
    totp = ((tot + blk - 1) // blk) * blk
    flat = np.zeros(totp, np.float32)
    o = 0
    for l in range(L_layers):
        for name, R, Cc in W_NAMES:
            flat[o:o + R * Cc] = inp[name][l].reshape(-1)
            o += R * Cc
    Lc = totp // 1024
    pieces = flat.reshape(NCORES, 128, Lc)
    res = _run(build_cast(Lc), [{"src": pieces[c]} for c in range(NCORES)])
    del flat
    wbf = np.concatenate([res[c]["dst"].reshape(-1) for c in range(NCORES)])
    W = []
    o = 0
    for l in range(L_layers):
        d = {}
        for name, R, Cc in W_NAMES:
            d[name] = wbf[o:o + R * Cc].reshape(R, Cc)
            o += R * Cc
        W.append(d)
    inv = (1.0 / (10000.0 ** (np.arange(0, 64, 2, dtype=np.float32) / 64))).astype(np.float32)
    invf = np.concatenate([inv, inv])[:, None].astype(np.float32)
    sgn = np.concatenate([-np.ones(32), np.ones(32)])[:, None].astype(np.float32)
    pos = inp["positions"][0].astype(np.int32)
    XT = np.ascontiguousarray(inp["x"][0].T)
    memT = np.ascontiguousarray(inp["mem"][0].T)
    nc_a = build_a(T)
    nc_att = build_att(S)
    nc_b = build_b(T)
    for l in range(L_layers):
        xpad = np.concatenate([np.zeros((D, 16), np.float32), XT], axis=1)
        wa = prep_a_weights(W[l])
        ins = []
        for c in range(NCORES):
            ins.append({
                "xT": np.ascontiguousarray(xpad[:, c * T:c * T + T + 16]),
                **wa,
                "pos": pos[None, c * T:(c + 1) * T], "tix": np.arange(c * T, (c + 1) * T, dtype=np.float32)[None, :],
                "gpre": _tm(inp["g_mix_pre"][l]), "gq": _tm(inp["g_q"][l]), "gkv": _tm(inp["g_kv"][l]),
                "spool": _tm(inp["s_pool"][l]), "invf": invf, "sgn": sgn})
        ra = _run(nc_a, ins)
        QN = np.concatenate([ra[c]["qn_o"] for c in range(NCORES)], axis=2)
        QR = np.concatenate([ra[c]["qr_o"] for c in range(NCORES)], axis=2)
        KN = np.concatenate([ra[c]["kn_o"] for c in range(NCORES)], axis=2)
        KR = np.concatenate([ra[c]["kr_o"] for c in range(NCORES)], axis=1)
        V = np.concatenate([ra[c]["v_o"] for c in range(NCORES)], axis=0)
        YT = np.concatenate([ra[c]["y_o"] for c in range(NCORES)], axis=1)
        del ra
        ins = []
        for c in range(NCORES):
            ins.append({"qn": np.ascontiguousarray(QN[2 * c:2 * c + 2]), "qr": np.ascontiguousarray(QR[2 * c:2 * c + 2]),
                        "kn": np.ascontiguousarray(KN[2 * c:2 * c + 2]), "kr": KR,
                        "v": np.ascontiguousarray(V[:, 256 * c:256 * c + 256])})
        rt = _run(nc_att, ins)
        AT = np.concatenate([rt[c]["aT"] for c in range(NCORES)], axis=0)
        del rt, QN, QR, KN, V
        apad = np.concatenate([np.zeros((2048, 16), NPBF), AT], axis=1)
        ypad = np.concatenate([np.zeros((2048, 16), NPBF), YT], axis=1)
        gains = np.ascontiguousarray(np.stack(
            [_tm(inp[k][l]) for k in ("g_mix_post", "g_x_pre", "g_x_post", "g_mem", "g_ffn_pre", "g_ffn_post")], axis=1))
        cwb = np.ascontiguousarray(np.stack([_tm(inp["conv_w"][l][0]), _tm(inp["conv_w"][l][1]),
                                             _tm(inp["conv_w"][l][2]), _tm(inp["conv_b"][l])], axis=2))
        wb_ = prep_b_weights(W[l])
        ins = []
        for c in range(NCORES):
            d = {"xT": np.ascontiguousarray(xpad[:, c * T:c * T + T + 16]),
                 "aT": np.ascontiguousarray(apad[:, c * T:c * T + T + 16]),
                 "yT": np.ascontiguousarray(ypad[:, c * T:c * T + T + 16]),
                 "memT": memT, "gains": gains, "cwb": cwb,
                 "hflag": np.full((128, 1), 0.0 if c == 0 else 1.0, np.float32)}
            d.update(wb_)
            ins.append(d)
        rb = _run(nc_b, ins)
        XT = np.concatenate([rb[c]["xo"] for c in range(NCORES)], axis=1)
        del rb
    return np.ascontiguousarray(XT.T)[None].astype(np.float32)
```

```python
import numpy as np
from contextlib import ExitStack
import concourse.bass as bass
import concourse.mybir as mybir
from concourse.bass_utils import run_bass_kernel_spmd
import ml_dtypes

F32 = mybir.dt.float32
BF16 = mybir.dt.bfloat16
I32 = mybir.dt.int32
AF = mybir.ActivationFunctionType
ALU = mybir.AluOpType
NPBF = ml_dtypes.bfloat16
NCORES = 8

D = 4096
HEADS = 16
QLORA = 1024
KVLORA = 512
ROPE = 64
NOPE = 128
VD = 128
POOLCH = 2048
INCOLS = QLORA + KVLORA + ROPE + POOLCH
DFF = 11008
FCH = DFF // 128
MEM = 256
XH = 4
XHD = 256
EPS = 1e-6
PI = float(np.pi)
QA = "pool"
CHUNK_B = ("w_gate", "w_up", "w_out", "w_cq", "w_ck", "w_down")


class DSem:
    def __init__(self, sem):
        self.sem = sem
        self.count = 0


def _flat(deps):
    out = []
    for t in deps:
        if t is None:
            continue
        if isinstance(t, list):
            out.extend(_flat(t))
        else:
            out.append(t)
    return out


class Prog:
    ENGS = ("pe", "act", "dve", "pool", "sp")

    def __init__(self, nc, stack):
        self.nc = nc
        self.stack = stack
        self.ops = {e: [] for e in self.ENGS}
        self.sems = {e: stack.enter_context(nc.semaphore("sem_" + e)) for e in self.ENGS}
        self.cnt = {e: 0 for e in self.ENGS}
        self.waited = {e: {} for e in self.ENGS}
        self.nsem = 0
        self.nt = 0

    def sbuf(self, shape, dt, name=None):
        self.nt += 1
        return self.stack.enter_context(
            self.nc.sbuf_tensor(name or ("sb%d" % self.nt), list(shape), dt))

    def psum(self, shape, dt, name=None):
        self.nt += 1
        return self.stack.enter_context(
            self.nc.psum_tensor(name or ("ps%d" % self.nt), list(shape), dt))

    def dsem(self):
        self.nsem += 1
        return DSem(self.stack.enter_context(self.nc.semaphore("dsem%d" % self.nsem)))

    def _waits(self, eng, deps):
        w = self.waited[eng]
        best = {}
        for sem, val in _flat(deps):
            k = id(sem)
            if k not in best or best[k][1] < val:
                best[k] = (sem, val)
        waits = []
        for k, (sem, val) in best.items():
            if w.get(k, 0) < val:
                w[k] = val
                waits.append((sem, val))
        return waits

    def op(self, eng, fn, deps=(), sig=True):
        waits = self._waits(eng, deps)
        tok = None
        if sig:
            self.cnt[eng] += 1
            tok = (self.sems[eng], self.cnt[eng])
        self.ops[eng].append((waits, fn, 1 if sig else 0, None))
        return tok

    def dma(self, q, out, in_, dsem, deps=()):
        waits = self._waits(q, deps)
        dsem.count += 16
        self.ops[q].append((waits, ("dma", out, in_), 16, dsem.sem))
        return (dsem.sem, dsem.count)

    def emit(self, final_tokens):
        nc = self.nc
        self.op("sp", None, deps=final_tokens, sig=False)
        block = self.stack.enter_context(nc.Block())
        engmap = {"pe": block.tensor, "act": block.scalar, "dve": block.vector,
                  "pool": block.gpsimd, "sp": block.sync}
        for ename in self.ENGS:
            ops = self.ops[ename]
            mysem = self.sems[ename]

            def body(e, ops=ops, mysem=mysem):
                for waits, fn, inc, dsem in ops:
                    for sem, val in waits:
                        e.wait_ge(sem, val)
                    if fn is None:
                        continue
                    if isinstance(fn, tuple):
                        _, out, in_ = fn
                        e.dma_start(out=out, in_=in_).then_inc(dsem, 16)
                    else:
                        ins = fn(e)
                        if inc:
                            ins.then_inc(mysem, 1)
            engmap[ename](body)


class Ring:
    def __init__(self, P, bufs, with_dsem=False):
        self.bufs = bufs
        self.free = [None] * len(bufs)
        self.i = 0
        self.dsems = [P.dsem() for _ in bufs] if with_dsem else None

    def get(self):
        i = self.i
        self.i = (i + 1) % len(self.bufs)
        return i, self.bufs[i], self.free[i]

    def rel(self, i, tok):
        self.free[i] = tok


def build_att(S, HPC=2):
    nc = bass.Bass("TRN2", target_bir_lowering=False)
    NQT = S // 512
    NKT = S // 128
    scale = float((NOPE + ROPE) ** -0.5)
    qn = nc.dram_tensor("qn", [HPC, 128, S], BF16, kind="ExternalInput").ap()
    qr = nc.dram_tensor("qr", [HPC, 64, S], BF16, kind="ExternalInput").ap()
    kn = nc.dram_tensor("kn", [HPC, 128, S], BF16, kind="ExternalInput").ap()
    kr = nc.dram_tensor("kr", [64, S], BF16, kind="ExternalInput").ap()
    v = nc.dram_tensor("v", [S, HPC * 128], BF16, kind="ExternalInput").ap()
    aT = nc.dram_tensor("aT", [HPC * 128, S], BF16, kind="ExternalOutput").ap()
    with ExitStack() as st:
        P = Prog(nc, st)
        kn_sb = P.sbuf([128, S], BF16)
        kr_sb = P.sbuf([64, S], BF16)
        v_sb = P.sbuf([128, NKT, 128], BF16)
        ones = P.sbuf([128, 128], BF16)
        qn_r = Ring(P, [P.sbuf([128, 512], BF16) for _ in range(2)], True)
        qr_r = Ring(P, [P.sbuf([64, 512], BF16) for _ in range(2)], True)
        pt_r = Ring(P, [P.sbuf([128, 512], BF16) for _ in range(4)])
        ps_s = Ring(P, [P.psum([128, 512], F32) for _ in range(3)])
        ps_o = Ring(P, [P.psum([128, 512], F32) for _ in range(2)])
        ps_l = Ring(P, [P.psum([128, 512], F32) for _ in range(2)])
        rl_r = Ring(P, [P.sbuf([128, 512], F32) for _ in range(2)])
        o_r = Ring(P, [P.sbuf([128, 512], BF16) for _ in range(2)], True)
        dk = P.dsem()
        t_ones = P.op("pool", lambda e: e.memset(ones[:], 1.0))
        t_kr = P.dma("sp", kr_sb[:], kr[:, :], dk)
        kv_free = None
        outs = []
        for h in range(HPC):
            t1 = P.dma("sp", kn_sb[:], kn[h], dk, deps=[kv_free])
            for part in range(4):
                r0 = part * (S // 4)
                tkv = P.dma("sp", v_sb[:, part * (NKT // 4):(part + 1) * (NKT // 4), :],
                            v[r0:r0 + S // 4, h * 128:(h + 1) * 128].rearrange("(kt p) d -> p kt d", p=128),
                            dk, deps=[kv_free])
            last_pe = None
            def do_qt(qt, h, tkv):
                qi, qnb, qfree = qn_r.get()
                tq1 = P.dma("sp", qnb[:], qn[h, :, qt * 512:(qt + 1) * 512], qn_r.dsems[qi], deps=[qfree])
                qj, qrb, qfree2 = qr_r.get()
                tq2 = P.dma("sp", qrb[:], qr[h, :, qt * 512:(qt + 1) * 512], qr_r.dsems[qj], deps=[qfree2])
                oi, pso, ofree = ps_o.get()
                li, psl, lfree = ps_l.get()
                nk = 4 * (qt + 1)
                pend = []

                def issue_s(kt):
                    j = kt - 4 * qt
                    c0 = 128 * j if j > 0 else 0
                    si, pss, sfree = ps_s.get()
                    P.op("pe", lambda e, pss=pss, kt=kt, c0=c0: e.matmul(
                        pss[:, c0:], lhsT=kn_sb[:, kt * 128:(kt + 1) * 128], rhs=qnb[:, c0:],
                        start=True, stop=False), deps=[tkv, tq1, tq2, t_kr, sfree], sig=False)
                    tmm = P.op("pe", lambda e, pss=pss, kt=kt, c0=c0: e.matmul(
                        pss[:, c0:], lhsT=kr_sb[:, kt * 128:(kt + 1) * 128], rhs=qrb[:, c0:],
                        start=False, stop=True))
                    pend.append((kt, j, c0, si, pss, tmm))

                issue_s(0)
                if nk > 1:
                    issue_s(1)
                for kt_ in range(nk):
                    if kt_ + 2 < nk:
                        issue_s(kt_ + 2)
                    kt, j, c0, si, pss, tmm = pend.pop(0)
                    pi, ptb, pfree = pt_r.get()
                    tex = P.op("act", lambda e, ptb=ptb, pss=pss, c0=c0: e.activation(
                        out=ptb[:, c0:], in_=pss[:, c0:], func=AF.Exp, scale=scale), deps=[tmm, pfree])
                    ps_s.rel(si, tex)
                    tp = tex
                    if j >= 0:
                        tp = P.op("pool", lambda e, ptb=ptb, c0=c0: e.memset(ptb[64:128, c0:c0 + 64], 0.0),
                                  deps=[tex])
                    P.op("pe", lambda e, kt=kt, ptb=ptb, c0=c0: e.matmul(
                        pso[:, c0:], lhsT=v_sb[:, kt, :], rhs=ptb[:, c0:],
                        start=(kt == 0), stop=(kt == nk - 1)), deps=[tp, ofree, lfree, t_ones], sig=False)
                    tpv = P.op("pe", lambda e, kt=kt, ptb=ptb, c0=c0: e.matmul(
                        psl[:, c0:], lhsT=ones[:], rhs=ptb[:, c0:],
                        start=(kt == 0), stop=(kt == nk - 1)))
                    pt_r.rel(pi, tpv)
                    last_pe = tpv
                qn_r.rel(qi, last_pe)
                qr_r.rel(qj, last_pe)
                ri, rlb, rfree = rl_r.get()
                trl = P.op("dve", lambda e, rlb=rlb, psl=psl: e.reciprocal(out=rlb[:], in_=psl[:]),
                           deps=[last_pe, rfree])
                ps_l.rel(li, trl)
                ob_i, ob, obfree = o_r.get()
                tmul = P.op("dve", lambda e, ob=ob, pso=pso, rlb=rlb: e.tensor_tensor(
                    out=ob[:], in0=pso[:], in1=rlb[:], op=ALU.mult), deps=[trl, obfree])
                ps_o.rel(oi, tmul)
                rl_r.rel(ri, tmul)
                tst = P.dma(QA, aT[h * 128:(h + 1) * 128, qt * 512:(qt + 1) * 512], ob[:],
                            o_r.dsems[ob_i], deps=[tmul])
                o_r.rel(ob_i, tst)
                outs.append(tst)
                return last_pe

            for qt in range(NQT):
                last_pe = do_qt(qt, h, tkv)
            kv_free = last_pe
        P.emit(outs[-2:])
    return nc


def wlayout(spec):
    offs = {}
    o = 0
    for name, R, C in spec:
        offs[name] = (o, R, C)
        o += R * C
    blk = 1024 * 2048
    tot = ((o + blk - 1) // blk) * blk
    return offs, tot // 1024


def host_wpieces(spec, arrays, Lq):
    flat = np.zeros(1024 * Lq, np.float32)
    o = 0
    for name, R, C in spec:
        flat[o:o + R * C] = np.asarray(arrays[name], np.float32).reshape(-1)
        o += R * C
    return flat.reshape(NCORES, 128, Lq)


def wprep(P, nc, wpiece, Lq):
    CW = 512
    piece_bf = nc.dram_tensor("wpiece_bf", [128, Lq], BF16)
    wflat = nc.dram_tensor("wflat", [NCORES * 128, Lq], BF16)
    st_r = Ring(P, [P.sbuf([128, CW], F32) for _ in range(2)], True)
    bf_r = Ring(P, [P.sbuf([128, CW], BF16) for _ in range(2)], True)
    engs = ["act", "dve"]
    toks = []
    for ci in range(Lq // CW):
        si, sb, sfree = st_r.get()
        tl = P.dma("sp", sb[:], wpiece[:, ci * CW:(ci + 1) * CW], st_r.dsems[si], deps=[sfree])
        bi, bb, bfree = bf_r.get()
        eng = engs[ci % 2]
        if eng == "act":
            tc = P.op("act", lambda e, bb=bb, sb=sb: e.activation(out=bb[:], in_=sb[:], func=AF.Copy),
                      deps=[tl, bfree])
        else:
            tc = P.op(eng, lambda e, bb=bb, sb=sb: e.tensor_copy(out=bb[:], in_=sb[:]), deps=[tl, bfree])
        st_r.rel(si, tc)
        ts = P.dma("sp", piece_bf.ap()[:, ci * CW:(ci + 1) * CW], bb[:], bf_r.dsems[bi], deps=[tc])
        bf_r.rel(bi, ts)
        toks.append(ts)
    tg = P.op("pool", lambda e: e.collective_compute(
        "AllGather", ALU.bypass, replica_groups=[list(range(NCORES))],
        ins=[piece_bf.ap().opt()], outs=[wflat.ap().opt()]), deps=toks[-2:])
    return wflat, tg


def wchunk_ap(wflat, off, C, KC, f0, M, k0=0):
    return bass.AP(wflat, off + k0 * C + f0, [[C, 128], [128 * C, KC], [1, M]])


class Ctx:
    pass


def gemm(P, C, wloads, KC, fchunks, rhs_fn, groups, gwidth, epi, wring, rhs_tok, krows=128, wdeps=None):
    for fc, M in fchunks:
        wi, wb, wfree = wring.get()
        wtok = None
        for (o, i) in wloads(fc, wb):
            wtok = P.dma("sp", o, i, wring.dsems[wi], deps=[wfree, wdeps])
        last = None
        for g in groups(fc):
            pi, ps, pfree = C.psg.get()
            n = gwidth(g)
            for kc in range(KC):
                last = P.op("pe", lambda e, ps=ps, wb=wb, kc=kc, M=M, n=n, g=g: e.matmul(
                    ps[:M, :n], lhsT=wb[:krows, kc, :M], rhs=rhs_fn(kc, g),
                    start=(kc == 0), stop=(kc == KC - 1)),
                    deps=[wtok, rhs_tok, pfree] if kc == 0 else [], sig=(kc == KC - 1))
            tok = epi(fc, M, g, ps, last)
            C.psg.rel(pi, tok)
        wring.rel(wi, last)
    return last


def rstd_bc(P, C, ps_ap, n, Dn, out_ap, deps):
    t = P.op("act", lambda e: e.activation(out=out_ap, in_=ps_ap, func=AF.Sqrt,
                                           bias=C.eps[:, 0:1], scale=1.0 / Dn), deps=deps)
    t2 = P.op("dve", lambda e: e.reciprocal(out=out_ap, in_=out_ap), deps=[t])
    return t, t2


A_SPEC = [("w_in", D, INCOLS), ("w_uq", QLORA, HEADS * 192), ("w_ukv", KVLORA, HEADS * 256),
          ("w_pool", 4 * 512, 512)]


def build_a(T):
    nc = bass.Bass("TRN2", target_bir_lowering=False)
    NT = T // 512
    offs, Lq = wlayout(A_SPEC)
    xT_t = nc.dram_tensor("xT", [D, 16 + T], F32, kind="ExternalInput")
    xT = xT_t.ap()
    w_in_c = nc.dram_tensor("w_in", [30, 128, 4096], BF16, kind="ExternalInput").ap()
    w_uq_c = nc.dram_tensor("w_uq", [HEADS, 128, 8 * 256], BF16, kind="ExternalInput").ap()
    w_uk_c = nc.dram_tensor("w_uk", [HEADS, 128, 4 * 128], BF16, kind="ExternalInput").ap()
    w_uv = nc.dram_tensor("w_uv", [KVLORA, HEADS * 128], BF16, kind="ExternalInput").ap()
    w_pool_c = nc.dram_tensor("w_pool", [16, 128, 4 * 128], BF16, kind="ExternalInput").ap()
    pos_t = nc.dram_tensor("pos", [1, T], I32, kind="ExternalInput")
    tix_t = nc.dram_tensor("tix", [1, T], F32, kind="ExternalInput")
    gpre = nc.dram_tensor("gpre", [128, 32], F32, kind="ExternalInput").ap()
    gq = nc.dram_tensor("gq", [128, 8], F32, kind="ExternalInput").ap()
    gkv = nc.dram_tensor("gkv", [128, 4], F32, kind="ExternalInput").ap()
    spool = nc.dram_tensor("spool", [128, 16], F32, kind="ExternalInput").ap()
    invf = nc.dram_tensor("invf", [64, 1], F32, kind="ExternalInput").ap()
    sgn = nc.dram_tensor("sgn", [64, 1], F32, kind="ExternalInput").ap()
    qn_o = nc.dram_tensor("qn_o", [HEADS, 128, T], BF16, kind="ExternalOutput").ap()
    qr_o = nc.dram_tensor("qr_o", [HEADS, 64, T], BF16, kind="ExternalOutput").ap()
    kn_o = nc.dram_tensor("kn_o", [HEADS, 128, T], BF16, kind="ExternalOutput").ap()
    kr_o = nc.dram_tensor("kr_o", [64, T], BF16, kind="ExternalOutput").ap()
    v_o = nc.dram_tensor("v_o", [T, HEADS * 128], BF16, kind="ExternalOutput").ap()
    y_o = nc.dram_tensor("y_o", [POOLCH, T], BF16, kind="ExternalOutput").ap()
    with ExitStack() as st:
        P = Prog(nc, st)
        C = Ctx()
        tg = None
        C.eps = P.sbuf([128, 1], F32)
        negpi = P.sbuf([64, 1], F32)
        ones = P.sbuf([128, 128], BF16)
        gpre_sb = P.sbuf([128, 32], F32)
        gq_sb = P.sbuf([128, 8], F32)
        gkv_sb = P.sbuf([128, 4], F32)
        sp_sb = P.sbuf([128, 16], F32)
        invf_sb = P.sbuf([64, 1], F32)
        sgn_sb = P.sbuf([64, 1], F32)
        dc = P.dsem()
        tcs = [P.op("pool", lambda e: e.memset(C.eps[:], EPS)),
               P.op("pool", lambda e: e.memset(negpi[:], -PI)),
               P.op("pool", lambda e: e.memset(ones[:], 1.0))]
        for sb, src in ((gpre_sb, gpre), (gq_sb, gq), (gkv_sb, gkv), (sp_sb, spool), (invf_sb, invf), (sgn_sb, sgn)):
            tcd = P.dma("sp", sb[:], src[:, :], dc)
        tconst = tcs + [tcd]
        hT = P.sbuf([128, 32, 528], BF16)
        pT = P.sbuf([128, 16, 512], BF16)
        cq32 = P.sbuf([128, 8, 512], F32)
        ckv32 = P.sbuf([128, 4, 512], F32)
        cqn = P.sbuf([128, 8, 512], BF16)
        ckvn = P.sbuf([128, 4, 512], BF16)
        wv_sb = P.sbuf([128, 4, HEADS * 128], BF16)
        rstd = P.sbuf([128, 528], F32)
        rstd2 = P.sbuf([128, 512], F32)
        cos2 = P.sbuf([64, 512], F32)
        sinS = P.sbuf([64, 512], F32)
        posi = P.sbuf([64, 512], I32)
        ang = P.sbuf([64, 512], F32)
        angm = P.sbuf([64, 512], F32)
        ang2 = P.sbuf([64, 512], F32)
        tixb = P.sbuf([128, 512], F32)
        icnt = P.sbuf([128, 4, 512], F32)
        kr32 = P.sbuf([64, 512], F32)
        krs32 = P.sbuf([64, 512], F32)
        xs_r = Ring(P, [P.sbuf([128, 528], F32) for _ in range(3)], True)
        sq_r = Ring(P, [P.sbuf([128, 528], BF16) for _ in range(9)])
        u_r = Ring(P, [P.sbuf([128, 528], F32) for _ in range(2)])
        s_r = Ring(P, [P.sbuf([128, 528], F32) for _ in range(3)])
        t_r = Ring(P, [P.sbuf([128, 512], F32) for _ in range(4)])
        ob_r = Ring(P, [P.sbuf([128, 512], BF16) for _ in range(4)], True)
        win_r = Ring(P, [P.sbuf([128, 32, 128], BF16) for _ in range(2)], True)
        wq_r = Ring(P, [P.sbuf([128, 8, 256], BF16) for _ in range(2)], True)
        wk_r = Ring(P, [P.sbuf([128, 4, 128], BF16) for _ in range(2)], True)
        C.psg = Ring(P, [P.psum([128, 512], F32) for _ in range(5)])
        ps_ssm = P.psum([128, 512], F32)
        ps_ssh = P.psum([128, 512], F32)
        ps_ss2 = P.psum([128, 512], F32)
        ss_free = None
        ss2_free = None
        dwv = P.dsem()
        twv = P.dma("sp", wv_sb[:, :, :], w_uv.rearrange("(kc p) m -> p kc m", p=128), dwv)
        outs = []
        hT_free = None
        cq_free = None
        pT_free = None
        pe_prev = None
        for j in range(NT):
            c0 = j * 512
            dtab = P.dsem()
            tp1 = P.dma("sp", posi[:], bass.AP(pos_t, c0, [[0, 64], [1, 512]]), dtab, deps=[cq_free])
            tp2 = P.dma("sp", tixb[:], bass.AP(tix_t, c0, [[0, 128], [1, 512]]), dtab, deps=[cq_free])
            ta = P.op("dve", lambda e: e.tensor_copy(out=ang[:], in_=posi[:]), deps=[tp2, tconst])
            ta = P.op("dve", lambda e: e.tensor_scalar(out=ang[:], in0=ang[:], scalar1=invf_sb[:, 0:1], scalar2=None,
                                                       op0=ALU.mult), deps=[ta])
            def sin_of(dst, shift, dep):
                t0 = P.op("dve", lambda e: e.tensor_scalar(out=angm[:], in0=ang[:], scalar1=shift, scalar2=1.0 / (2 * PI),
                                                           op0=ALU.add, op1=ALU.mult), deps=[dep])
                t0 = P.op("dve", lambda e: e.tensor_copy(out=posi[:], in_=angm[:]), deps=[t0])
                t0 = P.op("dve", lambda e: e.tensor_copy(out=angm[:], in_=posi[:]), deps=[t0])
                t0 = P.op("dve", lambda e: e.tensor_scalar(out=angm[:], in0=angm[:], scalar1=-2 * PI, scalar2=shift,
                                                           op0=ALU.mult, op1=ALU.add), deps=[t0])
                t0 = P.op("dve", lambda e: e.tensor_tensor(out=angm[:], in0=angm[:], in1=ang[:], op=ALU.add), deps=[t0])
                t1 = P.op("dve", lambda e: e.tensor_scalar(out=ang2[:], in0=angm[:], scalar1=PI, scalar2=-2 * PI,
                                                           op0=ALU.is_gt, op1=ALU.mult), deps=[t0])
                t1 = P.op("dve", lambda e: e.tensor_tensor(out=angm[:], in0=angm[:], in1=ang2[:], op=ALU.add), deps=[t1])
                t1 = P.op("dve", lambda e: e.tensor_scalar(out=ang2[:], in0=angm[:], scalar1=-PI, scalar2=2 * PI,
                                                           op0=ALU.is_lt, op1=ALU.mult), deps=[t1])
                t1 = P.op("dve", lambda e: e.tensor_tensor(out=angm[:], in0=angm[:], in1=ang2[:], op=ALU.add), deps=[t1])
                return P.op("act", lambda e: e.activation(out=dst[:], in_=angm[:], func=AF.Sin), deps=[t1])

            tsin = sin_of(sinS, 0.0, ta)
            tcos = sin_of(cos2, 0.5 * PI, tsin)
            tsin = P.op("dve", lambda e: e.tensor_scalar(out=sinS[:], in0=sinS[:], scalar1=sgn_sb[:, 0:1],
                                                         scalar2=None, op0=ALU.mult), deps=[tsin])
            trope = [tsin, tcos]
            tic = None
            for g, w in enumerate((2, 4, 8, 16)):
                tic = P.op("dve", lambda e, g=g, w=w: e.tensor_scalar(
                    out=icnt[:, g, :], in0=tixb[:], scalar1=1.0, scalar2=float(w), op0=ALU.add, op1=ALU.min),
                    deps=[tp2, tic])
                tic = P.op("dve", lambda e, g=g: e.reciprocal(out=icnt[:, g, :], in_=icnt[:, g, :]), deps=[tic])
            sqt = None
            for c in range(32):
                xi, xb, xfree = xs_r.get()
                tl = P.dma("sp", xb[:], xT[c * 128:(c + 1) * 128, c0:c0 + 528], xs_r.dsems[xi], deps=[xfree])
                qi, qb, qfree = sq_r.get()
                tsq = P.op("act", lambda e, qb=qb, xb=xb: e.activation(out=qb[:], in_=xb[:], func=AF.Square),
                           deps=[tl, qfree])
                xs_r.rel(xi, tsq)
                P.op("pe", lambda e, qb=qb, c=c: e.matmul(ps_ssm[:, :], lhsT=ones[:], rhs=qb[:, 16:528],
                                                        start=(c == 0), stop=(c == 31)),
                     deps=[tsq, ss_free, tconst], sig=False)
                tmm = P.op("pe", lambda e, qb=qb, c=c: e.matmul(ps_ssh[:, :16], lhsT=ones[:], rhs=qb[:, 0:16],
                                                              start=(c == 0), stop=(c == 31)))
                sq_r.rel(qi, tmm)
            ta1, tr1 = rstd_bc(P, C, ps_ssm[:, :], 512, D, rstd[:, 16:528], [tmm, hT_free])
            ta2, tr2 = rstd_bc(P, C, ps_ssh[:, :16], 16, D, rstd[:, 0:16], [tmm])
            ss_free = [ta1, ta2]
            th = None
            for c in range(32):
                xi, xb, xfree = xs_r.get()
                tl = P.dma("sp", xb[:], xT[c * 128:(c + 1) * 128, c0:c0 + 528], xs_r.dsems[xi], deps=[xfree])
                th = P.op("dve", lambda e, xb=xb, c=c: e.scalar_tensor_tensor(
                    out=hT[:, c, :], in0=xb[:], scalar=gpre_sb[:, c:c + 1], in1=rstd[:], op0=ALU.mult, op1=ALU.mult),
                    deps=[tl, tr1, tr2, hT_free])
                xs_r.rel(xi, th)
                if c == 30:
                    th30 = th
            th_all = [th, th30]
            pend = []

            def win_loads(fc, wb):
                kind, idx = fc
                ci = {"cq": idx, "ckv": 8 + idx, "kr": 12, "krs": 13, "pool": 14 + idx}[kind]
                return [(wb.rearrange("p k m -> p (k m)"), w_in_c[ci])]

            ust = {}

            def win_epi(fc, M, g, ps, mm):
                kind, idx = fc
                if kind in ("cq", "ckv"):
                    dst = cq32 if kind == "cq" else ckv32
                    t1 = P.op("act", lambda e: e.activation(out=dst[:, idx, :], in_=ps[:, :], func=AF.Copy),
                              deps=[mm, cq_free])
                    qi, qb, qfree = sq_r.get()
                    t2 = P.op("act", lambda e: e.activation(out=qb[:, 0:512], in_=ps[:, :], func=AF.Square),
                              deps=[qfree])
                    pend.append((qi, qb, t2))
                    return t2
                if kind in ("kr", "krs"):
                    dst = kr32 if kind == "kr" else krs32
                    return P.op("act", lambda e: e.activation(out=dst[:, :], in_=ps[:64, :], func=AF.Copy),
                                deps=[mm, cq_free])
                if g == "halo":
                    ui, ub, ufree = u_r.get()
                    ust["u"] = (ui, ub)
                    t = P.op("act", lambda e: e.activation(out=ub[:, 0:16], in_=ps[:, 0:16], func=AF.Copy),
                             deps=[mm, ufree])
                    ust["t"] = t
                    return t
                ui, ub = ust["u"]
                tu = P.op("act", lambda e: e.activation(out=ub[:, 16:528], in_=ps[:, :], func=AF.Copy), deps=[mm])
                grp = idx // 4
                cur = ub
                tcur = [tu, ust["t"]]
                sh = 1
                rel = []
                for step in range(grp + 1):
                    si, sb, sfree = s_r.get()
                    eng = "dve" if (idx + step) % 2 == 0 else "pool"
                    tn = P.op(eng, lambda e, sb=sb, cur=cur, sh=sh: e.tensor_tensor(
                        out=sb[:, sh:528], in0=cur[:, sh:528], in1=cur[:, 0:528 - sh], op=ALU.add),
                        deps=[tcur, sfree])
                    if step > 0:
                        s_r.rel(psi, tn)
                    psi = si
                    cur = sb
                    tcur = [tn]
                    sh *= 2
                ti, tb_, tfree = t_r.get()
                tm = P.op("dve", lambda e, tb_=tb_, cur=cur, grp=grp: e.tensor_tensor(
                    out=tb_[:], in0=cur[:, 16:528], in1=icnt[:, grp, :], op=ALU.mult), deps=[tcur, tfree, tic])
                s_r.rel(psi, tm)
                tp = P.op("pool", lambda e, tb_=tb_, ub=ub: e.tensor_tensor(
                    out=pT[:, idx, :], in0=tb_[:], in1=ub[:, 16:528], op=ALU.subtract), deps=[tm, pT_free])
                t_r.rel(ti, tp)
                u_r.rel(ui, tp)
                ust["last"] = tp
                return tu

            fch = [(("cq", i), 128) for i in range(8)]
            gemm(P, C, win_loads, 32, fch, lambda kc, g: hT[:, kc, 16:528], lambda fc: ["main"], lambda g: 512,
                 win_epi, win_r, [th_all, tg])
            for n_, (qi, qb, t2) in enumerate(pend):
                tmm = P.op("pe", lambda e, qb=qb, n_=n_: e.matmul(ps_ss2[:, :], lhsT=ones[:], rhs=qb[:, 0:512],
                                                                start=(n_ == 0), stop=(n_ == 7)),
                           deps=[t2, ss2_free])
                sq_r.rel(qi, tmm)
            pend.clear()
            ta_, trq = rstd_bc(P, C, ps_ss2[:, :], 512, QLORA, rstd2[:, :], [tmm])
            ss2_free = ta_
            tq = None
            for i in range(8):
                tq = P.op("dve", lambda e, i=i: e.scalar_tensor_tensor(
                    out=cqn[:, i, :], in0=cq32[:, i, :], scalar=gq_sb[:, i:i + 1], in1=rstd2[:], op0=ALU.mult,
                    op1=ALU.mult), deps=[trq, pe_prev])
            fch = [(("ckv", i), 128) for i in range(4)] + [(("kr", 0), 64), (("krs", 0), 64)]
            gemm(P, C, win_loads, 32, fch, lambda kc, g: hT[:, kc, 16:528], lambda fc: ["main"], lambda g: 512,
                 win_epi, win_r, [th_all, tg])
            for n_, (qi, qb, t2) in enumerate(pend):
                tmm = P.op("pe", lambda e, qb=qb, n_=n_: e.matmul(ps_ss2[:, :], lhsT=ones[:], rhs=qb[:, 0:512],
                                                                start=(n_ == 0), stop=(n_ == 3)),
                           deps=[t2, ss2_free, tq])
                sq_r.rel(qi, tmm)
            pend.clear()
            fch = [(("pool", i), 128) for i in range(16)]
            hT_free = gemm(P, C, win_loads, 32, fch,
                           lambda kc, g: hT[:, kc, 16:528] if g == "main" else hT[:, kc, 0:16],
                           lambda fc: ["halo", "main"], lambda g: 512 if g == "main" else 16,
                           win_epi, win_r, [th_all, tg])
            ti, tb_, tfree = t_r.get()
            t1 = P.op("dve", lambda e, tb_=tb_: e.tensor_tensor(out=tb_[:64, :], in0=kr32[:], in1=cos2[:], op=ALU.mult),
                      deps=[trope, tfree, C.psg.free])
            ti2, tb2_, tfree2 = t_r.get()
            t2 = P.op("dve", lambda e, tb2_=tb2_: e.tensor_tensor(out=tb2_[:64, :], in0=krs32[:], in1=sinS[:],
                                                                 op=ALU.mult), deps=[trope, tfree2])
            oi, ob, ofree = ob_r.get()
            t3 = P.op("dve", lambda e, ob=ob, tb_=tb_, tb2_=tb2_: e.tensor_tensor(
                out=ob[:64, :], in0=tb_[:64, :], in1=tb2_[:64, :], op=ALU.add), deps=[t1, t2, ofree])
            t_r.rel(ti, t3)
            t_r.rel(ti2, t3)
            ts = P.dma(QA, kr_o[:, c0:c0 + 512], ob[:64, :], ob_r.dsems[oi], deps=[t3])
            ob_r.rel(oi, ts)
            outs.append(ts)
            qst = {}

            def wq_loads(fc, wb):
                return [(wb.rearrange("p k m -> p (k m)"), w_uq_c[fc])]

            for h in range(HEADS):
                wi, wb, wfree = wq_r.get()
                for (o, i_) in wq_loads(h, wb):
                    wtok = P.dma("sp", o, i_, wq_r.dsems[wi], deps=[wfree, tg])
                res = []
                for (m0, M) in ((0, 128), (128, 64), (192, 64)):
                    pi, ps, pfree = C.psg.get()
                    for kc in range(8):
                        last = P.op("pe", lambda e, ps=ps, wb=wb, kc=kc, m0=m0, M=M: e.matmul(
                            ps[:M, :], lhsT=wb[:, kc, m0:m0 + M], rhs=cqn[:, kc, :], start=(kc == 0), stop=(kc == 7)),
                            deps=[wtok, tq, pfree] if kc == 0 else [], sig=(kc == 7))
                    res.append((pi, ps, last))
                wq_r.rel(wi, last)
                (p0, ps0, l0), (p1, ps1, l1), (p2, ps2, l2) = res
                oi, ob, ofree = ob_r.get()
                tn = P.op("act", lambda e, ob=ob, ps0=ps0: e.activation(out=ob[:], in_=ps0[:, :], func=AF.Copy),
                          deps=[l0, ofree])
                C.psg.rel(p0, tn)
                ts = P.dma(QA, qn_o[h, :, c0:c0 + 512], ob[:], ob_r.dsems[oi], deps=[tn])
                ob_r.rel(oi, ts)
                ti, tb_, tfree = t_r.get()
                t1 = P.op("dve", lambda e, tb_=tb_, ps1=ps1: e.tensor_tensor(out=tb_[:64, :], in0=ps1[:64, :],
                                                                           in1=cos2[:], op=ALU.mult),
                          deps=[l1, trope, tfree])
                C.psg.rel(p1, t1)
                ti2, tb2_, tfree2 = t_r.get()
                t2 = P.op("dve", lambda e, tb2_=tb2_, ps2=ps2: e.tensor_tensor(out=tb2_[:64, :], in0=ps2[:64, :],
                                                                             in1=sinS[:], op=ALU.mult),
                          deps=[l2, tfree2])
                C.psg.rel(p2, t2)
                oi, ob, ofree = ob_r.get()
                t3 = P.op("pool", lambda e, ob=ob, tb_=tb_, tb2_=tb2_: e.tensor_tensor(
                    out=ob[:64, :], in0=tb_[:64, :], in1=tb2_[:64, :], op=ALU.add), deps=[t1, t2, ofree])
                t_r.rel(ti, t3)
                t_r.rel(ti2, t3)
                ts = P.dma(QA, qr_o[h, :, c0:c0 + 512], ob[:64, :], ob_r.dsems[oi], deps=[t3])
                ob_r.rel(oi, ts)
                outs.append(ts)
            ta_, trk = rstd_bc(P, C, ps_ss2[:, :], 512, KVLORA, rstd2[:, :], [tmm, tq])
            ss2_free = ta_
            tkv = None
            for i in range(4):
                tkv = P.op("dve", lambda e, i=i: e.scalar_tensor_tensor(
                    out=ckvn[:, i, :], in0=ckv32[:, i, :], scalar=gkv_sb[:, i:i + 1], in1=rstd2[:], op0=ALU.mult,
                    op1=ALU.mult), deps=[trk, pe_prev])
            cq_free = tkv
            def wk_loads(fc, wb):
                return [(wb.rearrange("p k m -> p (k m)"), w_uk_c[fc])]

            def k_epi(fc, M, g, ps, mm):
                oi, ob, ofree = ob_r.get()
                tn = P.op("act", lambda e: e.activation(out=ob[:], in_=ps[:, :], func=AF.Copy), deps=[mm, ofree])
                ts = P.dma(QA, kn_o[fc, :, c0:c0 + 512], ob[:], ob_r.dsems[oi], deps=[tn])
                ob_r.rel(oi, ts)
                outs.append(ts)
                return tn

            gemm(P, C, wk_loads, 4, [(h, 128) for h in range(HEADS)], lambda kc, g: ckvn[:, kc, :],
                 lambda fc: ["main"], lambda g: 512, k_epi, wk_r, [tkv, tg])
            for s in range(4):
                for cg in range(4):
                    pi, ps, pfree = C.psg.get()
                    for kc in range(4):
                        last = P.op("pe", lambda e, ps=ps, kc=kc, s=s, cg=cg: e.matmul(
                            ps[:, :], lhsT=ckvn[:, kc, s * 128:(s + 1) * 128], rhs=wv_sb[:, kc, cg * 512:(cg + 1) * 512],
                            start=(kc == 0), stop=(kc == 3)), deps=[tkv, twv, pfree] if kc == 0 else [],
                            sig=(kc == 3))
                    oi, ob, ofree = ob_r.get()
                    tn = P.op("act", lambda e, ob=ob, ps=ps: e.activation(out=ob[:], in_=ps[:, :], func=AF.Copy),
                              deps=[last, ofree])
                    C.psg.rel(pi, tn)
                    ts = P.dma(QA, v_o[c0 + s * 128:c0 + (s + 1) * 128, cg * 512:(cg + 1) * 512], ob[:],
                               ob_r.dsems[oi], deps=[tn])
                    ob_r.rel(oi, ts)
                    outs.append(ts)
            def wp_loads(fc, wb):
                g_, fi = fc
                return [(wb.rearrange("p k m -> p (k m)"), w_pool_c[g_ * 4 + fi])]

            def p_epi(fc, M, g, ps, mm):
                g_, fi = fc
                ch = g_ * 4 + fi
                oi, ob, ofree = ob_r.get()
                tn = P.op("act", lambda e: e.activation(out=ob[:], in_=ps[:, :], func=AF.Copy,
                                                        scale=sp_sb[:, ch:ch + 1]), deps=[mm, ofree])
                ts = P.dma(QA, y_o[ch * 128:(ch + 1) * 128, c0:c0 + 512], ob[:], ob_r.dsems[oi], deps=[tn])
                ob_r.rel(oi, ts)
                outs.append(ts)
                return tn

            for g_ in range(4):
                pe_last = gemm(P, C, wp_loads, 4, [((g_, fi), 128) for fi in range(4)],
                               lambda kc, g, g_=g_: pT[:, g_ * 4 + kc, :], lambda fc: ["main"], lambda g: 512,
                               p_epi, wk_r, [ust["last"], tg])
            pT_free = pe_last
            pe_prev = pe_last
        P.emit(outs[-8:])
    return nc


def wsl(w, KC, f0, M, k0=0):
    return w[k0:k0 + KC * 128, f0:f0 + M].rearrange("(kc p) m -> p kc m", p=128)


def build_b(T):
    nc = bass.Bass("TRN2", target_bir_lowering=False)
    NT = T // 512
    TW = 16 + T
    xT = nc.dram_tensor("xT", [D, TW], F32, kind="ExternalInput").ap()
    aT = nc.dram_tensor("aT", [2048, TW], BF16, kind="ExternalInput").ap()
    yT = nc.dram_tensor("yT", [2048, TW], BF16, kind="ExternalInput").ap()
    memT = nc.dram_tensor("memT", [D, MEM], F32, kind="ExternalInput").ap()
    gains = nc.dram_tensor("gains", [128, 6, 32], F32, kind="ExternalInput").ap()
    cwb = nc.dram_tensor("cwb", [128, FCH, 4], F32, kind="ExternalInput").ap()
    hflag = nc.dram_tensor("hflag", [128, 1], F32, kind="ExternalInput").ap()
    def wdecl(name, K, F):
        if name in CHUNK_B:
            return nc.dram_tensor(name, [F // 128, 128, K], BF16, kind="ExternalInput").ap()
        return nc.dram_tensor(name, [K, F], BF16, kind="ExternalInput").ap()

    w_out = wdecl("w_out", D, D)
    w_cq = wdecl("w_cq", D, 1024)
    w_ck = wdecl("w_ck", D, 1024)
    w_cv = nc.dram_tensor("w_cv", [D, 1024], BF16, kind="ExternalInput").ap()
    w_co = wdecl("w_co", 1024, D)
    w_gate = wdecl("w_gate", D, DFF)
    w_up = wdecl("w_up", D, DFF)
    w_down = wdecl("w_down", DFF, D)
    wname = {id(w_out): "w_out", id(w_cq): "w_cq", id(w_ck): "w_ck", id(w_co): "w_co", id(w_gate): "w_gate",
             id(w_up): "w_up", id(w_down): "w_down"}
    xo = nc.dram_tensor("xo", [D, T], F32, kind="ExternalOutput").ap()
    br = nc.dram_tensor("br", [D, 512], F32).ap()
    xa = nc.dram_tensor("xa", [D, 512], F32).ap()
    xb = nc.dram_tensor("xb", [D, 512], F32).ap()
    hidd = nc.dram_tensor("hidd", [128, FCH, 512], BF16).ap()
    xscale = float(XHD ** -0.5)
    with ExitStack() as st:
        P = Prog(nc, st)
        C = Ctx()
        C.eps = P.sbuf([128, 1], F32)
        ones = P.sbuf([128, 128], BF16)
        g_sb = P.sbuf([128, 6, 32], F32)
        cw_sb = P.sbuf([128, FCH, 4], F32)
        hf_sb = P.sbuf([128, 1], F32)
        dc = P.dsem()
        P.dma("sp", hf_sb[:], hflag[:, :], dc)
        tconst = [P.op("pool", lambda e: e.memset(C.eps[:], EPS)),
                  P.op("pool", lambda e: e.memset(ones[:], 1.0)),
                  P.dma("sp", g_sb[:], gains[:, :, :], dc), P.dma("sp", cw_sb[:], cwb[:, :, :], dc)]
        tconst = [tconst[0], tconst[1], tconst[3]]
        hidb = P.sbuf([128, FCH, 512], BF16)
        scr = P.sbuf([128, 28672], BF16)
        hT = scr[:, 0:16384].rearrange("p (c n) -> p c n", n=512)
        wA = [scr[:, 16384 + i * 4096:16384 + (i + 1) * 4096].rearrange("p (k m) -> p k m", m=128) for i in range(3)]
        wD = [scr[:, i * 11008:(i + 1) * 11008].rearrange("p (k m) -> p k m", m=128) for i in range(2)]
        wA_r = Ring(P, wA, True)
        wD_r = Ring(P, wD, True)
        flat_of = {}
        for i in range(3):
            flat_of[id(wA[i])] = scr[:, 16384 + i * 4096:16384 + (i + 1) * 4096]
        for i in range(2):
            flat_of[id(wD[i])] = scr[:, i * 11008:(i + 1) * 11008]

        def FL(wb):
            return flat_of[id(wb)]

        def ld(w, fc, wb, KC):
            if wname[id(w)] in CHUNK_B:
                return [(FL(wb)[:, 0:KC * 128], w[fc])]
            return [(wb[:, 0:KC, :], wsl(w, KC, fc * 128, 128))]

        def ld_down(fc, wb):
            if "w_down" in CHUNK_B:
                return [(FL(wb)[:, q * 2752:(q + 1) * 2752], w_down[fc][:, q * 2752:(q + 1) * 2752]) for q in range(4)]
            return [(wb[:, 0:43, :], wsl(w_down, 43, fc * 128, 128)),
                    (wb[:, 43:86, :], wsl(w_down, 43, fc * 128, 128, k0=43 * 128))]
        in1 = hidb[:, 0:32, :]
        qx = hidb[:, 32:40, :]
        ox = hidb[:, 40:48, :]
        kx_sb = P.sbuf([128, 8, MEM], BF16)
        vx_sb = P.sbuf([128, 2, 1024], BF16)
        rstd = P.sbuf([128, 512], F32)
        ghalo = P.sbuf([128, FCH, 2], F32)
        xs_r = Ring(P, [P.sbuf([128, 512], F32) for _ in range(6)], True)
        bs_r = Ring(P, [P.sbuf([128, 512], F32) for _ in range(6)], True)
        sq_r = Ring(P, [P.sbuf([128, 512], BF16) for _ in range(4)])
        st_r = Ring(P, [P.sbuf([128, 512], F32) for _ in range(3)], True)
        gb_r = Ring(P, [P.sbuf([128, 514], F32) for _ in range(2)])
        t_r = Ring(P, [P.sbuf([128, 512], F32) for _ in range(3)])
        pt_r = Ring(P, [P.sbuf([128, 512], BF16) for _ in range(2)])
        hb_r = Ring(P, [P.sbuf([128, 512], BF16) for _ in range(3)], True)
        C.psg = Ring(P, [P.psum([128, 512], F32) for _ in range(5)])
        ps_ss = P.psum([128, 512], F32)
        ps_o2 = [P.psum([128, 512], F32) for _ in range(2)]
        state = {"ss_free": None, "scr_free": None, "hid_free": None, "o2_free": None}
        dmisc = P.dsem()

        def prenorm(src, gi, n, dst, extra):
            tmm = None
            for c in range(32):
                xi, xbuf, xfree = xs_r.get()
                tl = P.dma("sp", xbuf[:, :n], src(c), xs_r.dsems[xi], deps=[xfree, extra])
                qi, qb, qfree = sq_r.get()
                tsq = P.op("act", lambda e, qb=qb, xbuf=xbuf: e.activation(out=qb[:, :n], in_=xbuf[:, :n],
                                                                          func=AF.Square), deps=[tl, qfree])
                xs_r.rel(xi, tsq)
                tmm = P.op("pe", lambda e, qb=qb, c=c: e.matmul(ps_ss[:, :n], lhsT=ones[:], rhs=qb[:, :n],
                                                              start=(c == 0), stop=(c == 31)),
                           deps=[tsq, state["ss_free"], tconst])
                sq_r.rel(qi, tmm)
            ta, tr = rstd_bc(P, C, ps_ss[:, :n], n, D, rstd[:, :n], [tmm])
            state["ss_free"] = ta
            th = None
            for c in range(32):
                xi, xbuf, xfree = xs_r.get()
                tl = P.dma("sp", xbuf[:, :n], src(c), xs_r.dsems[xi], deps=[xfree])
                th = P.op("dve", lambda e, xbuf=xbuf, c=c: e.scalar_tensor_tensor(
                    out=dst[:, c, :n], in0=xbuf[:, :n], scalar=g_sb[:, gi, c:c + 1], in1=rstd[:, :n],
                    op0=ALU.mult, op1=ALU.mult), deps=[tl, tr, state["scr_free"]])
                xs_r.rel(xi, th)
            return th

        def branch_epi(n, stores):
            def epi(fc, M, g, ps, mm):
                si, sb, sfree = st_r.get()
                t1 = P.op("act", lambda e: e.activation(out=sb[:, :n], in_=ps[:, :n], func=AF.Copy), deps=[mm, sfree])
                qi, qb, qfree = sq_r.get()
                t2 = P.op("act", lambda e: e.activation(out=qb[:, :n], in_=ps[:, :n], func=AF.Square), deps=[qfree])
                ts = P.dma(QA, br[fc * 128:(fc + 1) * 128, :n], sb[:, :n], st_r.dsems[si], deps=[t1])
                st_r.rel(si, ts)
                stores.append(ts)
                tmm = P.op("pe", lambda e: e.matmul(ps_ss[:, :n], lhsT=ones[:], rhs=qb[:, :n],
                                                    start=(fc == 0), stop=(fc == 31)), deps=[t2, state["ss_free"]])
                sq_r.rel(qi, tmm)
                stores.append(tmm)
                return t2
            return epi

        def postnorm(n, gi, xsrc, xdst, stores, final=None):
            ta, tr = rstd_bc(P, C, ps_ss[:, :n], n, D, rstd[:, :n], [stores[-1]])
            state["ss_free"] = ta
            touts = []
            for c in range(32):
                xi, xbuf, xfree = xs_r.get()
                tl = P.dma("sp", xbuf[:, :n], xsrc(c), xs_r.dsems[xi], deps=[xfree])
                bi, bb, bfree = bs_r.get()
                tl2 = P.dma("sp", bb[:, :n], br[c * 128:(c + 1) * 128, :n], bs_r.dsems[bi], deps=[bfree, stores])
                t1 = P.op("dve", lambda e, bb=bb, c=c: e.scalar_tensor_tensor(
                    out=bb[:, :n], in0=bb[:, :n], scalar=g_sb[:, gi, c:c + 1], in1=rstd[:, :n],
                    op0=ALU.mult, op1=ALU.mult), deps=[tl2, tr])
                t2 = P.op("dve", lambda e, bb=bb, xbuf=xbuf: e.tensor_tensor(
                    out=bb[:, :n], in0=bb[:, :n], in1=xbuf[:, :n], op=ALU.add), deps=[t1, tl])
                xs_r.rel(xi, t2)
                ts = P.dma(QA, xdst(c), bb[:, :n], bs_r.dsems[bi], deps=[t2])
                bs_r.rel(bi, ts)
                touts.append(ts)
            return touts[-6:]

        memn = hT[:, :, 0:MEM]
        th = prenorm(lambda c: memT[c * 128:(c + 1) * 128, :], 3, MEM, hT, None)

        def kx_epi(fc, M, g, ps, mm):
            return P.op("act", lambda e: e.activation(out=kx_sb[:, fc, :], in_=ps[:, :MEM], func=AF.Copy), deps=[mm])

        gemm(P, C, lambda fc, wb: ld(w_ck, fc, wb, 32), 32, [(i, 128) for i in range(8)],
             lambda kc, g: hT[:, kc, 0:MEM], lambda fc: ["m"], lambda g: MEM, kx_epi, wA_r, [th],
             wdeps=state["scr_free"])
        wvx = hidb[:, 0:64, :].rearrange("p a b -> p (a b)")[:, 0:32768].rearrange("p (k m) -> p k m", m=1024)
        twv = P.dma("sp", wvx, w_cv.rearrange("(kc p) m -> p kc m", p=128), dmisc)
        last = None
        for mt in range(2):
            for cg in range(2):
                pi, ps, pfree = C.psg.get()
                for kc in range(32):
                    last = P.op("pe", lambda e, ps=ps, kc=kc, mt=mt, cg=cg: e.matmul(
                        ps[:, :], lhsT=hT[:, kc, mt * 128:(mt + 1) * 128], rhs=wvx[:, kc, cg * 512:(cg + 1) * 512],
                        start=(kc == 0), stop=(kc == 31)), deps=[th, twv, pfree] if kc == 0 else [], sig=(kc == 31))
                tn = P.op("act", lambda e, ps=ps, mt=mt, cg=cg: e.activation(
                    out=vx_sb[:, mt, cg * 512:(cg + 1) * 512], in_=ps[:, :], func=AF.Copy), deps=[last])
                C.psg.rel(pi, tn)
        state["scr_free"] = last
        state["hid_free"] = last
        outs = []
        tiles = [(0, 16)] + [(16 + j * 512, 512) for j in range(NT)]
        def do_tile(c0, n):
            halo = (n == 16)
            dl = P.dsem()
            t_in = P.dma("sp", in1[:, 0:16, :n], aT[:, c0:c0 + n].rearrange("(c p) n -> p c n", p=128), dl,
                         deps=[state["hid_free"]])
            t_in = P.dma("sp", in1[:, 16:32, :n], yT[:, c0:c0 + n].rearrange("(c p) n -> p c n", p=128), dl,
                         deps=[state["hid_free"]])
            stores = []
            gemm(P, C, lambda fc, wb: ld(w_out, fc, wb, 32), 32,
                 [(i, 128) for i in range(32)], lambda kc, g: in1[:, kc, :n], lambda fc: ["m"], lambda g: n,
                 branch_epi(n, stores), wA_r, [t_in], wdeps=state["scr_free"])
            t1 = postnorm(n, 0, lambda c: xT[c * 128:(c + 1) * 128, c0:c0 + n],
                          lambda c: xa[c * 128:(c + 1) * 128, :n], stores)
            th = prenorm(lambda c: xa[c * 128:(c + 1) * 128, :n], 1, n, hT, t1)

            def q_epi(fc, M, g, ps, mm):
                return P.op("act", lambda e: e.activation(out=qx[:, fc, :n], in_=ps[:, :n], func=AF.Copy), deps=[mm])

            lastq = gemm(P, C, lambda fc, wb: ld(w_cq, fc, wb, 32), 32,
                         [(i, 128) for i in range(8)], lambda kc, g: hT[:, kc, :n], lambda fc: ["m"], lambda g: n,
                         q_epi, wA_r, [th], wdeps=state["scr_free"])
            tq_done = C.psg.free[(C.psg.i - 1) % 5]
            tox = None
            for h in range(XH):
                pts = []
                for mt in range(2):
                    pi, ps, pfree = C.psg.get()
                    for dcn in range(2):
                        tmm = P.op("pe", lambda e, ps=ps, h=h, dcn=dcn, mt=mt: e.matmul(
                            ps[:, :n], lhsT=kx_sb[:, 2 * h + dcn, mt * 128:(mt + 1) * 128], rhs=qx[:, 2 * h + dcn, :n],
                            start=(dcn == 0), stop=(dcn == 1)), deps=[tq_done, pfree], sig=(dcn == 1))
                    qi, ptb, pfree2 = pt_r.get()
                    tex = P.op("act", lambda e, ptb=ptb, ps=ps: e.activation(out=ptb[:, :n], in_=ps[:, :n], func=AF.Exp,
                                                                             scale=xscale), deps=[tmm, pfree2])
                    C.psg.rel(pi, tex)
                    pts.append((qi, ptb, tex))
                pl_i, psl, plfree = C.psg.get()
                for mt in range(2):
                    tl_ = P.op("pe", lambda e, psl=psl, mt=mt: e.matmul(psl[:, :n], lhsT=ones[:], rhs=pts[mt][1][:, :n],
                                                                       start=(mt == 0), stop=(mt == 1)),
                               deps=[pts[mt][2], plfree])
                for dv in range(2):
                    for mt in range(2):
                        tpv = P.op("pe", lambda e, dv=dv, mt=mt, h=h: e.matmul(
                            ps_o2[dv][:, :n], lhsT=vx_sb[:, mt, h * 256 + dv * 128:h * 256 + (dv + 1) * 128],
                            rhs=pts[mt][1][:, :n], start=(mt == 0), stop=(mt == 1)), deps=[state["o2_free"]])
                for (qi, ptb, tex) in pts:
                    pt_r.rel(qi, tpv)
                ti, tb_, tfree = t_r.get()
                trl = P.op("dve", lambda e, tb_=tb_, psl=psl: e.reciprocal(out=tb_[:, :n], in_=psl[:, :n]),
                           deps=[tl_, tfree])
                C.psg.rel(pl_i, trl)
                for dv in range(2):
                    tox = P.op("dve", lambda e, dv=dv, h=h, tb_=tb_: e.tensor_tensor(
                        out=ox[:, 2 * h + dv, :n], in0=ps_o2[dv][:, :n], in1=tb_[:, :n], op=ALU.mult),
                        deps=[tpv, trl])
                state["o2_free"] = tox
                t_r.rel(ti, tox)
            stores = []
            gemm(P, C, lambda fc, wb: ld(w_co, fc, wb, 8), 8,
                 [(i, 128) for i in range(32)], lambda kc, g: ox[:, kc, :n], lambda fc: ["m"], lambda g: n,
                 branch_epi(n, stores), wA_r, [tox], wdeps=state["scr_free"])
            t2 = postnorm(n, 2, lambda c: xa[c * 128:(c + 1) * 128, :n],
                          lambda c: xb[c * 128:(c + 1) * 128, :n], stores)
            th = prenorm(lambda c: xb[c * 128:(c + 1) * 128, :n], 4, n, hT, t2)
            hst = []
            lastff = None
            for fc in range(FCH):
                gps = {}
                for which, w in (("g", w_gate), ("u", w_up)):
                    if halo and which == "u":
                        continue
                    wi, wb, wfree = wA_r.get()
                    (o_, i_), = ld(w, fc, wb, 32)
                    wtok = P.dma("sp", o_, i_, wA_r.dsems[wi], deps=[wfree, state["scr_free"]])
                    pi, ps, pfree = C.psg.get()
                    for kc in range(32):
                        lastff = P.op("pe", lambda e, ps=ps, wb=wb, kc=kc: e.matmul(
                            ps[:, :n], lhsT=wb[:, kc, :], rhs=hT[:, kc, :n], start=(kc == 0), stop=(kc == 31)),
                            deps=[wtok, th, pfree] if kc == 0 else [], sig=(kc == 31))
                    wA_r.rel(wi, lastff)
                    gps[which] = (pi, ps, lastff)
                gi_, gb, gfree = gb_r.get()
                pi, ps, mm = gps["g"]
                tg1 = P.op("act", lambda e, gb=gb, ps=ps: e.activation(out=gb[:, 2:2 + n], in_=ps[:, :n], func=AF.Copy),
                           deps=[mm, gfree])
                C.psg.rel(pi, tg1)
                if halo:
                    tsv = P.op("dve", lambda e, gb=gb, fc=fc: e.tensor_scalar(
                        out=ghalo[:, fc, :], in0=gb[:, n:n + 2], scalar1=hf_sb[:, 0:1], scalar2=None, op0=ALU.mult),
                        deps=[tg1, tconst])
                    gb_r.rel(gi_, tsv)
                    continue
                tg0 = P.op("pool", lambda e, gb=gb, fc=fc: e.tensor_copy(out=gb[:, 0:2], in_=ghalo[:, fc, :]),
                           deps=[gfree])
                tsv = P.op("pool", lambda e, gb=gb, fc=fc: e.tensor_copy(out=ghalo[:, fc, :], in_=gb[:, n:n + 2]),
                           deps=[tg0, tg1])
                ti, tb_, tfree = t_r.get()
                tc1 = P.op("act", lambda e, tb_=tb_, gb=gb, fc=fc: e.activation(
                    out=tb_[:, :n], in_=gb[:, 2:2 + n], func=AF.Identity, bias=cw_sb[:, fc, 3:4],
                    scale=cw_sb[:, fc, 2:3]), deps=[tg1, tfree, tconst])
                tc2 = P.op("dve", lambda e, tb_=tb_, gb=gb, fc=fc: e.scalar_tensor_tensor(
                    out=tb_[:, :n], in0=gb[:, 1:1 + n], scalar=cw_sb[:, fc, 1:2], in1=tb_[:, :n], op0=ALU.mult,
                    op1=ALU.add), deps=[tc1, tg0])
                tc3 = P.op("dve", lambda e, tb_=tb_, gb=gb, fc=fc: e.scalar_tensor_tensor(
                    out=tb_[:, :n], in0=gb[:, 0:n], scalar=cw_sb[:, fc, 0:1], in1=tb_[:, :n], op0=ALU.mult,
                    op1=ALU.add), deps=[tc2])
                gb_r.rel(gi_, [tc3, tsv])
                tsl = P.op("act", lambda e, tb_=tb_: e.activation(out=tb_[:, :n], in_=tb_[:, :n], func=AF.Silu),
                           deps=[tc3])
                pi2, ps2, mm2 = gps["u"]
                hi, hb, hfree = hb_r.get()
                thd = P.op("dve", lambda e, hb=hb, tb_=tb_, ps2=ps2: e.tensor_tensor(
                    out=hb[:, :n], in0=ps2[:, :n], in1=tb_[:, :n], op=ALU.mult), deps=[tsl, mm2, hfree])
                C.psg.rel(pi2, thd)
                t_r.rel(ti, thd)
                tsh = P.dma(QA, hidd[:, fc, :n], hb[:, :n], hb_r.dsems[hi], deps=[thd])
                hb_r.rel(hi, tsh)
                hst.append(tsh)
            if halo:
                state["scr_free"] = lastff
                state["hid_free"] = lastq
                return
            dh = P.dsem()
            for part in range(2):
                thl = P.dma("sp", hidb[:, part * 43:(part + 1) * 43, :n], hidd[:, part * 43:(part + 1) * 43, :n], dh,
                            deps=[hst[-3:], lastq])
            stores = []
            lastd = gemm(P, C, ld_down,
                         FCH, [(i, 128) for i in range(32)], lambda kc, g: hidb[:, kc, :n], lambda fc: ["m"],
                         lambda g: n, branch_epi(n, stores), wD_r, [thl, lastff], wdeps=lastff)
            state["scr_free"] = lastd
            state["hid_free"] = lastd
            t3 = postnorm(n, 5, lambda c: xb[c * 128:(c + 1) * 128, :n],
                          lambda c, c0=c0: xo[c * 128:(c + 1) * 128, c0 - 16:c0 - 16 + n], stores)
            state["outs"] = t3
        for (c0_, n_) in tiles:
            do_tile(c0_, n_)
        P.emit(state["outs"])
    return nc


def build_cast(L):
    nc = bass.Bass("TRN2", target_bir_lowering=False)
    CW = 4096
    src = nc.dram_tensor("src", [128, L], F32, kind="ExternalInput").ap()
    dst = nc.dram_tensor("dst", [128, L], BF16, kind="ExternalOutput").ap()
    with ExitStack() as st:
        P = Prog(nc, st)
        st_r = Ring(P, [P.sbuf([128, CW], F32) for _ in range(3)], True)
        bf_r = Ring(P, [P.sbuf([128, CW], BF16) for _ in range(3)], True)
        toks = []
        for ci in range(L // CW):
            si, sb, sfree = st_r.get()
            tl = P.dma("sp", sb[:], src[:, ci * CW:(ci + 1) * CW], st_r.dsems[si], deps=[sfree])
            bi, bb, bfree = bf_r.get()
            if ci % 2 == 0:
                tc = P.op("act", lambda e, bb=bb, sb=sb: e.activation(out=bb[:], in_=sb[:], func=AF.Copy),
                          deps=[tl, bfree])
            else:
                tc = P.op("dve", lambda e, bb=bb, sb=sb: e.tensor_copy(out=bb[:], in_=sb[:]), deps=[tl, bfree])
            st_r.rel(si, tc)
            ts = P.dma(QA, dst[:, ci * CW:(ci + 1) * CW], bb[:], bf_r.dsems[bi], deps=[tc])
            bf_r.rel(bi, ts)
            toks.append(ts)
        P.emit(toks[-3:])
    return nc


W_NAMES = [("w_in", D, INCOLS), ("w_uq", QLORA, HEADS * 192), ("w_ukv", KVLORA, HEADS * 256),
           ("w_pool", 2048, 512), ("w_out", D, D), ("w_cq", D, 1024), ("w_ck", D, 1024), ("w_cv", D, 1024),
           ("w_co", 1024, D), ("w_gate", D, DFF), ("w_up", D, DFF), ("w_down", DFF, D)]


def chunked(Wm, M=128):
    K, F = Wm.shape
    return np.ascontiguousarray(
        Wm.reshape(K // 128, 128, F // M, M).transpose(2, 1, 0, 3)).reshape(F // M, 128, (K // 128) * M)


def prep_a_weights(Wl):
    w_in = Wl["w_in"]
    z64 = np.zeros((D, 64), w_in.dtype)
    kr = w_in[:, 1536:1600]
    krs = np.concatenate([w_in[:, 1568:1600], w_in[:, 1536:1568]], axis=1)
    cols = np.concatenate([w_in[:, 0:1536], kr, z64, krs, z64, w_in[:, 1600:]], axis=1)
    uq = Wl["w_uq"].reshape(QLORA, HEADS, 192)
    uq = np.concatenate([uq, uq[:, :, 160:192], uq[:, :, 128:160]], axis=2).reshape(QLORA, HEADS * 256)
    ukv = Wl["w_ukv"].reshape(KVLORA, HEADS, 256)
    uk = np.ascontiguousarray(ukv[:, :, 0:128]).reshape(KVLORA, HEADS * 128)
    uv = np.ascontiguousarray(ukv[:, :, 128:256]).reshape(KVLORA, HEADS * 128)
    wp = Wl["w_pool"].reshape(4, 512, 512)
    wpc = np.concatenate([chunked(wp[g]) for g in range(4)], axis=0)
    return {"w_in": chunked(cols), "w_uq": chunked(uq, 256), "w_uk": chunked(uk), "w_uv": uv, "w_pool": wpc}


def prep_b_weights(Wl):
    d = {k: (chunked(Wl[k]) if k in CHUNK_B else Wl[k])
         for k in ("w_out", "w_cq", "w_ck", "w_co", "w_gate", "w_up", "w_down")}
    d["w_cv"] = Wl["w_cv"]
    return d


def _run(nc, ins):
    return run_bass_kernel_spmd(nc, ins, core_ids=list(range(NCORES))).results


def _tm(v):
    v = np.asarray(v, np.float32)
    return np.ascontiguousarray(v.reshape(-1, 128).T)


def kernel(**inp):
    inp = {k: np.asarray(v) for k, v in inp.items()}
    S = inp["x"].shape[1]
    T = S // NCORES
    L_layers = inp["w_in"].shape[0]
    tot = sum(R * Cc for _, R, Cc in W_NAMES) * L_layers
    blk = 1024 * 4096
    totp = ((tot + blk - 1) // blk) * blk
    flat = np.zeros(totp, np.float32)
    o = 0
    for l in range(L_layers):
        for name, R, Cc in W_NAMES:
            flat[o:o + R * Cc] = inp[name][l].reshape(-1)
            o += R * Cc
    Lc = totp // 1024
    pieces = flat.reshape(NCORES, 128, Lc)
    res = _run(build_cast(Lc), [{"src": pieces[c]} for c in range(NCORES)])
    del flat
    wbf = np.concatenate([res[c]["dst"].reshape(-1) for c in range(NCORES)])
    W = []
    o = 0
    for l in range(L_layers):
        d = {}
        for name, R, Cc in W_NAMES:
            d[name] = wbf[o:o + R * Cc].reshape(R, Cc)
            o += R * Cc
        W.append(d)
    inv = (1.0 / (10000.0 ** (np.arange(0, 64, 2, dtype=np.float32) / 64))).astype(np.float32)
    invf = np.concatenate([inv, inv])[:, None].astype(np.float32)
    sgn = np.concatenate([-np.ones(32), np.ones(32)])[:, None].astype(np.float32)
    pos = inp["positions"][0].astype(np.int32)
    XT = np.ascontiguousarray(inp["x"][0].T)
    memT = np.ascontiguousarray(inp["mem"][0].T)
    nc_a = build_a(T)
    nc_att = build_att(S)
    nc_b = build_b(T)
    for l in range(L_layers):
        xpad = np.concatenate([np.zeros((D, 16), np.float32), XT], axis=1)
        wa = prep_a_weights(W[l])
        ins = []
        for c in range(NCORES):
            ins.append({
                "xT": np.ascontiguousarray(xpad[:, c * T:c * T + T + 16]),
                **wa,
                "pos": pos[None, c * T:(c + 1) * T], "tix": np.arange(c * T, (c + 1) * T, dtype=np.float32)[None, :],
                "gpre": _tm(inp["g_mix_pre"][l]), "gq": _tm(inp["g_q"][l]), "gkv": _tm(inp["g_kv"][l]),
                "spool": _tm(inp["s_pool"][l]), "invf": invf, "sgn": sgn})
        ra = _run(nc_a, ins)
        QN = np.concatenate([ra[c]["qn_o"] for c in range(NCORES)], axis=2)
        QR = np.concatenate([ra[c]["qr_o"] for c in range(NCORES)], axis=2)
        KN = np.concatenate([ra[c]["kn_o"] for c in range(NCORES)], axis=2)
        KR = np.concatenate([ra[c]["kr_o"] for c in range(NCORES)], axis=1)
        V = np.concatenate([ra[c]["v_o"] for c in range(NCORES)], axis=0)
        YT = np.concatenate([ra[c]["y_o"] for c in range(NCORES)], axis=1)
        del ra
        ins = []
        for c in range(NCORES):
            ins.append({"qn": np.ascontiguousarray(QN[2 * c:2 * c + 2]), "qr": np.ascontiguousarray(QR[2 * c:2 * c + 2]),
                        "kn": np.ascontiguousarray(KN[2 * c:2 * c + 2]), "kr": KR,
                        "v": np.ascontiguousarray(V[:, 256 * c:256 * c + 256])})
        rt = _run(nc_att, ins)
        AT = np.concatenate([rt[c]["aT"] for c in range(NCORES)], axis=0)
        del rt, QN, QR, KN, V
        apad = np.concatenate([np.zeros((2048, 16), NPBF), AT], axis=1)
        ypad = np.concatenate([np.zeros((2048, 16), NPBF), YT], axis=1)
        gains = np.ascontiguousarray(np.stack(
            [_tm(inp[k][l]) for k in ("g_mix_post", "g_x_pre", "g_x_post", "g_mem", "g_ffn_pre", "g_ffn_post")], axis=1))
        cwb = np.ascontiguousarray(np.stack([_tm(inp["conv_w"][l][0]), _tm(inp["conv_w"][l][1]),
                                             _tm(inp["conv_w"][l][2]), _tm(inp["conv_b"][l])], axis=2))
        wb_ = prep_b_weights(W[l])
        ins = []
        for c in range(NCORES):
            d = {"xT": np.ascontiguousarray(xpad[:, c * T:c * T + T + 16]),
                 "aT": np.ascontiguousarray(apad[:, c * T:c * T + T + 16]),
                 "yT": np.ascontiguousarray(ypad[:, c * T:c * T + T + 16]),
                 "memT": memT, "gains": gains, "cwb": cwb,
                 "hflag": np.full((128, 1), 0.0 if c == 0 else 1.0, np.float32)}
            d.update(wb_)
            ins.append(d)
        rb = _run(nc_b, ins)
        XT = np.concatenate([rb[c]["xo"] for c in range(NCORES)], axis=1)
        del rb
    return np.ascontiguousarray(XT.T)[None].astype(np.float32)
```

```python
import numpy as np
from contextlib import ExitStack
import concourse.bass as bass
import concourse.mybir as mybir
from concourse.bass_utils import run_bass_kernel_spmd
import ml_dtypes

F32 = mybir.dt.float32
BF16 = mybir.dt.bfloat16
I32 = mybir.dt.int32
AF = mybir.ActivationFunctionType
ALU = mybir.AluOpType
NPBF = ml_dtypes.bfloat16
NCORES = 8

D = 4096
HEADS = 16
QLORA = 1024
KVLORA = 512
ROPE = 64
NOPE = 128
VD = 128
POOLCH = 2048
INCOLS = QLORA + KVLORA + ROPE + POOLCH
DFF = 11008
FCH = DFF // 128
MEM = 256
XH = 4
XHD = 256
EPS = 1e-6
PI = float(np.pi)
QA = "pool"
CHUNK_B = ("w_gate", "w_up", "w_out", "w_cq", "w_ck", "w_down")


class DSem:
    def __init__(self, sem):
        self.sem = sem
        self.count = 0


def _flat(deps):
    out = []
    for t in deps:
        if t is None:
            continue
        if isinstance(t, list):
            out.extend(_flat(t))
        else:
            out.append(t)
    return out


class Prog:
    ENGS = ("pe", "act", "dve", "pool", "sp")

    def __init__(self, nc, stack):
        self.nc = nc
        self.stack = stack
        self.ops = {e: [] for e in self.ENGS}
        self.sems = {e: stack.enter_context(nc.semaphore("sem_" + e)) for e in self.ENGS}
        self.cnt = {e: 0 for e in self.ENGS}
        self.waited = {e: {} for e in self.ENGS}
        self.nsem = 0
        self.nt = 0

    def sbuf(self, shape, dt, name=None):
        self.nt += 1
        return self.stack.enter_context(
            self.nc.sbuf_tensor(name or ("sb%d" % self.nt), list(shape), dt))

    def psum(self, shape, dt, name=None):
        self.nt += 1
        return self.stack.enter_context(
            self.nc.psum_tensor(name or ("ps%d" % self.nt), list(shape), dt))

    def dsem(self):
        self.nsem += 1
        return DSem(self.stack.enter_context(self.nc.semaphore("dsem%d" % self.nsem)))

    def _waits(self, eng, deps):
        w = self.waited[eng]
        best = {}
        for sem, val in _flat(deps):
            k = id(sem)
            if k not in best or best[k][1] < val:
                best[k] = (sem, val)
        waits = []
        for k, (sem, val) in best.items():
            if w.get(k, 0) < val:
                w[k] = val
                waits.append((sem, val))
        return waits

    def op(self, eng, fn, deps=(), sig=True):
        waits = self._waits(eng, deps)
        tok = None
        if sig:
            self.cnt[eng] += 1
            tok = (self.sems[eng], self.cnt[eng])
        self.ops[eng].append((waits, fn, 1 if sig else 0, None))
        return tok

    def dma(self, q, out, in_, dsem, deps=()):
        waits = self._waits(q, deps)
        dsem.count += 16
        self.ops[q].append((waits, ("dma", out, in_), 16, dsem.sem))
        return (dsem.sem, dsem.count)

    def emit(self, final_tokens):
        nc = self.nc
        self.op("sp", None, deps=final_tokens, sig=False)
        block = self.stack.enter_context(nc.Block())
        engmap = {"pe": block.tensor, "act": block.scalar, "dve": block.vector,
                  "pool": block.gpsimd, "sp": block.sync}
        for ename in self.ENGS:
            ops = self.ops[ename]
            mysem = self.sems[ename]

            def body(e, ops=ops, mysem=mysem):
                for waits, fn, inc, dsem in ops:
                    for sem, val in waits:
                        e.wait_ge(sem, val)
                    if fn is None:
                        continue
                    if isinstance(fn, tuple):
                        _, out, in_ = fn
                        e.dma_start(out=out, in_=in_).then_inc(dsem, 16)
                    else:
                        ins = fn(e)
                        if inc:
                            ins.then_inc(mysem, 1)
            engmap[ename](body)


class Ring:
    def __init__(self, P, bufs, with_dsem=False):
        self.bufs = bufs
        self.free = [None] * len(bufs)
        self.i = 0
        self.dsems = [P.dsem() for _ in bufs] if with_dsem else None

    def get(self):
        i = self.i
        self.i = (i + 1) % len(self.bufs)
        return i, self.bufs[i], self.free[i]

    def rel(self, i, tok):
        self.free[i] = tok


def build_att(S, HPC=2):
    nc = bass.Bass("TRN2", target_bir_lowering=False)
    NQT = S // 512
    NKT = S // 128
    scale = float((NOPE + ROPE) ** -0.5)
    qn = nc.dram_tensor("qn", [HPC, 128, S], BF16, kind="ExternalInput").ap()
    qr = nc.dram_tensor("qr", [HPC, 64, S], BF16, kind="ExternalInput").ap()
    kn = nc.dram_tensor("kn", [HPC, 128, S], BF16, kind="ExternalInput").ap()
    kr = nc.dram_tensor("kr", [64, S], BF16, kind="ExternalInput").ap()
    v = nc.dram_tensor("v", [S, HPC * 128], BF16, kind="ExternalInput").ap()
    aT = nc.dram_tensor("aT", [HPC * 128, S], BF16, kind="ExternalOutput").ap()
    with ExitStack() as st:
        P = Prog(nc, st)
        kn_sb = P.sbuf([128, S], BF16)
        kr_sb = P.sbuf([64, S], BF16)
        v_sb = P.sbuf([128, NKT, 128], BF16)
        ones = P.sbuf([128, 128], BF16)
        qn_r = Ring(P, [P.sbuf([128, 512], BF16) for _ in range(2)], True)
        qr_r = Ring(P, [P.sbuf([64, 512], BF16) for _ in range(2)], True)
        pt_r = Ring(P, [P.sbuf([128, 512], BF16) for _ in range(4)])
        ps_s = Ring(P, [P.psum([128, 512], F32) for _ in range(3)])
        ps_o = Ring(P, [P.psum([128, 512], F32) for _ in range(2)])
        ps_l = Ring(P, [P.psum([128, 512], F32) for _ in range(2)])
        rl_r = Ring(P, [P.sbuf([128, 512], F32) for _ in range(2)])
        o_r = Ring(P, [P.sbuf([128, 512], BF16) for _ in range(2)], True)
        dk = P.dsem()
        t_ones = P.op("pool", lambda e: e.memset(ones[:], 1.0))
        t_kr = P.dma("sp", kr_sb[:], kr[:, :], dk)
        kv_free = None
        outs = []
        for h in range(HPC):
            t1 = P.dma("sp", kn_sb[:], kn[h], dk, deps=[kv_free])
            for part in range(4):
                r0 = part * (S // 4)
                tkv = P.dma("sp", v_sb[:, part * (NKT // 4):(part + 1) * (NKT // 4), :],
                            v[r0:r0 + S // 4, h * 128:(h + 1) * 128].rearrange("(kt p) d -> p kt d", p=128),
                            dk, deps=[kv_free])
            last_pe = None
            def do_qt(qt, h, tkv):
                qi, qnb, qfree = qn_r.get()
                tq1 = P.dma("sp", qnb[:], qn[h, :, qt * 512:(qt + 1) * 512], qn_r.dsems[qi], deps=[qfree])
                qj, qrb, qfree2 = qr_r.get()
                tq2 = P.dma("sp", qrb[:], qr[h, :, qt * 512:(qt + 1) * 512], qr_r.dsems[qj], deps=[qfree2])
                oi, pso, ofree = ps_o.get()
                li, psl, lfree = ps_l.get()
                nk = 4 * (qt + 1)
                pend = []

                def issue_s(kt):
                    j = kt - 4 * qt
                    c0 = 128 * j if j > 0 else 0
                    si, pss, sfree = ps_s.get()
                    P.op("pe", lambda e, pss=pss, kt=kt, c0=c0: e.matmul(
                        pss[:, c0:], lhsT=kn_sb[:, kt * 128:(kt + 1) * 128], rhs=qnb[:, c0:],
                        start=True, stop=False), deps=[tkv, tq1, tq2, t_kr, sfree], sig=False)
                    tmm = P.op("pe", lambda e, pss=pss, kt=kt, c0=c0: e.matmul(
                        pss[:, c0:], lhsT=kr_sb[:, kt * 128:(kt + 1) * 128], rhs=qrb[:, c0:],
                        start=False, stop=True))
                    pend.append((kt, j, c0, si, pss, tmm))

                issue_s(0)
                if nk > 1:
                    issue_s(1)
                for kt_ in range(nk):
                    if kt_ + 2 < nk:
                        issue_s(kt_ + 2)
                    kt, j, c0, si, pss, tmm = pend.pop(0)
                    pi, ptb, pfree = pt_r.get()
                    tex = P.op("act", lambda e, ptb=ptb, pss=pss, c0=c0: e.activation(
                        out=ptb[:, c0:], in_=pss[:, c0:], func=AF.Exp, scale=scale), deps=[tmm, pfree])
                    ps_s.rel(si, tex)
                    tp = tex
                    if j >= 0:
                        tp = P.op("pool", lambda e, ptb=ptb, c0=c0: e.memset(ptb[64:128, c0:c0 + 64], 0.0),
                                  deps=[tex])
                    P.op("pe", lambda e, kt=kt, ptb=ptb, c0=c0: e.matmul(
                        pso[:, c0:], lhsT=v_sb[:, kt, :], rhs=ptb[:, c0:],
                        start=(kt == 0), stop=(kt == nk - 1)), deps=[tp, ofree, lfree, t_ones], sig=False)
                    tpv = P.op("pe", lambda e, kt=kt, ptb=ptb, c0=c0: e.matmul(
                        psl[:, c0:], lhsT=ones[:], rhs=ptb[:, c0:],
                        start=(kt == 0), stop=(kt == nk - 1)))
                    pt_r.rel(pi, tpv)
                    last_pe = tpv
                qn_r.rel(qi, last_pe)
                qr_r.rel(qj, last_pe)
                ri, rlb, rfree = rl_r.get()
                trl = P.op("dve", lambda e, rlb=rlb, psl=psl: e.reciprocal(out=rlb[:], in_=psl[:]),
                           deps=[last_pe, rfree])
                ps_l.rel(li, trl)
                ob_i, ob, obfree = o_r.get()
                tmul = P.op("dve", lambda e, ob=ob, pso=pso, rlb=rlb: e.tensor_tensor(
                    out=ob[:], in0=pso[:], in1=rlb[:], op=ALU.mult), deps=[trl, obfree])
                ps_o.rel(oi, tmul)
                rl_r.rel(ri, tmul)
                tst = P.dma(QA, aT[h * 128:(h + 1) * 128, qt * 512:(qt + 1) * 512], ob[:],
                            o_r.dsems[ob_i], deps=[tmul])
                o_r.rel(ob_i, tst)
                outs.append(tst)
                return last_pe

            for qt in range(NQT):
                last_pe = do_qt(qt, h, tkv)
            kv_free = last_pe
        P.emit(outs[-2:])
    return nc


def wlayout(spec):
    offs = {}
    o = 0
    for name, R, C in spec:
        offs[name] = (o, R, C)
        o += R * C
    blk = 1024 * 2048
    tot = ((o + blk - 1) // blk) * blk
    return offs, tot // 1024


def host_wpieces(spec, arrays, Lq):
    flat = np.zeros(1024 * Lq, np.float32)
    o = 0
    for name, R, C in spec:
        flat[o:o + R * C] = np.asarray(arrays[name], np.float32).reshape(-1)
        o += R * C
    return flat.reshape(NCORES, 128, Lq)


def wprep(P, nc, wpiece, Lq):
    CW = 512
    piece_bf = nc.dram_tensor("wpiece_bf", [128, Lq], BF16)
    wflat = nc.dram_tensor("wflat", [NCORES * 128, Lq], BF16)
    st_r = Ring(P, [P.sbuf([128, CW], F32) for _ in range(2)], True)
    bf_r = Ring(P, [P.sbuf([128, CW], BF16) for _ in range(2)], True)
    engs = ["act", "dve"]
    toks = []
    for ci in range(Lq // CW):
        si, sb, sfree = st_r.get()
        tl = P.dma("sp", sb[:], wpiece[:, ci * CW:(ci + 1) * CW], st_r.dsems[si], deps=[sfree])
        bi, bb, bfree = bf_r.get()
        eng = engs[ci % 2]
        if eng == "act":
            tc = P.op("act", lambda e, bb=bb, sb=sb: e.activation(out=bb[:], in_=sb[:], func=AF.Copy),
                      deps=[tl, bfree])
        else:
            tc = P.op(eng, lambda e, bb=bb, sb=sb: e.tensor_copy(out=bb[:], in_=sb[:]), deps=[tl, bfree])
        st_r.rel(si, tc)
        ts = P.dma("sp", piece_bf.ap()[:, ci * CW:(ci + 1) * CW], bb[:], bf_r.dsems[bi], deps=[tc])
        bf_r.rel(bi, ts)
        toks.append(ts)
    tg = P.op("pool", lambda e: e.collective_compute(
        "AllGather", ALU.bypass, replica_groups=[list(range(NCORES))],
        ins=[piece_bf.ap().opt()], outs=[wflat.ap().opt()]), deps=toks[-2:])
    return wflat, tg


def wchunk_ap(wflat, off, C, KC, f0, M, k0=0):
    return bass.AP(wflat, off + k0 * C + f0, [[C, 128], [128 * C, KC], [1, M]])


class Ctx:
    pass


def gemm(P, C, wloads, KC, fchunks, rhs_fn, groups, gwidth, epi, wring, rhs_tok, krows=128, wdeps=None):
    for fc, M in fchunks:
        wi, wb, wfree = wring.get()
        wtok = None
        for (o, i) in wloads(fc, wb):
            wtok = P.dma("sp", o, i, wring.dsems[wi], deps=[wfree, wdeps])
        last = None
        for g in groups(fc):
            pi, ps, pfree = C.psg.get()
            n = gwidth(g)
            for kc in range(KC):
                last = P.op("pe", lambda e, ps=ps, wb=wb, kc=kc, M=M, n=n, g=g: e.matmul(
                    ps[:M, :n], lhsT=wb[:krows, kc, :M], rhs=rhs_fn(kc, g),
                    start=(kc == 0), stop=(kc == KC - 1)),
                    deps=[wtok, rhs_tok, pfree] if kc == 0 else [], sig=(kc == KC - 1))
            tok = epi(fc, M, g, ps, last)
            C.psg.rel(pi, tok)
        wring.rel(wi, last)
    return last


def rstd_bc(P, C, ps_ap, n, Dn, out_ap, deps):
    t = P.op("act", lambda e: e.activation(out=out_ap, in_=ps_ap, func=AF.Sqrt,
                                           bias=C.eps[:, 0:1], scale=1.0 / Dn), deps=deps)
    t2 = P.op("dve", lambda e: e.reciprocal(out=out_ap, in_=out_ap), deps=[t])
    return t, t2


A_SPEC = [("w_in", D, INCOLS), ("w_uq", QLORA, HEADS * 192), ("w_ukv", KVLORA, HEADS * 256),
          ("w_pool", 4 * 512, 512)]


def build_a(T):
    nc = bass.Bass("TRN2", target_bir_lowering=False)
    NT = T // 512
    offs, Lq = wlayout(A_SPEC)
    xT_t = nc.dram_tensor("xT", [D, 16 + T], F32, kind="ExternalInput")
    xT = xT_t.ap()
    w_in_c = nc.dram_tensor("w_in", [30, 128, 4096], BF16, kind="ExternalInput").ap()
    w_uq_c = nc.dram_tensor("w_uq", [HEADS, 128, 8 * 256], BF16, kind="ExternalInput").ap()
    w_uk_c = nc.dram_tensor("w_uk", [HEADS, 128, 4 * 128], BF16, kind="ExternalInput").ap()
    w_uv = nc.dram_tensor("w_uv", [KVLORA, HEADS * 128], BF16, kind="ExternalInput").ap()
    w_pool_c = nc.dram_tensor("w_pool", [16, 128, 4 * 128], BF16, kind="ExternalInput").ap()
    pos_t = nc.dram_tensor("pos", [1, T], I32, kind="ExternalInput")
    tix_t = nc.dram_tensor("tix", [1, T], F32, kind="ExternalInput")
    gpre = nc.dram_tensor("gpre", [128, 32], F32, kind="ExternalInput").ap()
    gq = nc.dram_tensor("gq", [128, 8], F32, kind="ExternalInput").ap()
    gkv = nc.dram_tensor("gkv", [128, 4], F32, kind="ExternalInput").ap()
    spool = nc.dram_tensor("spool", [128, 16], F32, kind="ExternalInput").ap()
    invf = nc.dram_tensor("invf", [64, 1], F32, kind="ExternalInput").ap()
    sgn = nc.dram_tensor("sgn", [64, 1], F32, kind="ExternalInput").ap()
    qn_o = nc.dram_tensor("qn_o", [HEADS, 128, T], BF16, kind="ExternalOutput").ap()
    qr_o = nc.dram_tensor("qr_o", [HEADS, 64, T], BF16, kind="ExternalOutput").ap()
    kn_o = nc.dram_tensor("kn_o", [HEADS, 128, T], BF16, kind="ExternalOutput").ap()
    kr_o = nc.dram_tensor("kr_o", [64, T], BF16, kind="ExternalOutput").ap()
    v_o = nc.dram_tensor("v_o", [T, HEADS * 128], BF16, kind="ExternalOutput").ap()
    y_o = nc.dram_tensor("y_o", [POOLCH, T], BF16, kind="ExternalOutput").ap()
    with ExitStack() as st:
        P = Prog(nc, st)
        C = Ctx()
        tg = None
        C.eps = P.sbuf([128, 1], F32)
        negpi = P.sbuf([64, 1], F32)
        ones = P.sbuf([128, 128], BF16)
        gpre_sb = P.sbuf([128, 32], F32)
        gq_sb = P.sbuf([128, 8], F32)
        gkv_sb = P.sbuf([128, 4], F32)
        sp_sb = P.sbuf([128, 16], F32)
        invf_sb = P.sbuf([64, 1], F32)
        sgn_sb = P.sbuf([64, 1], F32)
        dc = P.dsem()
        tcs = [P.op("pool", lambda e: e.memset(C.eps[:], EPS)),
               P.op("pool", lambda e: e.memset(negpi[:], -PI)),
               P.op("pool", lambda e: e.memset(ones[:], 1.0))]
        for sb, src in ((gpre_sb, gpre), (gq_sb, gq), (gkv_sb, gkv), (sp_sb, spool), (invf_sb, invf), (sgn_sb, sgn)):
            tcd = P.dma("sp", sb[:], src[:, :], dc)
        tconst = tcs + [tcd]
        hT = P.sbuf([128, 32, 528], BF16)
        pT = P.sbuf([128, 16, 512], BF16)
        cq32 = P.sbuf([128, 8, 512], F32)
        ckv32 = P.sbuf([128, 4, 512], F32)
        cqn = P.sbuf([128, 8, 512], BF16)
        ckvn = P.sbuf([128, 4, 512], BF16)
        wv_sb = P.sbuf([128, 4, HEADS * 128], BF16)
        rstd = P.sbuf([128, 528], F32)
        rstd2 = P.sbuf([128, 512], F32)
        cos2 = P.sbuf([64, 512], F32)
        sinS = P.sbuf([64, 512], F32)
        posi = P.sbuf([64, 512], I32)
        ang = P.sbuf([64, 512], F32)
        angm = P.sbuf([64, 512], F32)
        ang2 = P.sbuf([64, 512], F32)
        tixb = P.sbuf([128, 512], F32)
        icnt = P.sbuf([128, 4, 512], F32)
        kr32 = P.sbuf([64, 512], F32)
        krs32 = P.sbuf([64, 512], F32)
        xs_r = Ring(P, [P.sbuf([128, 528], F32) for _ in range(3)], True)
        sq_r = Ring(P, [P.sbuf([128, 528], BF16) for _ in range(9)])
        u_r = Ring(P, [P.sbuf([128, 528], F32) for _ in range(2)])
        s_r = Ring(P, [P.sbuf([128, 528], F32) for _ in range(3)])
        t_r = Ring(P, [P.sbuf([128, 512], F32) for _ in range(4)])
        ob_r = Ring(P, [P.sbuf([128, 512], BF16) for _ in range(4)], True)
        win_r = Ring(P, [P.sbuf([128, 32, 128], BF16) for _ in range(2)], True)
        wq_r = Ring(P, [P.sbuf([128, 8, 256], BF16) for _ in range(2)], True)
        wk_r = Ring(P, [P.sbuf([128, 4, 128], BF16) for _ in range(2)], True)
        C.psg = Ring(P, [P.psum([128, 512], F32) for _ in range(5)])
        ps_ssm = P.psum([128, 512], F32)
        ps_ssh = P.psum([128, 512], F32)
        ps_ss2 = P.psum([128, 512], F32)
        ss_free = None
        ss2_free = None
        dwv = P.dsem()
        twv = P.dma("sp", wv_sb[:, :, :], w_uv.rearrange("(kc p) m -> p kc m", p=128), dwv)
        outs = []
        hT_free = None
        cq_free = None
        pT_free = None
        pe_prev = None
        for j in range(NT):
            c0 = j * 512
            dtab = P.dsem()
            tp1 = P.dma("sp", posi[:], bass.AP(pos_t, c0, [[0, 64], [1, 512]]), dtab, deps=[cq_free])
            tp2 = P.dma("sp", tixb[:], bass.AP(tix_t, c0, [[0, 128], [1, 512]]), dtab, deps=[cq_free])
            ta = P.op("dve", lambda e: e.tensor_copy(out=ang[:], in_=posi[:]), deps=[tp2, tconst])
            ta = P.op("dve", lambda e: e.tensor_scalar(out=ang[:], in0=ang[:], scalar1=invf_sb[:, 0:1], scalar2=None,
                                                       op0=ALU.mult), deps=[ta])
            def sin_of(dst, shift, dep):
                t0 = P.op("dve", lambda e: e.tensor_scalar(out=angm[:], in0=ang[:], scalar1=shift, scalar2=1.0 / (2 * PI),
                                                           op0=ALU.add, op1=ALU.mult), deps=[dep])
                t0 = P.op("dve", lambda e: e.tensor_copy(out=posi[:], in_=angm[:]), deps=[t0])
                t0 = P.op("dve", lambda e: e.tensor_copy(out=angm[:], in_=posi[:]), deps=[t0])
                t0 = P.op("dve", lambda e: e.tensor_scalar(out=angm[:], in0=angm[:], scalar1=-2 * PI, scalar2=shift,
                                                           op0=ALU.mult, op1=ALU.add), deps=[t0])
                t0 = P.op("dve", lambda e: e.tensor_tensor(out=angm[:], in0=angm[:], in1=ang[:], op=ALU.add), deps=[t0])
                t1 = P.op("dve", lambda e: e.tensor_scalar(out=ang2[:], in0=angm[:], scalar1=PI, scalar2=-2 * PI,
                                                           op0=ALU.is_gt, op1=ALU.mult), deps=[t0])
                t1 = P.op("dve", lambda e: e.tensor_tensor(out=angm[:], in0=angm[:], in1=ang2[:], op=ALU.add), deps=[t1])
                t1 = P.op("dve", lambda e: e.tensor_scalar(out=ang2[:], in0=angm[:], scalar1=-PI, scalar2=2 * PI,
                                                           op0=ALU.is_lt, op1=ALU.mult), deps=[t1])
                t1 = P.op("dve", lambda e: e.tensor_tensor(out=angm[:], in0=angm[:], in1=ang2[:], op=ALU.add), deps=[t1])
                return P.op("act", lambda e: e.activation(out=dst[:], in_=angm[:], func=AF.Sin), deps=[t1])

            tsin = sin_of(sinS, 0.0, ta)
            tcos = sin_of(cos2, 0.5 * PI, tsin)
            tsin = P.op("dve", lambda e: e.tensor_scalar(out=sinS[:], in0=sinS[:], scalar1=sgn_sb[:, 0:1],
                                                         scalar2=None, op0=ALU.mult), deps=[tsin])
            trope = [tsin, tcos]
            tic = None
            for g, w in enumerate((2, 4, 8, 16)):
                tic = P.op("dve", lambda e, g=g, w=w: e.tensor_scalar(
                    out=icnt[:, g, :], in0=tixb[:], scalar1=1.0, scalar2=float(w), op0=ALU.add, op1=ALU.min),
                    deps=[tp2, tic])
                tic = P.op("dve", lambda e, g=g: e.reciprocal(out=icnt[:, g, :], in_=icnt[:, g, :]), deps=[tic])
            sqt = None
            for c in range(32):
                xi, xb, xfree = xs_r.get()
                tl = P.dma("sp", xb[:], xT[c * 128:(c + 1) * 128, c0:c0 + 528], xs_r.dsems[xi], deps=[xfree])
                qi, qb, qfree = sq_r.get()
                tsq = P.op("act", lambda e, qb=qb, xb=xb: e.activation(out=qb[:], in_=xb[:], func=AF.Square),
                           deps=[tl, qfree])
                xs_r.rel(xi, tsq)
                P.op("pe", lambda e, qb=qb, c=c: e.matmul(ps_ssm[:, :], lhsT=ones[:], rhs=qb[:, 16:528],
                                                        start=(c == 0), stop=(c == 31)),
                     deps=[tsq, ss_free, tconst], sig=False)
                tmm = P.op("pe", lambda e, qb=qb, c=c: e.matmul(ps_ssh[:, :16], lhsT=ones[:], rhs=qb[:, 0:16],
                                                              start=(c == 0), stop=(c == 31)))
                sq_r.rel(qi, tmm)
            ta1, tr1 = rstd_bc(P, C, ps_ssm[:, :], 512, D, rstd[:, 16:528], [tmm, hT_free])
            ta2, tr2 = rstd_bc(P, C, ps_ssh[:, :16], 16, D, rstd[:, 0:16], [tmm])
            ss_free = [ta1, ta2]
            th = None
            for c in range(32):
                xi, xb, xfree = xs_r.get()
                tl = P.dma("sp", xb[:], xT[c * 128:(c + 1) * 128, c0:c0 + 528], xs_r.dsems[xi], deps=[xfree])
                th = P.op("dve", lambda e, xb=xb, c=c: e.scalar_tensor_tensor(
                    out=hT[:, c, :], in0=xb[:], scalar=gpre_sb[:, c:c + 1], in1=rstd[:], op0=ALU.mult, op1=ALU.mult),
                    deps=[tl, tr1, tr2, hT_free])
                xs_r.rel(xi, th)
                if c == 30:
                    th30 = th
            th_all = [th, th30]
            pend = []

            def win_loads(fc, wb):
                kind, idx = fc
                ci = {"cq": idx, "ckv": 8 + idx, "kr": 12, "krs": 13, "pool": 14 + idx}[kind]
                return [(wb.rearrange("p k m -> p (k m)"), w_in_c[ci])]

            ust = {}

            def win_epi(fc, M, g, ps, mm):
                kind, idx = fc
                if kind in ("cq", "ckv"):
                    dst = cq32 if kind == "cq" else ckv32
                    t1 = P.op("act", lambda e: e.activation(out=dst[:, idx, :], in_=ps[:, :], func=AF.Copy),
                              deps=[mm, cq_free])
                    qi, qb, qfree = sq_r.get()
                    t2 = P.op("act", lambda e: e.activation(out=qb[:, 0:512], in_=ps[:, :], func=AF.Square),
                              deps=[qfree])
                    pend.append((qi, qb, t2))
                    return t2
                if kind in ("kr", "krs"):
                    dst = kr32 if kind == "kr" else krs32
                    return P.op("act", lambda e: e.activation(out=dst[:, :], in_=ps[:64, :], func=AF.Copy),
                                deps=[mm, cq_free])
                if g == "halo":
                    ui, ub, ufree = u_r.get()
                    ust["u"] = (ui, ub)
                    t = P.op("act", lambda e: e.activation(out=ub[:, 0:16], in_=ps[:, 0:16], func=AF.Copy),
                             deps=[mm, ufree])
                    ust["t"] = t
                    return t
                ui, ub = ust["u"]
                tu = P.op("act", lambda e: e.activation(out=ub[:, 16:528], in_=ps[:, :], func=AF.Copy), deps=[mm])
                grp = idx // 4
                cur = ub
                tcur = [tu, ust["t"]]
                sh = 1
                rel = []
                for step in range(grp + 1):
                    si, sb, sfree = s_r.get()
                    eng = "dve" if (idx + step) % 2 == 0 else "pool"
                    tn = P.op(eng, lambda e, sb=sb, cur=cur, sh=sh: e.tensor_tensor(
                        out=sb[:, sh:528], in0=cur[:, sh:528], in1=cur[:, 0:528 - sh], op=ALU.add),
                        deps=[tcur, sfree])
                    if step > 0:
                        s_r.rel(psi, tn)
                    psi = si
                    cur = sb
                    tcur = [tn]
                    sh *= 2
                ti, tb_, tfree = t_r.get()
                tm = P.op("dve", lambda e, tb_=tb_, cur=cur, grp=grp: e.tensor_tensor(
                    out=tb_[:], in0=cur[:, 16:528], in1=icnt[:, grp, :], op=ALU.mult), deps=[tcur, tfree, tic])
                s_r.rel(psi, tm)
                tp = P.op("pool", lambda e, tb_=tb_, ub=ub: e.tensor_tensor(
                    out=pT[:, idx, :], in0=tb_[:], in1=ub[:, 16:528], op=ALU.subtract), deps=[tm, pT_free])
                t_r.rel(ti, tp)
                u_r.rel(ui, tp)
                ust["last"] = tp
                return tu

            fch = [(("cq", i), 128) for i in range(8)]
            gemm(P, C, win_loads, 32, fch, lambda kc, g: hT[:, kc, 16:528], lambda fc: ["main"], lambda g: 512,
                 win_epi, win_r, [th_all, tg])
            for n_, (qi, qb, t2) in enumerate(pend):
                tmm = P.op("pe", lambda e, qb=qb, n_=n_: e.matmul(ps_ss2[:, :], lhsT=ones[:], rhs=qb[:, 0:512],
                                                                start=(n_ == 0), stop=(n_ == 7)),
                           deps=[t2, ss2_free])
                sq_r.rel(qi, tmm)
            pend.clear()
            ta_, trq = rstd_bc(P, C, ps_ss2[:, :], 512, QLORA, rstd2[:, :], [tmm])
            ss2_free = ta_
            tq = None
            for i in range(8):
                tq = P.op("dve", lambda e, i=i: e.scalar_tensor_tensor(
                    out=cqn[:, i, :], in0=cq32[:, i, :], scalar=gq_sb[:, i:i + 1], in1=rstd2[:], op0=ALU.mult,
                    op1=ALU.mult), deps=[trq, pe_prev])
            fch = [(("ckv", i), 128) for i in range(4)] + [(("kr", 0), 64), (("krs", 0), 64)]
            gemm(P, C, win_loads, 32, fch, lambda kc, g: hT[:, kc, 16:528], lambda fc: ["main"], lambda g: 512,
                 win_epi, win_r, [th_all, tg])
            for n_, (qi, qb, t2) in enumerate(pend):
                tmm = P.op("pe", lambda e, qb=qb, n_=n_: e.matmul(ps_ss2[:, :], lhsT=ones[:], rhs=qb[:, 0:512],
                                                                start=(n_ == 0), stop=(n_ == 3)),
                           deps=[t2, ss2_free, tq])
                sq_r.rel(qi, tmm)
            pend.clear()
            fch = [(("pool", i), 128) for i in range(16)]
            hT_free = gemm(P, C, win_loads, 32, fch,
                           lambda kc, g: hT[:, kc, 16:528] if g == "main" else hT[:, kc, 0:16],
                           lambda fc: ["halo", "main"], lambda g: 512 if g == "main" else 16,
                           win_epi, win_r, [th_all, tg])
            ti, tb_, tfree = t_r.get()
            t1 = P.op("dve", lambda e, tb_=tb_: e.tensor_tensor(out=tb_[:64, :], in0=kr32[:], in1=cos2[:], op=ALU.mult),
                      deps=[trope, tfree, C.psg.free])
            ti2, tb2_, tfree2 = t_r.get()
            t2 = P.op("dve", lambda e, tb2_=tb2_: e.tensor_tensor(out=tb2_[:64, :], in0=krs32[:], in1=sinS[:],
                                                                 op=ALU.mult), deps=[trope, tfree2])
            oi, ob, ofree = ob_r.get()
            t3 = P.op("dve", lambda e, ob=ob, tb_=tb_, tb2_=tb2_: e.tensor_tensor(
                out=ob[:64, :], in0=tb_[:64, :], in1=tb2_[:64, :], op=ALU.add), deps=[t1, t2, ofree])
            t_r.rel(ti, t3)
            t_r.rel(ti2, t3)
            ts = P.dma(QA, kr_o[:, c0:c0 + 512], ob[:64, :], ob_r.dsems[oi], deps=[t3])
            ob_r.rel(oi, ts)
            outs.append(ts)
            qst = {}

            def wq_loads(fc, wb):
                return [(wb.rearrange("p k m -> p (k m)"), w_uq_c[fc])]

            for h in range(HEADS):
                wi, wb, wfree = wq_r.get()
                for (o, i_) in wq_loads(h, wb):
                    wtok = P.dma("sp", o, i_, wq_r.dsems[wi], deps=[wfree, tg])
                res = []
                for (m0, M) in ((0, 128), (128, 64), (192, 64)):
                    pi, ps, pfree = C.psg.get()
                    for kc in range(8):
                        last = P.op("pe", lambda e, ps=ps, wb=wb, kc=kc, m0=m0, M=M: e.matmul(
                            ps[:M, :], lhsT=wb[:, kc, m0:m0 + M], rhs=cqn[:, kc, :], start=(kc == 0), stop=(kc == 7)),
                            deps=[wtok, tq, pfree] if kc == 0 else [], sig=(kc == 7))
                    res.append((pi, ps, last))
                wq_r.rel(wi, last)
                (p0, ps0, l0), (p1, ps1, l1), (p2, ps2, l2) = res
                oi, ob, ofree = ob_r.get()
                tn = P.op("act", lambda e, ob=ob, ps0=ps0: e.activation(out=ob[:], in_=ps0[:, :], func=AF.Copy),
                          deps=[l0, ofree])
                C.psg.rel(p0, tn)
                ts = P.dma(QA, qn_o[h, :, c0:c0 + 512], ob[:], ob_r.dsems[oi], deps=[tn])
                ob_r.rel(oi, ts)
                ti, tb_, tfree = t_r.get()
                t1 = P.op("dve", lambda e, tb_=tb_, ps1=ps1: e.tensor_tensor(out=tb_[:64, :], in0=ps1[:64, :],
                                                                           in1=cos2[:], op=ALU.mult),
                          deps=[l1, trope, tfree])
                C.psg.rel(p1, t1)
                ti2, tb2_, tfree2 = t_r.get()
                t2 = P.op("dve", lambda e, tb2_=tb2_, ps2=ps2: e.tensor_tensor(out=tb2_[:64, :], in0=ps2[:64, :],
                                                                             in1=sinS[:], op=ALU.mult),
                          deps=[l2, tfree2])
                C.psg.rel(p2, t2)
                oi, ob, ofree = ob_r.get()
                t3 = P.op("pool", lambda e, ob=ob, tb_=tb_, tb2_=tb2_: e.tensor_tensor(
                    out=ob[:64, :], in0=tb_[:64, :], in1=tb2_[:64, :], op=ALU.add), deps=[t1, t2, ofree])
                t_r.rel(ti, t3)
                t_r.rel(ti2, t3)
                ts = P.dma(QA, qr_o[h, :, c0:c0 + 512], ob[:64, :], ob_r.dsems[oi], deps=[t3])
                ob_r.rel(oi, ts)
                outs.append(ts)
            ta_, trk = rstd_bc(P, C, ps_ss2[:, :], 512, KVLORA, rstd2[:, :], [tmm, tq])
            ss2_free = ta_
            tkv = None
            for i in range(4):
                tkv = P.op("dve", lambda e, i=i: e.scalar_tensor_tensor(
                    out=ckvn[:, i, :], in0=ckv32[:, i, :], scalar=gkv_sb[:, i:i + 1], in1=rstd2[:], op0=ALU.mult,
                    op1=ALU.mult), deps=[trk, pe_prev])
            cq_free = tkv
            def wk_loads(fc, wb):
                return [(wb.rearrange("p k m -> p (k m)"), w_uk_c[fc])]

            def k_epi(fc, M, g, ps, mm):
                oi, ob, ofree = ob_r.get()
                tn = P.op("act", lambda e: e.activation(out=ob[:], in_=ps[:, :], func=AF.Copy), deps=[mm, ofree])
                ts = P.dma(QA, kn_o[fc, :, c0:c0 + 512], ob[:], ob_r.dsems[oi], deps=[tn])
                ob_r.rel(oi, ts)
                outs.append(ts)
                return tn

            gemm(P, C, wk_loads, 4, [(h, 128) for h in range(HEADS)], lambda kc, g: ckvn[:, kc, :],
                 lambda fc: ["main"], lambda g: 512, k_epi, wk_r, [tkv, tg])
            for s in range(4):
                for cg in range(4):
                    pi, ps, pfree = C.psg.get()
                    for kc in range(4):
                        last = P.op("pe", lambda e, ps=ps, kc=kc, s=s, cg=cg: e.matmul(
                            ps[:, :], lhsT=ckvn[:, kc, s * 128:(s + 1) * 128], rhs=wv_sb[:, kc, cg * 512:(cg + 1) * 512],
                            start=(kc == 0), stop=(kc == 3)), deps=[tkv, twv, pfree] if kc == 0 else [],
                            sig=(kc == 3))
                    oi, ob, ofree = ob_r.get()
                    tn = P.op("act", lambda e, ob=ob, ps=ps: e.activation(out=ob[:], in_=ps[:, :], func=AF.Copy),
                              deps=[last, ofree])
                    C.psg.rel(pi, tn)
                    ts = P.dma(QA, v_o[c0 + s * 128:c0 + (s + 1) * 128, cg * 512:(cg + 1) * 512], ob[:],
                               ob_r.dsems[oi], deps=[tn])
                    ob_r.rel(oi, ts)
                    outs.append(ts)
            def wp_loads(fc, wb):
                g_, fi = fc
                return [(wb.rearrange("p k m -> p (k m)"), w_pool_c[g_ * 4 + fi])]

            def p_epi(fc, M, g, ps, mm):
                g_, fi = fc
                ch = g_ * 4 + fi
                oi, ob, ofree = ob_r.get()
                tn = P.op("act", lambda e: e.activation(out=ob[:], in_=ps[:, :], func=AF.Copy,
                                                        scale=sp_sb[:, ch:ch + 1]), deps=[mm, ofree])
                ts = P.dma(QA, y_o[ch * 128:(ch + 1) * 128, c0:c0 + 512], ob[:], ob_r.dsems[oi], deps=[tn])
                ob_r.rel(oi, ts)
                outs.append(ts)
                return tn

            for g_ in range(4):
                pe_last = gemm(P, C, wp_loads, 4, [((g_, fi), 128) for fi in range(4)],
                               lambda kc, g, g_=g_: pT[:, g_ * 4 + kc, :], lambda fc: ["main"], lambda g: 512,
                               p_epi, wk_r, [ust["last"], tg])
            pT_free = pe_last
            pe_prev = pe_last
        P.emit(outs[-8:])
    return nc


def wsl(w, KC, f0, M, k0=0):
    return w[k0:k0 + KC * 128, f0:f0 + M].rearrange("(kc p) m -> p kc m", p=128)


def build_b(T):
    nc = bass.Bass("TRN2", target_bir_lowering=False)
    NT = T // 512
    TW = 16 + T
    xT = nc.dram_tensor("xT", [D, TW], F32, kind="ExternalInput").ap()
    aT = nc.dram_tensor("aT", [2048, TW], BF16, kind="ExternalInput").ap()
    yT = nc.dram_tensor("yT", [2048, TW], BF16, kind="ExternalInput").ap()
    memT = nc.dram_tensor("memT", [D, MEM], F32, kind="ExternalInput").ap()
    gains = nc.dram_tensor("gains", [128, 6, 32], F32, kind="ExternalInput").ap()
    cwb = nc.dram_tensor("cwb", [128, FCH, 4], F32, kind="ExternalInput").ap()
    hflag = nc.dram_tensor("hflag", [128, 1], F32, kind="ExternalInput").ap()
    def wdecl(name, K, F):
        if name in CHUNK_B:
            return nc.dram_tensor(name, [F // 128, 128, K], BF16, kind="ExternalInput").ap()
        return nc.dram_tensor(name, [K, F], BF16, kind="ExternalInput").ap()

    w_out = wdecl("w_out", D, D)
    w_cq = wdecl("w_cq", D, 1024)
    w_ck = wdecl("w_ck", D, 1024)
    w_cv = nc.dram_tensor("w_cv", [D, 1024], BF16, kind="ExternalInput").ap()
    w_co = wdecl("w_co", 1024, D)
    w_gate = wdecl("w_gate", D, DFF)
    w_up = wdecl("w_up", D, DFF)
    w_down = wdecl("w_down", DFF, D)
    wname = {id(w_out): "w_out", id(w_cq): "w_cq", id(w_ck): "w_ck", id(w_co): "w_co", id(w_gate): "w_gate",
             id(w_up): "w_up", id(w_down): "w_down"}
    xo = nc.dram_tensor("xo", [D, T], F32, kind="ExternalOutput").ap()
    br = nc.dram_tensor("br", [D, 512], F32).ap()
    xa = nc.dram_tensor("xa", [D, 512], F32).ap()
    xb = nc.dram_tensor("xb", [D, 512], F32).ap()
    hidd = nc.dram_tensor("hidd", [128, FCH, 512], BF16).ap()
    xscale = float(XHD ** -0.5)
    with ExitStack() as st:
        P = Prog(nc, st)
        C = Ctx()
        C.eps = P.sbuf([128, 1], F32)
        ones = P.sbuf([128, 128], BF16)
        g_sb = P.sbuf([128, 6, 32], F32)
        cw_sb = P.sbuf([128, FCH, 4], F32)
        hf_sb = P.sbuf([128, 1], F32)
        dc = P.dsem()
        P.dma("sp", hf_sb[:], hflag[:, :], dc)
        tconst = [P.op("pool", lambda e: e.memset(C.eps[:], EPS)),
                  P.op("pool", lambda e: e.memset(ones[:], 1.0)),
                  P.dma("sp", g_sb[:], gains[:, :, :], dc), P.dma("sp", cw_sb[:], cwb[:, :, :], dc)]
        tconst = [tconst[0], tconst[1], tconst[3]]
        hidb = P.sbuf([128, FCH, 512], BF16)
        scr = P.sbuf([128, 28672], BF16)
        hT = scr[:, 0:16384].rearrange("p (c n) -> p c n", n=512)
        wA = [scr[:, 16384 + i * 4096:16384 + (i + 1) * 4096].rearrange("p (k m) -> p k m", m=128) for i in range(3)]
        wD = [scr[:, i * 11008:(i + 1) * 11008].rearrange("p (k m) -> p k m", m=128) for i in range(2)]
        wA_r = Ring(P, wA, True)
        wD_r = Ring(P, wD, True)
        flat_of = {}
        for i in range(3):
            flat_of[id(wA[i])] = scr[:, 16384 + i * 4096:16384 + (i + 1) * 4096]
        for i in range(2):
            flat_of[id(wD[i])] = scr[:, i * 11008:(i + 1) * 11008]

        def FL(wb):
            return flat_of[id(wb)]

        def ld(w, fc, wb, KC):
            if wname[id(w)] in CHUNK_B:
                return [(FL(wb)[:, 0:KC * 128], w[fc])]
            return [(wb[:, 0:KC, :], wsl(w, KC, fc * 128, 128))]

        def ld_down(fc, wb):
            if "w_down" in CHUNK_B:
                return [(FL(wb)[:, q * 2752:(q + 1) * 2752], w_down[fc][:, q * 2752:(q + 1) * 2752]) for q in range(4)]
            return [(wb[:, 0:43, :], wsl(w_down, 43, fc * 128, 128)),
                    (wb[:, 43:86, :], wsl(w_down, 43, fc * 128, 128, k0=43 * 128))]
        in1 = hidb[:, 0:32, :]
        qx = hidb[:, 32:40, :]
        ox = hidb[:, 40:48, :]
        kx_sb = P.sbuf([128, 8, MEM], BF16)
        vx_sb = P.sbuf([128, 2, 1024], BF16)
        rstd = P.sbuf([128, 512], F32)
        ghalo = P.sbuf([128, FCH, 2], F32)
        xs_r = Ring(P, [P.sbuf([128, 512], F32) for _ in range(6)], True)
        bs_r = Ring(P, [P.sbuf([128, 512], F32) for _ in range(6)], True)
        sq_r = Ring(P, [P.sbuf([128, 512], BF16) for _ in range(4)])
        st_r = Ring(P, [P.sbuf([128, 512], F32) for _ in range(3)], True)
        gb_r = Ring(P, [P.sbuf([128, 514], F32) for _ in range(2)])
        t_r = Ring(P, [P.sbuf([128, 512], F32) for _ in range(3)])
        pt_r = Ring(P, [P.sbuf([128, 512], BF16) for _ in range(2)])
        hb_r = Ring(P, [P.sbuf([128, 512], BF16) for _ in range(3)], True)
        C.psg = Ring(P, [P.psum([128, 512], F32) for _ in range(5)])
        ps_ss = P.psum([128, 512], F32)
        ps_o2 = [P.psum([128, 512], F32) for _ in range(2)]
        state = {"ss_free": None, "scr_free": None, "hid_free": None, "o2_free": None}
        dmisc = P.dsem()

        def prenorm(src, gi, n, dst, extra):
            tmm = None
            for c in range(32):
                xi, xbuf, xfree = xs_r.get()
                tl = P.dma("sp", xbuf[:, :n], src(c), xs_r.dsems[xi], deps=[xfree, extra])
                qi, qb, qfree = sq_r.get()
                tsq = P.op("act", lambda e, qb=qb, xbuf=xbuf: e.activation(out=qb[:, :n], in_=xbuf[:, :n],
                                                                          func=AF.Square), deps=[tl, qfree])
                xs_r.rel(xi, tsq)
                tmm = P.op("pe", lambda e, qb=qb, c=c: e.matmul(ps_ss[:, :n], lhsT=ones[:], rhs=qb[:, :n],
                                                              start=(c == 0), stop=(c == 31)),
                           deps=[tsq, state["ss_free"], tconst])
                sq_r.rel(qi, tmm)
            ta, tr = rstd_bc(P, C, ps_ss[:, :n], n, D, rstd[:, :n], [tmm])
            state["ss_free"] = ta
            th = None
            for c in range(32):
                xi, xbuf, xfree = xs_r.get()
                tl = P.dma("sp", xbuf[:, :n], src(c), xs_r.dsems[xi], deps=[xfree])
                th = P.op("dve", lambda e, xbuf=xbuf, c=c: e.scalar_tensor_tensor(
                    out=dst[:, c, :n], in0=xbuf[:, :n], scalar=g_sb[:, gi, c:c + 1], in1=rstd[:, :n],
                    op0=ALU.mult, op1=ALU.mult), deps=[tl, tr, state["scr_free"]])
                xs_r.rel(xi, th)
            return th

        def branch_epi(n, stores):
            pend = []

            def flush():
                while pend:
                    qi, qb, t2, fc = pend.pop(0)
                    tmm = P.op("pe", lambda e, qb=qb, fc=fc: e.matmul(
                        ps_ss[:, :n], lhsT=ones[:], rhs=qb[:, :n], start=(fc == 0), stop=(fc == 31)),
                        deps=[t2, state["ss_free"]])
                    sq_r.rel(qi, tmm)
                    stores.append(tmm)

            def epi(fc, M, g, ps, mm):
                flush()
                si, sb, sfree = st_r.get()
                t1 = P.op("act", lambda e: e.activation(out=sb[:, :n], in_=ps[:, :n], func=AF.Copy), deps=[mm, sfree])
                qi, qb, qfree = sq_r.get()
                t2 = P.op("act", lambda e: e.activation(out=qb[:, :n], in_=ps[:, :n], func=AF.Square), deps=[qfree])
                ts = P.dma(QA, br[fc * 128:(fc + 1) * 128, :n], sb[:, :n], st_r.dsems[si], deps=[t1])
                st_r.rel(si, ts)
                stores.append(ts)
                pend.append((qi, qb, t2, fc))
                return t2
            epi.flush = flush
            return epi

        def postnorm(n, gi, xsrc, xdst, stores, final=None):
            ta, tr = rstd_bc(P, C, ps_ss[:, :n], n, D, rstd[:, :n], [stores[-1]])
            state["ss_free"] = ta
            touts = []
            for c in range(32):
                xi, xbuf, xfree = xs_r.get()
                tl = P.dma("sp", xbuf[:, :n], xsrc(c), xs_r.dsems[xi], deps=[xfree])
                bi, bb, bfree = bs_r.get()
                tl2 = P.dma("sp", bb[:, :n], br[c * 128:(c + 1) * 128, :n], bs_r.dsems[bi], deps=[bfree, stores])
                t1 = P.op("dve", lambda e, bb=bb, c=c: e.scalar_tensor_tensor(
                    out=bb[:, :n], in0=bb[:, :n], scalar=g_sb[:, gi, c:c + 1], in1=rstd[:, :n],
                    op0=ALU.mult, op1=ALU.mult), deps=[tl2, tr])
                t2 = P.op("dve", lambda e, bb=bb, xbuf=xbuf: e.tensor_tensor(
                    out=bb[:, :n], in0=bb[:, :n], in1=xbuf[:, :n], op=ALU.add), deps=[t1, tl])
                xs_r.rel(xi, t2)
                ts = P.dma(QA, xdst(c), bb[:, :n], bs_r.dsems[bi], deps=[t2])
                bs_r.rel(bi, ts)
                touts.append(ts)
            return touts[-6:]

        memn = hT[:, :, 0:MEM]
        th = prenorm(lambda c: memT[c * 128:(c + 1) * 128, :], 3, MEM, hT, None)

        def kx_epi(fc, M, g, ps, mm):
            return P.op("act", lambda e: e.activation(out=kx_sb[:, fc, :], in_=ps[:, :MEM], func=AF.Copy), deps=[mm])

        gemm(P, C, lambda fc, wb: ld(w_ck, fc, wb, 32), 32, [(i, 128) for i in range(8)],
             lambda kc, g: hT[:, kc, 0:MEM], lambda fc: ["m"], lambda g: MEM, kx_epi, wA_r, [th],
             wdeps=state["scr_free"])
        wvx = hidb[:, 0:64, :].rearrange("p a b -> p (a b)")[:, 0:32768].rearrange("p (k m) -> p k m", m=1024)
        twv = P.dma("sp", wvx, w_cv.rearrange("(kc p) m -> p kc m", p=128), dmisc)
        last = None
        for mt in range(2):
            for cg in range(2):
                pi, ps, pfree = C.psg.get()
                for kc in range(32):
                    last = P.op("pe", lambda e, ps=ps, kc=kc, mt=mt, cg=cg: e.matmul(
                        ps[:, :], lhsT=hT[:, kc, mt * 128:(mt + 1) * 128], rhs=wvx[:, kc, cg * 512:(cg + 1) * 512],
                        start=(kc == 0), stop=(kc == 31)), deps=[th, twv, pfree] if kc == 0 else [], sig=(kc == 31))
                tn = P.op("act", lambda e, ps=ps, mt=mt, cg=cg: e.activation(
                    out=vx_sb[:, mt, cg * 512:(cg + 1) * 512], in_=ps[:, :], func=AF.Copy), deps=[last])
                C.psg.rel(pi, tn)
        state["scr_free"] = last
        state["hid_free"] = last
        outs = []
        tiles = [(0, 16)] + [(16 + j * 512, 512) for j in range(NT)]
        def do_tile(c0, n):
            halo = (n == 16)
            dl = P.dsem()
            t_in = P.dma("sp", in1[:, 0:16, :n], aT[:, c0:c0 + n].rearrange("(c p) n -> p c n", p=128), dl,
                         deps=[state["hid_free"]])
            t_in = P.dma("sp", in1[:, 16:32, :n], yT[:, c0:c0 + n].rearrange("(c p) n -> p c n", p=128), dl,
                         deps=[state["hid_free"]])
            stores = []
            bepi = branch_epi(n, stores)
            gemm(P, C, lambda fc, wb: ld(w_out, fc, wb, 32), 32,
                 [(i, 128) for i in range(32)], lambda kc, g: in1[:, kc, :n], lambda fc: ["m"], lambda g: n,
                 bepi, wA_r, [t_in], wdeps=state["scr_free"])
            bepi.flush()
            t1 = postnorm(n, 0, lambda c: xT[c * 128:(c + 1) * 128, c0:c0 + n],
                          lambda c: xa[c * 128:(c + 1) * 128, :n], stores)
            th = prenorm(lambda c: xa[c * 128:(c + 1) * 128, :n], 1, n, hT, t1)

            def q_epi(fc, M, g, ps, mm):
                return P.op("act", lambda e: e.activation(out=qx[:, fc, :n], in_=ps[:, :n], func=AF.Copy), deps=[mm])

            lastq = gemm(P, C, lambda fc, wb: ld(w_cq, fc, wb, 32), 32,
                         [(i, 128) for i in range(8)], lambda kc, g: hT[:, kc, :n], lambda fc: ["m"], lambda g: n,
                         q_epi, wA_r, [th], wdeps=state["scr_free"])
            tq_done = C.psg.free[(C.psg.i - 1) % 5]
            tox = None
            for h in range(XH):
                pts = []
                for mt in range(2):
                    pi, ps, pfree = C.psg.get()
                    for dcn in range(2):
                        tmm = P.op("pe", lambda e, ps=ps, h=h, dcn=dcn, mt=mt: e.matmul(
                            ps[:, :n], lhsT=kx_sb[:, 2 * h + dcn, mt * 128:(mt + 1) * 128], rhs=qx[:, 2 * h + dcn, :n],
                            start=(dcn == 0), stop=(dcn == 1)), deps=[tq_done, pfree], sig=(dcn == 1))
                    qi, ptb, pfree2 = pt_r.get()
                    tex = P.op("act", lambda e, ptb=ptb, ps=ps: e.activation(out=ptb[:, :n], in_=ps[:, :n], func=AF.Exp,
                                                                             scale=xscale), deps=[tmm, pfree2])
                    C.psg.rel(pi, tex)
                    pts.append((qi, ptb, tex))
                pl_i, psl, plfree = C.psg.get()
                for mt in range(2):
                    tl_ = P.op("pe", lambda e, psl=psl, mt=mt: e.matmul(psl[:, :n], lhsT=ones[:], rhs=pts[mt][1][:, :n],
                                                                       start=(mt == 0), stop=(mt == 1)),
                               deps=[pts[mt][2], plfree])
                for dv in range(2):
                    for mt in range(2):
                        tpv = P.op("pe", lambda e, dv=dv, mt=mt, h=h: e.matmul(
                            ps_o2[dv][:, :n], lhsT=vx_sb[:, mt, h * 256 + dv * 128:h * 256 + (dv + 1) * 128],
                            rhs=pts[mt][1][:, :n], start=(mt == 0), stop=(mt == 1)), deps=[state["o2_free"]])
                for (qi, ptb, tex) in pts:
                    pt_r.rel(qi, tpv)
                ti, tb_, tfree = t_r.get()
                trl = P.op("dve", lambda e, tb_=tb_, psl=psl: e.reciprocal(out=tb_[:, :n], in_=psl[:, :n]),
                           deps=[tl_, tfree])
                C.psg.rel(pl_i, trl)
                for dv in range(2):
                    tox = P.op("dve", lambda e, dv=dv, h=h, tb_=tb_: e.tensor_tensor(
                        out=ox[:, 2 * h + dv, :n], in0=ps_o2[dv][:, :n], in1=tb_[:, :n], op=ALU.mult),
                        deps=[tpv, trl])
                state["o2_free"] = tox
                t_r.rel(ti, tox)
            stores = []
            bepi = branch_epi(n, stores)
            gemm(P, C, lambda fc, wb: ld(w_co, fc, wb, 8), 8,
                 [(i, 128) for i in range(32)], lambda kc, g: ox[:, kc, :n], lambda fc: ["m"], lambda g: n,
                 bepi, wA_r, [tox], wdeps=state["scr_free"])
            bepi.flush()
            t2 = postnorm(n, 2, lambda c: xa[c * 128:(c + 1) * 128, :n],
                          lambda c: xb[c * 128:(c + 1) * 128, :n], stores)
            th = prenorm(lambda c: xb[c * 128:(c + 1) * 128, :n], 4, n, hT, t2)
            hst = []
            lastff = None
            for fc in range(FCH):
                gps = {}
                for which, w in (("g", w_gate), ("u", w_up)):
                    if halo and which == "u":
                        continue
                    wi, wb, wfree = wA_r.get()
                    (o_, i_), = ld(w, fc, wb, 32)
                    wtok = P.dma("sp", o_, i_, wA_r.dsems[wi], deps=[wfree, state["scr_free"]])
                    pi, ps, pfree = C.psg.get()
                    for kc in range(32):
                        lastff = P.op("pe", lambda e, ps=ps, wb=wb, kc=kc: e.matmul(
                            ps[:, :n], lhsT=wb[:, kc, :], rhs=hT[:, kc, :n], start=(kc == 0), stop=(kc == 31)),
                            deps=[wtok, th, pfree] if kc == 0 else [], sig=(kc == 31))
                    wA_r.rel(wi, lastff)
                    gps[which] = (pi, ps, lastff)
                gi_, gb, gfree = gb_r.get()
                pi, ps, mm = gps["g"]
                tg1 = P.op("act", lambda e, gb=gb, ps=ps: e.activation(out=gb[:, 2:2 + n], in_=ps[:, :n], func=AF.Copy),
                           deps=[mm, gfree])
                C.psg.rel(pi, tg1)
                if halo:
                    tsv = P.op("dve", lambda e, gb=gb, fc=fc: e.tensor_scalar(
                        out=ghalo[:, fc, :], in0=gb[:, n:n + 2], scalar1=hf_sb[:, 0:1], scalar2=None, op0=ALU.mult),
                        deps=[tg1, tconst])
                    gb_r.rel(gi_, tsv)
                    continue
                tg0 = P.op("pool", lambda e, gb=gb, fc=fc: e.tensor_copy(out=gb[:, 0:2], in_=ghalo[:, fc, :]),
                           deps=[gfree])
                tsv = P.op("pool", lambda e, gb=gb, fc=fc: e.tensor_copy(out=ghalo[:, fc, :], in_=gb[:, n:n + 2]),
                           deps=[tg0, tg1])
                ti, tb_, tfree = t_r.get()
                tc1 = P.op("act", lambda e, tb_=tb_, gb=gb, fc=fc: e.activation(
                    out=tb_[:, :n], in_=gb[:, 2:2 + n], func=AF.Identity, bias=cw_sb[:, fc, 3:4],
                    scale=cw_sb[:, fc, 2:3]), deps=[tg1, tfree, tconst])
                tc2 = P.op("dve", lambda e, tb_=tb_, gb=gb, fc=fc: e.scalar_tensor_tensor(
                    out=tb_[:, :n], in0=gb[:, 1:1 + n], scalar=cw_sb[:, fc, 1:2], in1=tb_[:, :n], op0=ALU.mult,
                    op1=ALU.add), deps=[tc1, tg0])
                tc3 = P.op("dve", lambda e, tb_=tb_, gb=gb, fc=fc: e.scalar_tensor_tensor(
                    out=tb_[:, :n], in0=gb[:, 0:n], scalar=cw_sb[:, fc, 0:1], in1=tb_[:, :n], op0=ALU.mult,
                    op1=ALU.add), deps=[tc2])
                gb_r.rel(gi_, [tc3, tsv])
                tsl = P.op("act", lambda e, tb_=tb_: e.activation(out=tb_[:, :n], in_=tb_[:, :n], func=AF.Silu),
                           deps=[tc3])
                pi2, ps2, mm2 = gps["u"]
                hi, hb, hfree = hb_r.get()
                thd = P.op("dve", lambda e, hb=hb, tb_=tb_, ps2=ps2: e.tensor_tensor(
                    out=hb[:, :n], in0=ps2[:, :n], in1=tb_[:, :n], op=ALU.mult), deps=[tsl, mm2, hfree])
                C.psg.rel(pi2, thd)
                t_r.rel(ti, thd)
                tsh = P.dma(QA, hidd[:, fc, :n], hb[:, :n], hb_r.dsems[hi], deps=[thd])
                hb_r.rel(hi, tsh)
                hst.append(tsh)
            if halo:
                state["scr_free"] = lastff
                state["hid_free"] = lastq
                return
            dh = P.dsem()
            for part in range(2):
                thl = P.dma("sp", hidb[:, part * 43:(part + 1) * 43, :n], hidd[:, part * 43:(part + 1) * 43, :n], dh,
                            deps=[hst[-3:], lastq])
            stores = []
            bepi = branch_epi(n, stores)
            lastd = gemm(P, C, ld_down,
                         FCH, [(i, 128) for i in range(32)], lambda kc, g: hidb[:, kc, :n], lambda fc: ["m"],
                         lambda g: n, bepi, wD_r, [thl, lastff], wdeps=lastff)
            bepi.flush()
            state["scr_free"] = lastd
            state["hid_free"] = lastd
            t3 = postnorm(n, 5, lambda c: xb[c * 128:(c + 1) * 128, :n],
                          lambda c, c0=c0: xo[c * 128:(c + 1) * 128, c0 - 16:c0 - 16 + n], stores)
            state["outs"] = t3
        for (c0_, n_) in tiles:
            do_tile(c0_, n_)
        P.emit(state["outs"])
    return nc


def build_cast(L):
    nc = bass.Bass("TRN2", target_bir_lowering=False)
    CW = 4096
    src = nc.dram_tensor("src", [128, L], F32, kind="ExternalInput").ap()
    dst = nc.dram_tensor("dst", [128, L], BF16, kind="ExternalOutput").ap()
    with ExitStack() as st:
        P = Prog(nc, st)
        st_r = Ring(P, [P.sbuf([128, CW], F32) for _ in range(3)], True)
        bf_r = Ring(P, [P.sbuf([128, CW], BF16) for _ in range(3)], True)
        toks = []
        for ci in range(L // CW):
            si, sb, sfree = st_r.get()
            tl = P.dma("sp", sb[:], src[:, ci * CW:(ci + 1) * CW], st_r.dsems[si], deps=[sfree])
            bi, bb, bfree = bf_r.get()
            if ci % 2 == 0:
                tc = P.op("act", lambda e, bb=bb, sb=sb: e.activation(out=bb[:], in_=sb[:], func=AF.Copy),
                          deps=[tl, bfree])
            else:
                tc = P.op("dve", lambda e, bb=bb, sb=sb: e.tensor_copy(out=bb[:], in_=sb[:]), deps=[tl, bfree])
            st_r.rel(si, tc)
            ts = P.dma(QA, dst[:, ci * CW:(ci + 1) * CW], bb[:], bf_r.dsems[bi], deps=[tc])
            bf_r.rel(bi, ts)
            toks.append(ts)
        P.emit(toks[-3:])
    return nc


W_NAMES = [("w_in", D, INCOLS), ("w_uq", QLORA, HEADS * 192), ("w_ukv", KVLORA, HEADS * 256),
           ("w_pool", 2048, 512), ("w_out", D, D), ("w_cq", D, 1024), ("w_ck", D, 1024), ("w_cv", D, 1024),
           ("w_co", 1024, D), ("w_gate", D, DFF), ("w_up", D, DFF), ("w_down", DFF, D)]


def chunked(Wm, M=128):
    K, F = Wm.shape
    return np.ascontiguousarray(
        Wm.reshape(K // 128, 128, F // M, M).transpose(2, 1, 0, 3)).reshape(F // M, 128, (K // 128) * M)


def prep_a_weights(Wl):
    w_in = Wl["w_in"]
    z64 = np.zeros((D, 64), w_in.dtype)
    kr = w_in[:, 1536:1600]
    krs = np.concatenate([w_in[:, 1568:1600], w_in[:, 1536:1568]], axis=1)
    cols = np.concatenate([w_in[:, 0:1536], kr, z64, krs, z64, w_in[:, 1600:]], axis=1)
    uq = Wl["w_uq"].reshape(QLORA, HEADS, 192)
    uq = np.concatenate([uq, uq[:, :, 160:192], uq[:, :, 128:160]], axis=2).reshape(QLORA, HEADS * 256)
    ukv = Wl["w_ukv"].reshape(KVLORA, HEADS, 256)
    uk = np.ascontiguousarray(ukv[:, :, 0:128]).reshape(KVLORA, HEADS * 128)
    uv = np.ascontiguousarray(ukv[:, :, 128:256]).reshape(KVLORA, HEADS * 128)
    wp = Wl["w_pool"].reshape(4, 512, 512)
    wpc = np.concatenate([chunked(wp[g]) for g in range(4)], axis=0)
    return {"w_in": chunked(cols), "w_uq": chunked(uq, 256), "w_uk": chunked(uk), "w_uv": uv, "w_pool": wpc}


def prep_b_weights(Wl):
    d = {k: (chunked(Wl[k]) if k in CHUNK_B else Wl[k])
         for k in ("w_out", "w_cq", "w_ck", "w_co", "w_gate", "w_up", "w_down")}
    d["w_cv"] = Wl["w_cv"]
    return d


def _run(nc, ins):
    return run_bass_kernel_spmd(nc, ins, core_ids=list(range(NCORES))).results


def _tm(v):
    v = np.asarray(v, np.float32)
    return np.ascontiguousarray(v.reshape(-1, 128).T)


def kernel(**inp):
    inp = {k: np.asarray(v) for k, v in inp.items()}
    S = inp["x"].shape[1]
    T = S // NCORES
    L_layers = inp["w_in"].shape[0]
    tot = sum(R * Cc for _, R, Cc in W_NAMES) * L_layers
    blk = 1024 * 4096
    totp = ((tot + blk - 1) // blk) * blk
    flat = np.zeros(totp, np.float32)
    o = 0
    for l in range(L_layers):
        for name, R, Cc in W_NAMES:
            flat[o:o + R * Cc] = inp[name][l].reshape(-1)
            o += R * Cc
    Lc = totp // 1024
    pieces = flat.reshape(NCORES, 128, Lc)
    res = _run(build_cast(Lc), [{"src": pieces[c]} for c in range(NCORES)])
    del flat
    wbf = np.concatenate([res[c]["dst"].reshape(-1) for c in range(NCORES)])
    W = []
    o = 0
    for l in range(L_layers):
        d = {}
        for name, R, Cc in W_NAMES:
            d[name] = wbf[o:o + R * Cc].reshape(R, Cc)
            o += R * Cc
        W.append(d)
    inv = (1.0 / (10000.0 ** (np.arange(0, 64, 2, dtype=np.float32) / 64))).astype(np.float32)
    invf = np.concatenate([inv, inv])[:, None].astype(np.float32)
    sgn = np.concatenate([-np.ones(32), np.ones(32)])[:, None].astype(np.float32)
    pos = inp["positions"][0].astype(np.int32)
    XT = np.ascontiguousarray(inp["x"][0].T)
    memT = np.ascontiguousarray(inp["mem"][0].T)
    nc_a = build_a(T)
    nc_att = build_att(S)
    nc_b = build_b(T)
    for l in range(L_layers):
        xpad = np.concatenate([np.zeros((D, 16), np.float32), XT], axis=1)
        wa = prep_a_weights(W[l])
        ins = []
        for c in range(NCORES):
            ins.append({
                "xT": np.ascontiguousarray(xpad[:, c * T:c * T + T + 16]),
                **wa,
                "pos": pos[None, c * T:(c + 1) * T], "tix": np.arange(c * T, (c + 1) * T, dtype=np.float32)[None, :],
                "gpre": _tm(inp["g_mix_pre"][l]), "gq": _tm(inp["g_q"][l]), "gkv": _tm(inp["g_kv"][l]),
                "spool": _tm(inp["s_pool"][l]), "invf": invf, "sgn": sgn})
        ra = _run(nc_a, ins)
        QN = np.concatenate([ra[c]["qn_o"] for c in range(NCORES)], axis=2)
        QR = np.concatenate([ra[c]["qr_o"] for c in range(NCORES)], axis=2)
        KN = np.concatenate([ra[c]["kn_o"] for c in range(NCORES)], axis=2)
        KR = np.concatenate([ra[c]["kr_o"] for c in range(NCORES)], axis=1)
        V = np.concatenate([ra[c]["v_o"] for c in range(NCORES)], axis=0)
        YT = np.concatenate([ra[c]["y_o"] for c in range(NCORES)], axis=1)
        del ra
        ins = []
        for c in range(NCORES):
            ins.append({"qn": np.ascontiguousarray(QN[2 * c:2 * c + 2]), "qr": np.ascontiguousarray(QR[2 * c:2 * c + 2]),
                        "kn": np.ascontiguousarray(KN[2 * c:2 * c + 2]), "kr": KR,
                        "v": np.ascontiguousarray(V[:, 256 * c:256 * c + 256])})
        rt = _run(nc_att, ins)
        AT = np.concatenate([rt[c]["aT"] for c in range(NCORES)], axis=0)
        del rt, QN, QR, KN, V
        apad = np.concatenate([np.zeros((2048, 16), NPBF), AT], axis=1)
        ypad = np.concatenate([np.zeros((2048, 16), NPBF), YT], axis=1)
        gains = np.ascontiguousarray(np.stack(
            [_tm(inp[k][l]) for k in ("g_mix_post", "g_x_pre", "g_x_post", "g_mem", "g_ffn_pre", "g_ffn_post")], axis=1))
        cwb = np.ascontiguousarray(np.stack([_tm(inp["conv_w"][l][0]), _tm(inp["conv_w"][l][1]),
                                             _tm(inp["conv_w"][l][2]), _tm(inp["conv_b"][l])], axis=2))
        wb_ = prep_b_weights(W[l])
        ins = []
        for c in range(NCORES):
            d = {"xT": np.ascontiguousarray(xpad[:, c * T:c * T + T + 16]),
                 "aT": np.ascontiguousarray(apad[:, c * T:c * T + T + 16]),
                 "yT": np.ascontiguousarray(ypad[:, c * T:c * T + T + 16]),
                 "memT": memT, "gains": gains, "cwb": cwb,
                 "hflag": np.full((128, 1), 0.0 if c == 0 else 1.0, np.float32)}
            d.update(wb_)
            ins.append(d)
        rb = _run(nc_b, ins)
        XT = np.concatenate([rb[c]["xo"] for c in range(NCORES)], axis=1)
        del rb
    return np.ascontiguousarray(XT.T)[None].astype(np.float32)
```
